# Optimizing a Trainium2 kernel written in Bass

```python
import math
import jax, jax.numpy as jnp
from jax import lax
import numpy as np

D_MODEL = 2048
BATCH = 32
SEQ = 256
DEPTH = 2
DEC_BATCH = 8
DEC_SEQ = 4096
PAST_LEN = 256

GRID_W = 64
HEAD_DIM = 128
NA_HEADS = 8
NA_WIN_R = 8
NA_WIN_C = 16
GDN_HEADS = 8
GDN_CONV = 3
GDN_CHUNK = 64
DIFF_HEADS = 4
D_FF = 5632
N_MOD = 9
Q_BLOCK = 128
ROPE_BASE = 10000.0
EPS = 1e-6
NA_W = NA_HEADS * HEAD_DIM
GDN_W = GDN_HEADS * HEAD_DIM
DIFF_W = DIFF_HEADS * 2 * HEAD_DIM
IN_SPLITS = (NA_W, NA_W, NA_W, GDN_W, GDN_W, GDN_W, GDN_W, 4 * GDN_HEADS, DIFF_W, DIFF_W, DIFF_W, 3 * D_MODEL)
D_IN = sum(IN_SPLITS)

kernel_name = 'hybrid_na_gdn_diff_diffusion_step'


def rmsnorm(x, w):
    xf = x.astype(jnp.float32)
    y = xf * lax.rsqrt(jnp.mean(xf * xf, axis=-1, keepdims=True) + EPS)
    return (y * w.astype(jnp.float32)).astype(x.dtype)


def l2norm(x):
    xf = x.astype(jnp.float32)
    return xf * lax.rsqrt(jnp.sum(xf * xf, axis=-1, keepdims=True) + EPS)


def swiglu(h, w_gu, w_dn):
    g, u = jnp.split(h @ w_gu, 2, axis=-1)
    return (jax.nn.silu(g) * u) @ w_dn


def modulation(cvec, w_mod, b_mod):
    m = jax.nn.silu(cvec) @ w_mod + b_mod
    return m.reshape(cvec.shape[0], N_MOD, D_MODEL)


def modulated_norm(x, mod, i, w_pre):
    shift = mod[:, 3 * i][:, None]
    scale = mod[:, 3 * i + 1][:, None]
    return rmsnorm(x, w_pre) * (1 + scale) + shift


def gated_residual(x, y, mod, i, w_post, coef):
    return x + coef * mod[:, 3 * i + 2][:, None] * rmsnorm(y, w_post)


def axial_rope_tables(n_tok):
    t = jnp.arange(n_tok)
    pos = jnp.stack([t // GRID_W, t % GRID_W], axis=-1).astype(jnp.float32)
    nq = HEAD_DIM // 4
    inv = ROPE_BASE ** (-jnp.arange(nq, dtype=jnp.float32) / nq)
    ang = pos[:, :, None] * inv
    return jnp.cos(ang)[:, :, None, :], jnp.sin(ang)[:, :, None, :]


def apply_axial_rope(x, cos, sin):
    shp = x.shape
    n_mid = len(shp) - 3
    xr = x.astype(jnp.float32).reshape(shp[:-1] + (2, 2, HEAD_DIM // 4))
    c = cos.reshape((cos.shape[0],) + (1,) * n_mid + cos.shape[1:])
    s = sin.reshape((sin.shape[0],) + (1,) * n_mid + sin.shape[1:])
    rot = jnp.stack([-xr[..., 1, :], xr[..., 0, :]], axis=-2)
    return (xr * c + rot * s).reshape(shp).astype(x.dtype)


def softmax_attend(q, k, v):
    s = jnp.einsum('bqhd,bkhd->bhqk', q, k).astype(jnp.float32) * HEAD_DIM ** -0.5
    p = jax.nn.softmax(s, axis=-1).astype(v.dtype)
    return jnp.einsum('bhqk,bkhd->bqhd', p, v)


def na_latent(q, k, v, kc, vc, rpb):
    B, T, H, d = q.shape
    rows = T // GRID_W
    wr = min(NA_WIN_R, rows)
    qg = (q * HEAD_DIM ** -0.5).reshape(B, rows, GRID_W, H, d)
    kg = k.reshape(B, rows, GRID_W, H, d)
    vg = v.reshape(B, rows, GRID_W, H, d)
    col = jnp.arange(GRID_W)
    col_start = jnp.clip(col - NA_WIN_C // 2, 0, GRID_W - NA_WIN_C)
    col_idx = col_start[:, None] + jnp.arange(NA_WIN_C)
    dc = col_idx - col[:, None] + NA_WIN_C - 1
    n_loc = wr * NA_WIN_C

    def one_row(r):
        rs = jnp.clip(r - NA_WIN_R // 2, 0, rows - wr)
        qr = lax.dynamic_index_in_dim(qg, r, axis=1, keepdims=False)
        kr = lax.dynamic_slice_in_dim(kg, rs, wr, axis=1)
        vr = lax.dynamic_slice_in_dim(vg, rs, wr, axis=1)
        kw = kr[:, :, col_idx]
        vw = vr[:, :, col_idx]
        dr = rs + jnp.arange(wr) - r + NA_WIN_R - 1
        bias = rpb[:, dr[None, :, None], dc[:, None, :]]
        s_loc = jnp.einsum('bqhd,brqchd->bhqrc', qr, kw).astype(jnp.float32) + bias.astype(jnp.float32)[None]
        s_ctx = jnp.einsum('bqhd,bkhd->bhqk', qr, kc).astype(jnp.float32)
        s = jnp.concatenate([s_loc.reshape(B, H, GRID_W, n_loc), s_ctx], axis=-1)
        p = jax.nn.softmax(s, axis=-1).astype(v.dtype)
        p_loc = p[..., :n_loc].reshape(B, H, GRID_W, wr, NA_WIN_C)
        p_ctx = p[..., n_loc:]
        return jnp.einsum('bhqrc,brqchd->bqhd', p_loc, vw) + jnp.einsum('bhqk,bkhd->bqhd', p_ctx, vc)

    o = lax.map(one_row, jnp.arange(rows))
    return jnp.moveaxis(o, 0, 1).reshape(B, T, H, d)


def short_conv(x, w):
    K = w.shape[0]
    pad = K // 2
    T = x.shape[1]
    xp = jnp.pad(x, ((0, 0), (pad, pad), (0, 0)))
    out = xp[:, 0:T] * w[0]
    for j in range(1, K):
        out = out + xp[:, j:j + T] * w[j]
    return out


def gdn_inputs(g_q, g_k, g_v, g_ab, conv_w, a_log, dt_bias):
    B, T, _ = g_q.shape
    qkv = jax.nn.silu(short_conv(jnp.concatenate([g_q, g_k, g_v], axis=-1), conv_w))
    q, k, v = jnp.split(qkv, 3, axis=-1)
    q = l2norm(q.reshape(B, T, GDN_HEADS, HEAD_DIM)) * HEAD_DIM ** -0.5
    k = l2norm(k.reshape(B, T, GDN_HEADS, HEAD_DIM))
    v = v.reshape(B, T, GDN_HEADS, HEAD_DIM).astype(jnp.float32)
    ab = g_ab.astype(jnp.float32).reshape(B, T, 4, GDN_HEADS)
    beta = jax.nn.sigmoid(ab[:, :, 0:2])
    log_a = -jnp.exp(a_log.astype(jnp.float32)) * jax.nn.softplus(ab[:, :, 2:4] + dt_bias.astype(jnp.float32))
    return q, k, v, beta, log_a


def chunked_gated_delta(q, k, v, beta, log_a, s0):
    B, T, H, dk = q.shape
    dv = v.shape[-1]
    C = GDN_CHUNK
    n = T // C

    def chunks(t):
        t = t.reshape((B, n, C, H) + t.shape[3:])
        return jnp.moveaxis(jnp.moveaxis(t, 1, 0), 3, 2)

    qc, kc, vc = chunks(q), chunks(k), chunks(v)
    bc, gc = chunks(beta), chunks(log_a)
    gam = jnp.cumsum(gc, axis=-1)
    diff = gam[..., :, None] - gam[..., None, :]
    strict = jnp.tril(jnp.ones((C, C), bool), -1)
    incl = jnp.tril(jnp.ones((C, C), bool))
    dec_strict = jnp.exp(jnp.where(strict, diff, -jnp.inf))
    dec_incl = jnp.exp(jnp.where(incl, diff, -jnp.inf))
    kk = jnp.einsum('nbhid,nbhjd->nbhij', kc, kc)
    a_mat = jnp.eye(C, dtype=jnp.float32) + bc[..., :, None] * kk * dec_strict
    rhs = jnp.concatenate([bc[..., None] * vc, (bc * jnp.exp(gam))[..., None] * kc], axis=-1)
    sol = lax.linalg.triangular_solve(a_mat, rhs, left_side=True, lower=True, unit_diagonal=True)
    u, wk = sol[..., :dv], sol[..., dv:]
    p = jnp.einsum('nbhid,nbhjd->nbhij', qc, kc) * dec_incl
    qg = qc * jnp.exp(gam)[..., None]
    kd = kc * jnp.exp(gam[..., -1:] - gam)[..., None]
    g_last = jnp.exp(gam[..., -1])

    def step(s, xs):
        u_i, wk_i, p_i, qg_i, kd_i, gl_i = xs
        w_i = u_i - jnp.einsum('bhck,bhkv->bhcv', wk_i, s)
        o_i = jnp.einsum('bhck,bhkv->bhcv', qg_i, s) + jnp.einsum('bhij,bhjv->bhiv', p_i, w_i)
        s = gl_i[..., None, None] * s + jnp.einsum('bhck,bhcv->bhkv', kd_i, w_i)
        return s, o_i

    s_fin, o = lax.scan(step, s0, (u, wk, p, qg, kd, g_last))
    o = jnp.moveaxis(jnp.moveaxis(o, 2, 3), 0, 1).reshape(B, T, H, dv)
    return o, s_fin


def reverse_tokens(t):
    return t[:, ::-1]


def gdn_bidir(q, k, v, beta, log_a, s0_f, s0_b):
    o_f, s_f = chunked_gated_delta(q, k, v, beta[:, :, 0], log_a[:, :, 0], s0_f)
    o_b, s_b = chunked_gated_delta(reverse_tokens(q), reverse_tokens(k), reverse_tokens(v),
                                   reverse_tokens(beta[:, :, 1]), reverse_tokens(log_a[:, :, 1]), s0_b)
    return o_f + reverse_tokens(o_b), s_f, s_b


def diff_lambda_value(lam, l):
    lam_init = 0.8 - 0.6 * math.exp(-0.3 * l)
    lf = lam.astype(jnp.float32)
    val = jnp.exp(jnp.sum(lf[0] * lf[1])) - jnp.exp(jnp.sum(lf[2] * lf[3])) + lam_init
    return val, lam_init


def diff_combine(s, v, lam):
    p = jax.nn.softmax(s, axis=-1)
    a = p[:, :, 0] - lam * p[:, :, 1]
    return jnp.einsum('bhqk,bkhe->bqhe', a.astype(v.dtype), v)


def diff_latent(q, k, v, kc, vc, lam):
    B, T = q.shape[:2]
    keys = jnp.concatenate([k, kc.astype(k.dtype)], axis=1)
    vals = jnp.concatenate([v, vc.astype(v.dtype)], axis=1)
    nb = T // Q_BLOCK
    qb = jnp.moveaxis(q.reshape(B, nb, Q_BLOCK, DIFF_HEADS, 2, HEAD_DIM), 1, 0)

    def one_block(qi):
        s = jnp.einsum('bqhmd,bkhmd->bhmqk', qi, keys).astype(jnp.float32) * HEAD_DIM ** -0.5
        return diff_combine(s, vals, lam)

    o = lax.map(one_block, qb)
    return jnp.moveaxis(o, 0, 1).reshape(B, T, DIFF_HEADS, 2 * HEAD_DIM)


def run_layer(x, mod, l, lw, cache=None):
    B, T, _ = x.shape
    h = modulated_norm(x, mod, 0, lw['norm_pre'][0])
    x = gated_residual(x, swiglu(h, lw['ffn1_w_gu'], lw['ffn1_w_dn']), mod, 0, lw['norm_post'][0], 0.5)

    h = modulated_norm(x, mod, 1, lw['norm_pre'][1])
    offs = np.cumsum(IN_SPLITS)[:-1].tolist()
    (na_q, na_k, na_v, g_q, g_k, g_v, g_z, g_ab, d_q, d_k, d_v, gates) = jnp.split(h @ lw['w_in'], offs, axis=-1)
    na_q = na_q.reshape(B, T, NA_HEADS, HEAD_DIM)
    na_k = na_k.reshape(B, T, NA_HEADS, HEAD_DIM)
    na_v = na_v.reshape(B, T, NA_HEADS, HEAD_DIM)
    d_q = d_q.reshape(B, T, DIFF_HEADS, 2, HEAD_DIM)
    d_k = d_k.reshape(B, T, DIFF_HEADS, 2, HEAD_DIM)
    d_v = d_v.reshape(B, T, DIFF_HEADS, 2 * HEAD_DIM)
    q_g, k_g, v_g, beta, log_a = gdn_inputs(g_q, g_k, g_v, g_ab, lw['gdn_conv'], lw['gdn_a_log'], lw['gdn_dt_bias'])
    lam, lam_init = diff_lambda_value(lw['diff_lambda'], l)

    if cache is None:
        o_na = softmax_attend(na_q, na_k, na_v)
        s0 = jnp.zeros((B, GDN_HEADS, HEAD_DIM, HEAD_DIM), jnp.float32)
        o_g, s_f, s_b = gdn_bidir(q_g, k_g, v_g, beta, log_a, s0, s0)
        s = jnp.einsum('bqhmd,bkhmd->bhmqk', d_q, d_k).astype(jnp.float32) * HEAD_DIM ** -0.5
        o_d = diff_combine(s, d_v, lam)
        ctx_tensors = (na_k, na_v, jnp.stack([s_f, s_b], axis=1).astype(x.dtype), d_k, d_v)
    else:
        kc_na, vc_na, st, kc_d, vc_d = cache
        o_na = na_latent(na_q, na_k, na_v, kc_na.astype(na_k.dtype), vc_na.astype(na_v.dtype), lw['na_rpb'])
        o_g, _, _ = gdn_bidir(q_g, k_g, v_g, beta, log_a, st[:, 0].astype(jnp.float32), st[:, 1].astype(jnp.float32))
        cos, sin = axial_rope_tables(T)
        o_d = diff_latent(apply_axial_rope(d_q, cos, sin), apply_axial_rope(d_k, cos, sin), d_v, kc_d, vc_d, lam)
        ctx_tensors = None

    o_g = (rmsnorm(o_g, lw['gdn_norm']) * jax.nn.silu(g_z.reshape(B, T, GDN_HEADS, HEAD_DIM).astype(jnp.float32))).astype(x.dtype)
    o_d = rmsnorm(o_d, lw['diff_norm']) * (1.0 - lam_init)
    gate = jax.nn.sigmoid(gates.astype(jnp.float32)).astype(x.dtype).reshape(B, T, 3, D_MODEL)
    merged = (gate[:, :, 0] * (o_na.reshape(B, T, NA_W) @ lw['w_branch_na'])
              + gate[:, :, 1] * (o_g.reshape(B, T, GDN_W) @ lw['w_branch_gdn'])
              + gate[:, :, 2] * (o_d.reshape(B, T, DIFF_W) @ lw['w_branch_diff']))
    x = gated_residual(x, merged @ lw['w_out'], mod, 1, lw['norm_post'][1], 1.0)

    h = modulated_norm(x, mod, 2, lw['norm_pre'][2])
    x = gated_residual(x, swiglu(h, lw['ffn2_w_gu'], lw['ffn2_w_dn']), mod, 2, lw['norm_post'][2], 0.5)
    return x, ctx_tensors


def setup_inputs(seed: int = 0) -> dict:
    key = jax.random.key(seed)
    ks = jax.random.split(key, 32)
    f32 = jnp.float32
    L = DEPTH

    def nrm(i, shape, scale):
        return jax.random.normal(ks[i], shape, f32) * scale

    dt = jnp.exp(jax.random.uniform(ks[20], (L, 2, GDN_HEADS), f32, math.log(1e-3), math.log(1e-1)))
    return {
        'x_prompt': nrm(0, (BATCH, SEQ, D_MODEL), 1.0),
        'x_sample': nrm(1, (DEC_BATCH, DEC_SEQ, D_MODEL), 1.0),
        'cache_na_k': nrm(2, (DEC_BATCH, L, PAST_LEN, NA_HEADS, HEAD_DIM), 1.0),
        'cache_na_v': nrm(3, (DEC_BATCH, L, PAST_LEN, NA_HEADS, HEAD_DIM), 1.0),
        'state_gdn': nrm(4, (DEC_BATCH, L, 2, GDN_HEADS, HEAD_DIM, HEAD_DIM), 0.1),
        'cache_diff_k': nrm(5, (DEC_BATCH, L, PAST_LEN, DIFF_HEADS, 2, HEAD_DIM), 1.0),
        'cache_diff_v': nrm(6, (DEC_BATCH, L, PAST_LEN, DIFF_HEADS, 2 * HEAD_DIM), 1.0),
        'c': nrm(7, (DEC_BATCH, D_MODEL), 1.0),
        'c_ctx': nrm(8, (D_MODEL,), 1.0),
        'w_mod': nrm(9, (L, D_MODEL, N_MOD * D_MODEL), 0.5 * D_MODEL ** -0.5),
        'b_mod': nrm(10, (L, N_MOD * D_MODEL), 0.01),
        'norm_pre': 1.0 + nrm(11, (L, 3, D_MODEL), 0.02),
        'norm_post': 1.0 + nrm(12, (L, 3, D_MODEL), 0.02),
        'ffn1_w_gu': nrm(13, (L, D_MODEL, 2 * D_FF), D_MODEL ** -0.5),
        'ffn1_w_dn': nrm(14, (L, D_FF, D_MODEL), D_FF ** -0.5),
        'ffn2_w_gu': nrm(15, (L, D_MODEL, 2 * D_FF), D_MODEL ** -0.5),
        'ffn2_w_dn': nrm(16, (L, D_FF, D_MODEL), D_FF ** -0.5),
        'w_in': nrm(17, (L, D_MODEL, D_IN), D_MODEL ** -0.5),
        'na_rpb': nrm(18, (L, NA_HEADS, 2 * NA_WIN_R - 1, 2 * NA_WIN_C - 1), 0.1),
        'gdn_conv': nrm(19, (L, GDN_CONV, 3 * GDN_W), GDN_CONV ** -0.5),
        'gdn_a_log': jnp.log(jax.random.uniform(ks[21], (L, 2, GDN_HEADS), f32, 1.0, 16.0)),
        'gdn_dt_bias': dt + jnp.log(-jnp.expm1(-dt)),
        'gdn_norm': 1.0 + nrm(22, (L, HEAD_DIM), 0.02),
        'diff_lambda': nrm(23, (L, 4, HEAD_DIM), 0.1),
        'diff_norm': 1.0 + nrm(24, (L, 2 * HEAD_DIM), 0.02),
        'w_branch_na': nrm(25, (L, NA_W, D_MODEL), NA_W ** -0.5),
        'w_branch_gdn': nrm(26, (L, GDN_W, D_MODEL), GDN_W ** -0.5),
        'w_branch_diff': nrm(27, (L, DIFF_W, D_MODEL), DIFF_W ** -0.5),
        'w_out': nrm(28, (L, D_MODEL, D_MODEL), D_MODEL ** -0.5),
    }


def reference(x_prompt, x_sample, cache_na_k, cache_na_v, state_gdn, cache_diff_k, cache_diff_v, c, c_ctx,
              w_mod, b_mod, norm_pre, norm_post, ffn1_w_gu, ffn1_w_dn, ffn2_w_gu, ffn2_w_dn, w_in, na_rpb,
              gdn_conv, gdn_a_log, gdn_dt_bias, gdn_norm, diff_lambda, diff_norm, w_branch_na, w_branch_gdn,
              w_branch_diff, w_out):
    y_prompt = x_prompt
    y_sample = x_sample
    na_k_l, na_v_l, gdn_s_l, diff_k_l, diff_v_l = [], [], [], [], []
    for l in range(DEPTH):
        lw = {
            'norm_pre': norm_pre[l], 'norm_post': norm_post[l],
            'ffn1_w_gu': ffn1_w_gu[l], 'ffn1_w_dn': ffn1_w_dn[l],
            'ffn2_w_gu': ffn2_w_gu[l], 'ffn2_w_dn': ffn2_w_dn[l],
            'w_in': w_in[l], 'na_rpb': na_rpb[l], 'gdn_conv': gdn_conv[l],
            'gdn_a_log': gdn_a_log[l], 'gdn_dt_bias': gdn_dt_bias[l], 'gdn_norm': gdn_norm[l],
            'diff_lambda': diff_lambda[l], 'diff_norm': diff_norm[l],
            'w_branch_na': w_branch_na[l], 'w_branch_gdn': w_branch_gdn[l],
            'w_branch_diff': w_branch_diff[l], 'w_out': w_out[l],
        }
        mod_ctx = modulation(c_ctx[None], w_mod[l], b_mod[l])
        mod_lat = modulation(c, w_mod[l], b_mod[l])
        y_prompt, (k_na, v_na, s_gdn, k_d, v_d) = run_layer(y_prompt, mod_ctx, l, lw)
        na_k_l.append(k_na)
        na_v_l.append(v_na)
        gdn_s_l.append(s_gdn)
        diff_k_l.append(k_d)
        diff_v_l.append(v_d)
        layer_cache = (cache_na_k[:, l], cache_na_v[:, l], state_gdn[:, l], cache_diff_k[:, l], cache_diff_v[:, l])
        y_sample, _ = run_layer(y_sample, mod_lat, l, lw, layer_cache)
    new_na_k = jnp.stack(na_k_l, axis=1)
    new_na_v = jnp.stack(na_v_l, axis=1)
    new_gdn_state = jnp.stack(gdn_s_l, axis=1)
    new_diff_k = jnp.stack(diff_k_l, axis=1)
    new_diff_v = jnp.stack(diff_v_l, axis=1)
    return (y_prompt, y_sample, new_na_k, new_na_v, new_gdn_state, new_diff_k, new_diff_v)
```

```python
import math
from contextlib import ExitStack
import numpy as np
import concourse.bass as bass
import concourse.mybir as mybir
from concourse.bass_utils import run_bass_kernel_spmd

F32 = mybir.dt.float32
BF16 = mybir.dt.bfloat16
AF = mybir.ActivationFunctionType
ALU = mybir.AluOpType
AX = mybir.AxisListType

HEAD_DIM = 128
NA_HEADS = 8
GDN_HEADS = 8
DIFF_HEADS = 4
GRID_W = 64
NA_WIN_R = 8
NA_WIN_C = 16
GDN_CHUNK = 64
EPS = 1e-6
N_MOD = 9
ROPE_BASE = 10000.0


class Buf:
    __slots__ = ("name", "last_w", "readers")

    def __init__(self, name=""):
        self.name = name
        self.last_w = None
        self.readers = []


class Op:
    __slots__ = ("eng", "fn", "deps", "sig", "vc", "waits", "dma", "pool", "needed")

    def __init__(self, eng, fn, dma, pool):
        self.eng = eng
        self.fn = fn
        self.deps = set()
        self.sig = None
        self.vc = None
        self.waits = None
        self.dma = dma
        self.pool = pool
        self.needed = False


class Prog:
    ENGS = ("pe", "act", "dve", "pool", "sp")

    def __init__(self, nc, dma_pools):
        self.nc = nc
        self.ops = []
        self.dma_pools = dma_pools
        self.nosync_same = {"pe"}
        self.last_on = {}
        self.pending_dma = []

    def add(self, eng, fn, reads=(), writes=(), dma=False, pool=None):
        op = Op(eng, fn, dma, pool)
        for b in reads:
            if b.last_w is not None:
                op.deps.add(b.last_w)
            b.readers.append(op)
        for b in writes:
            if b.last_w is not None:
                op.deps.add(b.last_w)
            for r in b.readers:
                if r is not op:
                    op.deps.add(r)
            b.last_w = op
            b.readers = []
        self.ops.append(op)
        self.last_on[(eng, pool if dma else None)] = op
        if dma:
            self.pending_dma.append(op)
        return op

    def dma(self, eng, out, in_, reads=(), writes=(), pool="ld", **kw):
        return self.add(eng, lambda e: e.dma_start(out=out, in_=in_, **kw), reads, writes, dma=True, pool=pool)

    def barrier(self):
        lasts = list(self.last_on.values())
        pend = self.pending_dma
        self.pending_dma = []
        for e in self.ENGS:
            op = self.add(e, lambda e_: None)
            for l in lasts:
                if l is not op:
                    op.deps.add(l)
            for d in pend:
                op.deps.add(d)

    def finalize(self, stack):
        nc = self.nc
        for op in self.ops:
            for d in op.deps:
                d.needed = True
        self.eng_sem = {e: stack.enter_context(nc.semaphore("s_" + e)) for e in self.ENGS}
        self.pool_sems = {}
        for pname, n in self.dma_pools.items():
            self.pool_sems[pname] = [stack.enter_context(nc.semaphore("d_%s%d" % (pname, i))) for i in range(n)]
        pool_rr = {p: 0 for p in self.dma_pools}
        pool_state = {p: [[0, None] for _ in range(n)] for p, n in self.dma_pools.items()}
        cnt = {e: 0 for e in self.ENGS}
        clock = {e: {} for e in self.ENGS}
        nwaits = 0
        for op in self.ops:
            E = op.eng
            ck = clock[E]
            if op.dma:
                p = op.pool
                k = pool_rr[p]
                pool_rr[p] = (k + 1) % len(pool_state[p])
                st = pool_state[p][k]
                if st[1] is not None:
                    op.deps.add(st[1])
                st[0] += 16
                st[1] = op
                op.sig = ((p, k), st[0])
            elif op.needed:
                cnt[E] += 1
                op.sig = (E, cnt[E])
            waits = {}
            for d in op.deps:
                k, v = d.sig
                if ck.get(k, 0) >= v:
                    continue
                if (not d.dma) and d.eng == E and E in self.nosync_same:
                    continue
                if waits.get(k, 0) < v:
                    waits[k] = v
            for d in op.deps:
                k, v = d.sig
                if k in waits and waits[k] >= v and d.vc is not None:
                    for kk, vv in d.vc.items():
                        if ck.get(kk, 0) < vv:
                            ck[kk] = vv
            for k, v in waits.items():
                if ck.get(k, 0) < v:
                    ck[k] = v
            op.waits = waits
            nwaits += len(waits)
            if op.sig is not None:
                vc = dict(ck)
                vc[op.sig[0]] = op.sig[1]
                op.vc = vc
            op.deps = None
        self.nwaits = nwaits

    def _sem(self, key):
        if isinstance(key, tuple):
            return self.pool_sems[key[0]][key[1]]
        return self.eng_sem[key]

    def emit(self, block):
        per = {e: [] for e in self.ENGS}
        for op in self.ops:
            per[op.eng].append(op)

        def run(eng_handle, ops):
            for op in ops:
                for k, v in op.waits.items():
                    eng_handle.wait_ge(self._sem(k), v)
                ins = op.fn(eng_handle)
                if op.sig is not None:
                    if ins is None:
                        ins = eng_handle.engine_nop() if hasattr(eng_handle, "engine_nop") else None
                    if ins is not None:
                        ins.then_inc(self._sem(op.sig[0]), 16 if op.dma else 1)
                    else:
                        eng_handle.sem_inc(self._sem(op.sig[0]), 1)

        if per["sp"]:
            block.sync(lambda e: run(e, per["sp"]))
        if per["pe"]:
            block.tensor(lambda e: run(e, per["pe"]))
        if per["act"]:
            block.scalar(lambda e: run(e, per["act"]))
        if per["dve"]:
            block.vector(lambda e: run(e, per["dve"]))
        if per["pool"]:
            block.gpsimd(lambda e: run(e, per["pool"]))


class Cfg:
    def __init__(self, D=2048, DFF=5632, TL=4096, S=256, NB=4, PAST=256, L=2):
        self.D, self.DFF, self.TL, self.S, self.NB, self.PAST, self.L = D, DFF, TL, S, NB, PAST, L
        self.DC = D // 128
        self.FC = DFF // 128
        self.TT = TL + NB * S
        self.NAW = NA_HEADS * HEAD_DIM
        self.GW = GDN_HEADS * HEAD_DIM
        self.DW = DIFF_HEADS * 2 * HEAD_DIM
        sp = (self.NAW, self.NAW, self.NAW, self.GW, self.GW, self.GW, self.GW, 4 * GDN_HEADS,
              self.DW, self.DW, self.DW, 3 * D)
        self.offs = [0] + list(np.cumsum(sp))
        self.DIN = int(self.offs[-1])
        self.blocks = []
        for t in range(0, TL, 512):
            self.blocks.append((t, min(512, TL - t), 0))
        ctx_tok = NB * S
        step = 512 if ctx_tok >= 512 else ctx_tok
        for t in range(0, ctx_tok, step):
            self.blocks.append((TL + t, step, 1))


def na_tile_plan(rows):
    wr = min(NA_WIN_R, rows)
    col = np.arange(GRID_W)
    cs = np.clip(col - NA_WIN_C // 2, 0, GRID_W - NA_WIN_C)
    types = {}
    tiles = []
    plan = []
    for j in range(rows // 2):
        qr = np.array([2 * j, 2 * j + 1])
        rs = np.clip(qr - NA_WIN_R // 2, 0, rows - wr)
        need = set()
        for a in range(2):
            for r in range(rs[a], rs[a] + wr):
                need.add(r // 2)
        lst = []
        for m in sorted(need):
            kr = np.array([2 * m, 2 * m + 1])
            KR = np.repeat(kr, GRID_W)[:, None]
            KC = np.tile(col, 2)[:, None]
            QR = np.repeat(qr, GRID_W)[None, :]
            QC = np.tile(col, 2)[None, :]
            RS = np.repeat(rs, GRID_W)[None, :]
            CS = np.tile(cs, 2)[None, :]
            valid = (KR >= RS) & (KR < RS + wr) & (KC >= CS) & (KC < CS + NA_WIN_C)
            dr = KR - QR + NA_WIN_R - 1
            dc = KC - QC + NA_WIN_C - 1
            dr = np.where(valid, dr, 0)
            dc = np.where(valid, dc, 0)
            key = (dr.tobytes(), dc.tobytes(), valid.tobytes())
            if key not in types:
                types[key] = len(tiles)
                tiles.append((dr, dc, valid))
            lst.append((m, types[key]))
        plan.append(lst)
    return plan, tiles


class Builder:
    def __init__(self, cfg):
        self.cfg = cfg
        self.nc = bass.Bass("TRN2", target_bir_lowering=False)
        self.P = Prog(self.nc, {"ld": 8, "w": 6, "st": 8})
        self.dram = {}
        self.dbuf = {}

    def din(self, name, shape, dt=F32):
        self.dram[name] = self.nc.dram_tensor(name, list(shape), dt, kind="ExternalInput").ap()
        self.dbuf[name] = Buf(name)
        return self.dram[name]

    def dout(self, name, shape, dt=F32):
        self.dram[name] = self.nc.dram_tensor(name, list(shape), dt, kind="ExternalOutput").ap()
        self.dbuf[name] = Buf(name)
        return self.dram[name]

    def dscr(self, name, shape, dt):
        self.dram[name] = self.nc.dram_tensor(name, list(shape), dt, kind="Internal").ap()
        self.dbuf[name] = Buf(name)
        return self.dram[name]

    def reset_arena(self):
        self.aoff = self.const_end

    def alloc(self, words, name=""):
        off = self.aoff
        self.aoff += (words + 7) // 8 * 8
        assert self.aoff <= getattr(self, "topoff", self.AW), ("sbuf arena overflow", name, self.aoff)
        return off

    def f32v(self, off, n):
        return self.arena[:, off:off + n]

    def bf16v(self, off, nbf):
        return self.arena[:, off:off + (nbf + 1) // 2].bitcast(BF16)

    def declare(self):
        c = self.cfg
        L = c.L
        d = self.din
        d("xs", [c.TL, c.D]); d("xp", [c.NB * c.S, c.D])
        d("cna_k", [L, c.PAST, c.NAW]); d("cna_v", [L, c.PAST, c.NAW])
        d("sgdn", [L, 2, GDN_HEADS, 128, 128])
        d("cd_k", [L, c.PAST, c.DW]); d("cd_v", [L, c.PAST, c.DW])
        d("cvec", [2, c.D])
        d("w_mod", [L, c.D, 9 * c.D]); d("b_mod", [L, 9 * c.D])
        d("norm_pre", [L, 3, c.D]); d("norm_post", [L, 3, c.D])
        d("ffn1_w_gu", [L, c.D, 2 * c.DFF]); d("ffn1_w_dn", [L, c.DFF, c.D])
        d("ffn2_w_gu", [L, c.D, 2 * c.DFF]); d("ffn2_w_dn", [L, c.DFF, c.D])
        d("w_in", [L, c.D, c.DIN])
        d("gdn_conv", [L, 3, 3 * c.GW]); d("gdn_a_log", [L, 2, 8]); d("gdn_dt_bias", [L, 2, 8])
        d("gdn_norm", [L, 128]); d("diff_lambda", [L, 4, 128]); d("diff_norm", [L, 256])
        d("w_branch_na", [L, c.NAW, c.D]); d("w_branch_gdn", [L, c.GW, c.D]); d("w_branch_diff", [L, c.DW, c.D])
        d("w_out", [L, c.D, c.D])
        d("k_ident", [128, 128]); d("k_perm", [128, 128])
        d("k_cos", [128, c.TL]); d("k_sin", [128, c.TL])
        d("k_masks", [6, 64, 64]); d("k_tri", [2, 64, 64])
        d("nab", [L, NA_HEADS, self.nty, 128, 128])
        o = self.dout
        o("y_s", [c.TL, c.D]); o("y_p", [c.NB * c.S, c.D])
        o("o_na_k", [c.NB, L, c.S, c.NAW]); o("o_na_v", [c.NB, L, c.S, c.NAW])
        o("o_gdn", [c.NB, L, 2, GDN_HEADS, 128, 128])
        o("o_d_k", [c.NB, L, c.S, c.DW]); o("o_d_v", [c.NB, L, c.S, c.DW])
        s = self.dscr
        s("XR", [c.D, c.TT], F32)
        s("QNA", [c.NAW, c.TT], BF16); s("KNA", [c.NAW, c.TT], BF16); s("VNA", [c.TT, c.NAW], BF16)
        s("GQ", [c.GW, c.TT], F32); s("GK", [c.GW, c.TT], F32); s("GV", [c.GW, c.TT], F32)
        s("GZT", [c.TT, c.GW], F32); s("GAB", [c.TT, 32], F32)
        s("DQ", [c.DW, c.TT], BF16); s("DK", [c.DW, c.TT], BF16); s("DV", [c.TT, c.DW], BF16)
        s("GATE", [3 * c.D, c.TT], BF16)
        s("ONA", [c.NAW, c.TT], BF16); s("OG", [c.GW, c.TT], BF16); s("OD", [c.DW, c.TT], BF16)
        s("GQN", [c.GW, c.TT], F32); s("GKN", [c.GW, c.TT], F32)
        s("GKT", [c.TT, c.GW], F32); s("GVT", [c.TT, c.GW], F32)
        s("GAB2", [c.TT, 32], F32); s("GOF", [c.TT, c.GW], F32)

    def build(self):
        c = self.cfg
        nc = self.nc
        P = self.P
        self.plan, self.na_tiles = na_tile_plan(c.TL // GRID_W)
        self.nty = len(self.na_tiles)
        self.declare()
        with ExitStack() as st:
            self.AW = 46800
            self.arena = st.enter_context(nc.sbuf_tensor("arena", [128, self.AW], F32))
            self.ps = st.enter_context(nc.psum_tensor("ps", [128, 8, 512], F32))
            self.psb = [Buf("ps%d" % i) for i in range(8)]
            self.aoff = 0
            self.const_end = 0
            self.consts()
            self.const_end = self.aoff
            self.modulation()
            self.const_end = self.aoff
            for seg in range(c.L + 1):
                if seg >= getattr(self, "seg_limit", 99):
                    break
                P.barrier()
                self.reset_arena()
                self.dense_segment(seg - 1 if seg > 0 else None, seg if seg < c.L else None)
                if seg < c.L:
                    P.barrier()
                    self.reset_arena()
                    self.mixers(seg)
            P.barrier()
            P.finalize(st)
            with nc.Block() as block:
                P.emit(block)
        return nc

    def consts(self):
        c, P = self.cfg, self.P
        self.b_const = Buf("const")
        o = self.alloc(128); self.identF = self.f32v(o, 128)
        o = self.alloc(128); self.permF = self.f32v(o, 128)
        o = self.alloc(128); self.onesF = self.f32v(o, 128)
        o = self.alloc(64); self.identB = self.bf16v(o, 128)
        o = self.alloc(64); self.onesB = self.bf16v(o, 128)
        P.dma("sp", self.identF, self.dram["k_ident"], writes=[self.b_const])
        P.dma("sp", self.permF, self.dram["k_perm"], writes=[self.b_const])
        o = self.alloc(8); self.epsT = self.f32v(o, 1)
        P.add("dve", lambda e: e.memset(self.epsT, EPS), writes=[self.b_const])
        P.add("dve", lambda e: e.memset(self.onesF, 1.0), writes=[self.b_const])
        P.add("dve", lambda e: e.memset(self.onesB, 1.0), writes=[self.b_const])
        P.add("dve", lambda e: e.tensor_copy(out=self.identB, in_=self.identF), writes=[self.b_const])

    def wpanel_load(self, slot, w2d, r0, kc, col0, ncols):
        P = self.P
        view = self.wp[slot][:, 0:kc * ncols].rearrange("p (k n) -> p k n", k=kc)
        step = max(1, 512 // max(1, (ncols * 4) // 512)) if False else 4
        for k0 in range(0, kc, step):
            k1 = min(kc, k0 + step)
            P.dma("pool", view[:, k0:k1, :],
                  w2d[r0 + k0 * 128:r0 + k1 * 128, col0:col0 + ncols].rearrange("(k p) n -> p k n", p=128),
                  writes=[self.wpb[slot]], pool="w")
        return view

    def next_wslot(self):
        s = self.wslot
        self.wslot = (self.wslot + 1) % len(self.wp)
        return s

    def next_bank(self):
        b = self.banks[self.bank_i % len(self.banks)]
        self.bank_i += 1
        return b

    def modulation(self):
        c, P, ps = self.cfg, self.P, self.ps
        DC = c.DC
        b_m = Buf("modtmp")
        save = self.aoff
        o = self.alloc(DC * 2); cT = self.f32v(o, DC * 2).rearrange("p (k g) -> p k g", g=2)
        o = self.alloc(DC); cTb = self.bf16v(o, DC * 2).rearrange("p (k g) -> p k g", g=2)
        o = self.alloc(9 * c.D); brow = self.arena[0:1, o:o + 9 * c.D]
        o = self.alloc(2); ones2 = self.arena[0:1, o:o + 2]
        self.wp = []
        self.wpb = []
        for i in range(3):
            o = self.alloc(DC * 512 // 2)
            self.wp.append(self.bf16v(o, DC * 512)); self.wpb.append(Buf("wp%d" % i))
        self.wslot = 0
        npre = []
        self.modtab = {}
        keep = []
        P.add("dve", lambda e: e.memset(ones2, 1.0), writes=[b_m])
        for g_ in range(2):
            P.dma("sp", cT[:, :, g_], self.dram["cvec"][g_].rearrange("(k p) -> p k", p=128), writes=[b_m],
                  allow_slow_non_contiguous=True)
        P.add("act", lambda e: e.activation(out=cTb, in_=cT, func=AF.Silu), reads=[b_m], writes=[b_m])
        self.aoff_keep = None
        tabs = {}
        for l in range(c.L):
            P.dma("sp", brow, self.dram["b_mod"][l:l + 1, :], writes=[b_m])
            nch = 9 * DC
            modps = ps[:, 7, 0:nch * 2].rearrange("p (n g) -> p n g", g=2)
            bps = self.psb[7]
            for n0 in range(0, 9 * c.D, 512):
                slot = self.next_wslot()
                ncl = min(512, 9 * c.D - n0)
                wv = self.wpanel_load(slot, self.dram["w_mod"][l], 0, DC, n0, ncl)
                for j in range(ncl // 128):
                    n = n0 // 128 + j
                    for k in range(DC):
                        P.add("pe", lambda e, wv=wv, k=k, j=j, n=n: e.matmul(
                            modps[:, n, :], lhsT=wv[:, k, j * 128:(j + 1) * 128], rhs=cTb[:, k, :],
                            start=(k == 0), stop=False), reads=[self.wpb[slot], b_m], writes=[bps])
                    P.add("pe", lambda e, n=n: e.matmul(
                        modps[:, n, :], lhsT=brow[0:1, n * 128:(n + 1) * 128], rhs=ones2,
                        start=False, stop=True), reads=[b_m], writes=[bps])
            tabs[l] = None
            o = self._top_alloc(9 * DC * 2)
            mv = self.f32v(o, 9 * DC * 2).rearrange("p (i k g) -> p i k g", i=9, g=2)
            P.add("act", lambda e, mv=mv, modps=modps: e.copy(out=mv, in_=modps.rearrange("p (i k) g -> p i k g", i=9)),
                  reads=[], writes=[bps, self.b_const])
            o = self._top_alloc(3 * DC); npre_l = self.f32v(o, 3 * DC).rearrange("p (i k) -> p i k", i=3)
            o = self._top_alloc(3 * DC); npost_l = self.f32v(o, 3 * DC).rearrange("p (i k) -> p i k", i=3)
            for i_ in range(3):
                P.dma("sp", npre_l[:, i_, :], self.dram["norm_pre"][l, i_].rearrange("(k p) -> p k", p=128),
                      writes=[self.b_const], allow_slow_non_contiguous=True)
                P.dma("sp", npost_l[:, i_, :], self.dram["norm_post"][l, i_].rearrange("(k p) -> p k", p=128),
                      writes=[self.b_const], allow_slow_non_contiguous=True)
            for g in range(2):
                o = self._top_alloc(3 * DC); A = self.f32v(o, 3 * DC).rearrange("p (i k) -> p i k", i=3)
                o = self._top_alloc(3 * DC); Bt = self.f32v(o, 3 * DC).rearrange("p (i k) -> p i k", i=3)
                o = self._top_alloc(3 * DC); G = self.f32v(o, 3 * DC).rearrange("p (i k) -> p i k", i=3)
                for i in range(3):
                    coef = 1.0 if i == 1 else 0.5
                    P.add("dve", lambda e, A=A, mv=mv, npre_l=npre_l, i=i, g=g: e.scalar_tensor_tensor(
                        out=A[:, i, :], in0=mv[:, 3 * i + 1, :, g], scalar=1.0, in1=npre_l[:, i, :],
                        op0=ALU.add, op1=ALU.mult), reads=[self.b_const], writes=[self.b_const])
                    P.add("dve", lambda e, Bt=Bt, mv=mv, i=i, g=g: e.tensor_copy(out=Bt[:, i, :], in_=mv[:, 3 * i, :, g]),
                          reads=[self.b_const], writes=[self.b_const])
                    P.add("dve", lambda e, G=G, mv=mv, npost_l=npost_l, i=i, g=g, coef=coef: e.scalar_tensor_tensor(
                        out=G[:, i, :], in0=mv[:, 3 * i + 2, :, g], scalar=coef, in1=npost_l[:, i, :],
                        op0=ALU.mult, op1=ALU.mult), reads=[self.b_const], writes=[self.b_const])
                self.modtab[(l, g)] = (A, Bt, G)
        self.P.barrier()
        self.aoff = save

    def _top_alloc(self, words):
        if not hasattr(self, "topoff"):
            self.topoff = self.AW
        self.topoff -= (words + 7) // 8 * 8
        self.AW_eff = self.topoff
        return self.topoff

    def dense_alloc(self):
        c = self.cfg
        DC, FC = c.DC, c.FC
        A = {}
        def f32(name, n):
            A[name] = self.f32v(self.alloc(n, name), n); A["b_" + name] = Buf(name)
        def b16(name, n):
            A[name] = self.bf16v(self.alloc((n + 1) // 2, name), n); A["b_" + name] = Buf(name)
        f32("x", DC * 512)
        b16("h", DC * 512)
        b16("act", max((FC + 1) // 2, 8) * 512)
        f32("y", DC * 512)
        for i in range(3):
            f32("t%d" % i, 512)
        f32("rstd", 512)
        for i in range(4):
            b16("sb%d" % i, 512)
            f32("sf%d" % i, 512)
        f32("cos", 512); f32("sin", 512)
        self.wp, self.wpb = [], []
        for i in range(3):
            o = self.alloc(8192 // 2)
            self.wp.append(self.bf16v(o, 8192)); self.wpb.append(Buf("wp%d" % i))
        self.wslot = 0
        self.banks = [2, 3, 4, 5, 6, 7]
        self.bank_i = 0
        self.A = A
        self.rr = {"t": 0, "sb": 0, "sf": 0}
        return A

    def rot(self, kind, n):
        i = self.rr[kind]
        self.rr[kind] = (i + 1) % n
        return "%s%d" % (kind, i)

    def x3(self, name, nt):
        c = self.cfg
        return self.A[name][:, 0:c.DC * nt].rearrange("p (k t) -> p k t", t=nt)

    def norm_stats(self, src3, nt, srcbuf, kc, scale):
        P, ps, A = self.P, self.ps, self.A
        bank = 0 if self.bank_i % 2 == 0 else 1
        for k in range(kc):
            tn = self.rot("t", 3)
            t = A[tn][:, 0:nt]
            P.add("act", lambda e, t=t, k=k: e.activation(out=t, in_=src3[:, k, :], func=AF.Square),
                  reads=[srcbuf], writes=[A["b_" + tn]])
            P.add("pe", lambda e, t=t, k=k, bank=bank: e.matmul(ps[:, bank, 0:nt], lhsT=self.onesF, rhs=t,
                                                                start=(k == 0), stop=(k == kc - 1)),
                  reads=[A["b_" + tn], self.b_const], writes=[self.psb[bank]])
        rstd = A["rstd"][:, 0:nt]
        P.add("act", lambda e: e.activation(out=rstd, in_=ps[:, bank, 0:nt], func=AF.Ln, bias=self.epsT, scale=scale),
              reads=[self.b_const], writes=[self.psb[bank], A["b_rstd"]])
        P.add("act", lambda e: e.activation(out=rstd, in_=rstd, func=AF.Exp, scale=-0.5), writes=[A["b_rstd"]])
        return rstd

    def prenorm(self, l, i, g, nt):
        c, P, A = self.cfg, self.P, self.A
        x3, h3 = self.x3("x", nt), self.x3("h", nt)
        rstd = self.norm_stats(x3, nt, A["b_x"], c.DC, 1.0 / c.D)
        At, Bt, _ = self.modtab[(l, g)]
        for k in range(c.DC):
            tn = self.rot("t", 3)
            t = A[tn][:, 0:nt]
            P.add("dve", lambda e, t=t, k=k: e.scalar_tensor_tensor(out=t, in0=x3[:, k, :], scalar=At[:, i, k:k + 1],
                                                                   in1=rstd, op0=ALU.mult, op1=ALU.mult),
                  reads=[A["b_x"], A["b_rstd"], self.b_const], writes=[A["b_" + tn]])
            P.add("act", lambda e, t=t, k=k: e.activation(out=h3[:, k, :], in_=t, func=AF.Identity,
                                                          bias=Bt[:, i, k:k + 1], scale=1.0),
                  reads=[A["b_" + tn], self.b_const], writes=[A["b_h"]])

    def resid(self, l, i, g, nt):
        c, P, A = self.cfg, self.P, self.A
        x3, y3 = self.x3("x", nt), self.x3("y", nt)
        rstd = self.norm_stats(y3, nt, A["b_y"], c.DC, 1.0 / c.D)
        _, _, G = self.modtab[(l, g)]
        for k in range(c.DC):
            tn = self.rot("t", 3)
            t = A[tn][:, 0:nt]
            P.add("dve", lambda e, t=t, k=k: e.scalar_tensor_tensor(out=t, in0=y3[:, k, :], scalar=G[:, i, k:k + 1],
                                                                   in1=rstd, op0=ALU.mult, op1=ALU.mult),
                  reads=[A["b_y"], A["b_rstd"], self.b_const], writes=[A["b_" + tn]])
            P.add("pool", lambda e, t=t, k=k: e.tensor_tensor(out=x3[:, k, :], in0=x3[:, k, :], in1=t, op=ALU.add),
                  reads=[A["b_" + tn]], writes=[A["b_x"]])

    def gemm_fm(self, xname, kc, nt, w2d, r0, cols, evac):
        P, ps, A = self.P, self.ps, self.A
        x3 = A[xname][:, 0:kc * nt].rearrange("p (k t) -> p k t", t=nt)
        pc = min(512, (8192 // kc) // 128 * 128)
        i = 0
        while i < len(cols):
            j = i
            while j + 1 < len(cols) and cols[j + 1] == cols[j] + 128 and (cols[j + 1] + 128 - cols[i]) <= pc:
                j += 1
            slot = self.next_wslot()
            ncols = cols[j] + 128 - cols[i]
            wv = self.wpanel_load(slot, w2d, r0, kc, cols[i], ncols)
            for q in range(i, j + 1):
                bank = self.next_bank()
                off = cols[q] - cols[i]
                for k in range(kc):
                    P.add("pe", lambda e, wv=wv, k=k, off=off, bank=bank: e.matmul(
                        ps[:, bank, 0:nt], lhsT=wv[:, k, off:off + 128], rhs=x3[:, k, :],
                        start=(k == 0), stop=(k == kc - 1)),
                        reads=[self.wpb[slot], A["b_" + xname]], writes=[self.psb[bank]])
                evac(q, cols[q], bank)
            i = j + 1

    def gemm_tm(self, xname, kc, nt, w2d, r0, col0, ncols, evac):
        P, ps, A = self.P, self.ps, self.A
        x3 = A[xname][:, 0:kc * nt].rearrange("p (k t) -> p k t", t=nt)
        for c0 in range(col0, col0 + ncols, 512):
            ncl = min(512, col0 + ncols - c0)
            slot = self.next_wslot()
            wv = self.wpanel_load(slot, w2d, r0, kc, c0, ncl)
            for tt in range(nt // 128):
                bank = self.next_bank()
                for k in range(kc):
                    P.add("pe", lambda e, wv=wv, k=k, tt=tt, bank=bank, ncl=ncl: e.matmul(
                        ps[:, bank, 0:ncl], lhsT=x3[:, k, tt * 128:(tt + 1) * 128], rhs=wv[:, k, :],
                        start=(k == 0), stop=(k == kc - 1)),
                        reads=[self.wpb[slot], A["b_" + xname]], writes=[self.psb[bank]])
                evac(tt, c0, ncl, bank)

    def ffn(self, l, which, nt):
        c, P, ps, A = self.cfg, self.P, self.ps, self.A
        wgu = self.dram["ffn%d_w_gu" % which][l]
        wdn = self.dram["ffn%d_w_dn" % which][l]
        FH = (c.FC + 1) // 2
        act3 = A["act"][:, 0:FH * nt].rearrange("p (k t) -> p k t", t=nt)
        h3 = self.x3("h", nt)
        y3 = self.x3("y", nt)
        for hf in range(2):
            jlo, jhi = hf * FH, min(c.FC, (hf + 1) * FH)
            for j0 in range(jlo, jhi, 2):
                nj = min(2, jhi - j0)
                slot = self.next_wslot()
                wv = self.wp[slot][:, 0:c.DC * 2 * nj * 128].rearrange("p (k n) -> p k n", k=c.DC)
                for half in range(2):
                    for k0 in range(0, c.DC, 4):
                        k1 = min(c.DC, k0 + 4)
                        P.dma("pool", wv[:, k0:k1, half * nj * 128:(half + 1) * nj * 128],
                              wgu[k0 * 128:k1 * 128, half * c.DFF + j0 * 128: half * c.DFF + (j0 + nj) * 128]
                              .rearrange("(k p) n -> p k n", p=128), writes=[self.wpb[slot]], pool="w")
                for jj in range(nj):
                    j = j0 + jj
                    bg, bu = self.next_bank(), self.next_bank()
                    for half, bank in ((0, bg), (1, bu)):
                        off = half * nj * 128 + jj * 128
                        for k in range(c.DC):
                            P.add("pe", lambda e, wv=wv, k=k, off=off, bank=bank: e.matmul(
                                ps[:, bank, 0:nt], lhsT=wv[:, k, off:off + 128], rhs=h3[:, k, :],
                                start=(k == 0), stop=(k == c.DC - 1)),
                                reads=[self.wpb[slot], A["b_h"]], writes=[self.psb[bank]])
                    sn = self.rot("sf", 4)
                    sg = A[sn][:, 0:nt]
                    P.add("act", lambda e, sg=sg, bg=bg: e.activation(out=sg, in_=ps[:, bg, 0:nt], func=AF.Silu),
                          writes=[self.psb[bg], A["b_" + sn]])
                    P.add("dve", lambda e, sg=sg, bu=bu, j=j - jlo: e.tensor_tensor(out=act3[:, j, :], in0=sg,
                                                                            in1=ps[:, bu, 0:nt], op=ALU.mult),
                          reads=[A["b_" + sn]], writes=[self.psb[bu], A["b_act"]])

            def ev(q, col0, bank, hf=hf):
                if hf == 0:
                    P.add("act", lambda e: e.copy(out=y3[:, q, :], in_=ps[:, bank, 0:nt]),
                          writes=[self.psb[bank], A["b_y"]])
                else:
                    P.add("dve", lambda e: e.tensor_tensor(out=y3[:, q, :], in0=y3[:, q, :], in1=ps[:, bank, 0:nt],
                                                           op=ALU.add), writes=[self.psb[bank], A["b_y"]])
            self.gemm_fm("act", jhi - jlo, nt, wdn, jlo * 128, [k * 128 for k in range(c.DC)], ev)

    def load_x_input(self, blk):
        c, P, ps, A = self.cfg, self.P, self.ps, self.A
        t0, nt, g = blk
        src = self.dram["xs"] if g == 0 else self.dram["xp"]
        sb = self.dbuf["xs" if g == 0 else "xp"]
        r0 = t0 if g == 0 else t0 - c.TL
        x3 = self.x3("x", nt)
        stage = A["y"][:, 0:c.D]
        for tt in range(nt // 128):
            P.dma("sp", stage, src[r0 + tt * 128:r0 + (tt + 1) * 128, :], reads=[sb], writes=[A["b_y"]])
            for k in range(c.DC):
                bank = self.next_bank()
                P.add("pe", lambda e, k=k, bank=bank: e.transpose(out=ps[:, bank, 0:128], in_=stage[:, k * 128:(k + 1) * 128],
                                                                  identity=self.identF),
                      reads=[A["b_y"], self.b_const], writes=[self.psb[bank]])
                P.add("act" if k % 2 else "dve",
                      (lambda e, k=k, bank=bank, tt=tt: e.copy(out=x3[:, k, tt * 128:(tt + 1) * 128], in_=ps[:, bank, 0:128])) if k % 2 else
                      (lambda e, k=k, bank=bank, tt=tt: e.tensor_copy(out=x3[:, k, tt * 128:(tt + 1) * 128], in_=ps[:, bank, 0:128])),
                      writes=[self.psb[bank], A["b_x"]])

    def store_y_output(self, blk):
        c, P, ps, A = self.cfg, self.P, self.ps, self.A
        t0, nt, g = blk
        dst = self.dram["y_s"] if g == 0 else self.dram["y_p"]
        db = self.dbuf["y_s" if g == 0 else "y_p"]
        r0 = t0 if g == 0 else t0 - c.TL
        x3 = self.x3("x", nt)
        stage = A["y"][:, 0:c.D]
        for tt in range(nt // 128):
            for k in range(c.DC):
                bank = self.next_bank()
                P.add("pe", lambda e, k=k, bank=bank, tt=tt: e.transpose(out=ps[:, bank, 0:128], in_=x3[:, k, tt * 128:(tt + 1) * 128],
                                                                         identity=self.identF),
                      reads=[A["b_x"], self.b_const], writes=[self.psb[bank]])
                P.add("act", lambda e, k=k, bank=bank: e.copy(out=stage[:, k * 128:(k + 1) * 128], in_=ps[:, bank, 0:128]),
                      writes=[self.psb[bank], A["b_y"]])
            P.dma("sp", dst[r0 + tt * 128:r0 + (tt + 1) * 128, :], stage, reads=[A["b_y"]], writes=[db], pool="st")

    def dense_segment(self, lp, ln):
        c, P, A = self.cfg, self.P, self.dense_alloc()
        for blk in c.blocks:
            t0, nt, g = blk
            if lp is None:
                self.load_x_input(blk)
            else:
                P.dma("sp", self.x3("x", nt), self.dram["XR"][:, t0:t0 + nt].rearrange("(k p) t -> p k t", p=128),
                      reads=[self.dbuf["XR"]], writes=[A["b_x"]])
                self.merge(lp, blk)
                self.prenorm(lp, 2, g, nt)
                self.ffn(lp, 2, nt)
                self.resid(lp, 2, g, nt)
            if ln is not None:
                self.prenorm(ln, 0, g, nt)
                self.ffn(ln, 1, nt)
                self.resid(ln, 0, g, nt)
                self.prenorm(ln, 1, g, nt)
                self.proj(ln, blk)
                P.dma("sp", self.dram["XR"][:, t0:t0 + nt].rearrange("(k p) t -> p k t", p=128), self.x3("x", nt),
                      reads=[A["b_x"]], writes=[self.dbuf["XR"]], pool="st")
            else:
                self.store_y_output(blk)

    def proj(self, l, blk):
        c, P, ps, A = self.cfg, self.P, self.ps, self.A
        t0, nt, g = blk
        w = self.dram["w_in"][l]
        o = c.offs
        D = self.dram
        B = self.dbuf

        def fm_store(dst, dt_bf16, col_base, func=None):
            def ev(q, col0, bank):
                r = col0 - col_base
                sn = self.rot("sb", 4) if dt_bf16 else self.rot("sf", 4)
                sv = A[sn][:, 0:nt]
                if func is None:
                    P.add("act", lambda e: e.copy(out=sv, in_=ps[:, bank, 0:nt]), writes=[self.psb[bank], A["b_" + sn]])
                else:
                    P.add("act", lambda e: e.activation(out=sv, in_=ps[:, bank, 0:nt], func=func),
                          writes=[self.psb[bank], A["b_" + sn]])
                P.dma("sp", D[dst][r:r + 128, t0:t0 + nt], sv, reads=[A["b_" + sn]], writes=[B[dst]], pool="st")
            return ev

        def rope_store(dst, col_base):
            cos, sin = A["cos"][:, 0:nt], A["sin"][:, 0:nt]

            def ev(q, col0, bank):
                r = col0 - col_base
                fn_ = self.rot("sf", 4); qf = A[fn_][:, 0:nt]
                P.add("act", lambda e: e.copy(out=qf, in_=ps[:, bank, 0:nt]), writes=[self.psb[bank], A["b_" + fn_]])
                b2 = self.next_bank()
                P.add("pe", lambda e: e.matmul(ps[:, b2, 0:nt], lhsT=self.permF, rhs=qf, start=True, stop=True),
                      reads=[A["b_" + fn_], self.b_const], writes=[self.psb[b2]])
                tn = self.rot("t", 3); t = A[tn][:, 0:nt]
                P.add("dve", lambda e: e.tensor_tensor(out=t, in0=ps[:, b2, 0:nt], in1=sin, op=ALU.mult),
                      reads=[A["b_sin"]], writes=[self.psb[b2], A["b_" + tn]])
                P.add("pool", lambda e: e.tensor_tensor(out=qf, in0=qf, in1=cos, op=ALU.mult),
                      reads=[A["b_cos"]], writes=[A["b_" + fn_]])
                sn = self.rot("sb", 4); sv = A[sn][:, 0:nt]
                P.add("dve", lambda e: e.tensor_tensor(out=sv, in0=qf, in1=t, op=ALU.add),
                      reads=[A["b_" + fn_], A["b_" + tn]], writes=[A["b_" + sn]])
                P.dma("sp", D[dst][r:r + 128, t0:t0 + nt], sv, reads=[A["b_" + sn]], writes=[B[dst]], pool="st")
            return ev

        def chunks(i):
            return list(range(int(o[i]), int(o[i + 1]), 128))

        self.gemm_fm("h", c.DC, nt, w, 0, chunks(0), fm_store("QNA", True, int(o[0])))
        self.gemm_fm("h", c.DC, nt, w, 0, chunks(1), fm_store("KNA", True, int(o[1])))
        self.gemm_fm("h", c.DC, nt, w, 0, chunks(3), fm_store("GQ", False, int(o[3])))
        self.gemm_fm("h", c.DC, nt, w, 0, chunks(4), fm_store("GK", False, int(o[4])))
        self.gemm_fm("h", c.DC, nt, w, 0, chunks(5), fm_store("GV", False, int(o[5])))
        if g == 0:
            P.dma("sp", A["cos"][:, 0:nt], D["k_cos"][:, t0:t0 + nt], writes=[A["b_cos"]])
            P.dma("sp", A["sin"][:, 0:nt], D["k_sin"][:, t0:t0 + nt], writes=[A["b_sin"]])
            self.gemm_fm("h", c.DC, nt, w, 0, chunks(8), rope_store("DQ", int(o[8])))
            self.gemm_fm("h", c.DC, nt, w, 0, chunks(9), rope_store("DK", int(o[9])))
        else:
            self.gemm_fm("h", c.DC, nt, w, 0, chunks(8), fm_store("DQ", True, int(o[8])))
            self.gemm_fm("h", c.DC, nt, w, 0, chunks(9), fm_store("DK", True, int(o[9])))
        self.gemm_fm("h", c.DC, nt, w, 0, chunks(11), fm_store("GATE", True, int(o[11]), AF.Sigmoid))

        def tm_store(dst, col_base, bf, func=None, cache=None):
            def ev(tt, c0, ncl, bank):
                r = c0 - col_base
                tok = t0 + tt * 128
                fn_ = self.rot("sf", 4); sv = A[fn_][:, 0:ncl]
                if func is None:
                    P.add("act", lambda e: e.copy(out=sv, in_=ps[:, bank, 0:ncl]), writes=[self.psb[bank], A["b_" + fn_]])
                else:
                    P.add("act", lambda e: e.activation(out=sv, in_=ps[:, bank, 0:ncl], func=func),
                          writes=[self.psb[bank], A["b_" + fn_]])
                if dst is not None:
                    if bf:
                        sn = self.rot("sb", 4); sb_ = A[sn][:, 0:ncl]
                        P.add("dve", lambda e: e.tensor_copy(out=sb_, in_=sv), reads=[A["b_" + fn_]], writes=[A["b_" + sn]])
                        P.dma("sp", D[dst][tok:tok + 128, r:r + ncl], sb_, reads=[A["b_" + sn]], writes=[B[dst]], pool="st")
                    else:
                        P.dma("sp", D[dst][tok:tok + 128, r:r + ncl], sv, reads=[A["b_" + fn_]], writes=[B[dst]], pool="st")
                if cache is not None and g == 1:
                    ct = tok - c.TL
                    b_, s_ = ct // c.S, ct % c.S
                    P.dma("sp", D[cache][b_, l, s_:s_ + 128, r:r + ncl], sv, reads=[A["b_" + fn_]], writes=[B[cache]], pool="st")
            return ev

        self.gemm_tm("h", c.DC, nt, w, 0, int(o[2]), c.NAW, tm_store("VNA", int(o[2]), True, cache="o_na_v"))
        self.gemm_tm("h", c.DC, nt, w, 0, int(o[6]), c.GW, tm_store("GZT", int(o[6]), False, func=AF.Silu))
        self.gemm_tm("h", c.DC, nt, w, 0, int(o[7]), 32, tm_store("GAB", int(o[7]), False))
        self.gemm_tm("h", c.DC, nt, w, 0, int(o[10]), c.DW, tm_store("DV", int(o[10]), True, cache="o_d_v"))
        if g == 1:
            self.gemm_tm("h", c.DC, nt, w, 0, int(o[1]), c.NAW, tm_store(None, int(o[1]), False, cache="o_na_k"))
            self.gemm_tm("h", c.DC, nt, w, 0, int(o[9]), c.DW, tm_store(None, int(o[9]), False, cache="o_d_k"))

    def merge(self, l, blk):
        c, P, ps, A = self.cfg, self.P, self.ps, self.A
        t0, nt, g = blk
        D, B = self.dram, self.dbuf
        y3, h3 = self.x3("y", nt), self.x3("h", nt)
        o3 = A["act"][:, 0:8 * nt].rearrange("p (k t) -> p k t", t=nt)
        for br, (src, wn) in enumerate((("ONA", "w_branch_na"), ("OG", "w_branch_gdn"), ("OD", "w_branch_diff"))):
            P.dma("sp", o3, D[src][:, t0:t0 + nt].rearrange("(k p) t -> p k t", p=128), reads=[B[src]], writes=[A["b_act"]])

            def ev(q, col0, bank, br=br):
                sn = self.rot("sb", 4); gt = A[sn][:, 0:nt]
                r = br * c.D + col0
                P.dma("sp", gt, D["GATE"][r:r + 128, t0:t0 + nt], reads=[B["GATE"]], writes=[A["b_" + sn]])
                if br == 0:
                    P.add("dve", lambda e: e.tensor_tensor(out=y3[:, q, :], in0=gt, in1=ps[:, bank, 0:nt], op=ALU.mult),
                          reads=[A["b_" + sn]], writes=[self.psb[bank], A["b_y"]])
                else:
                    tn = self.rot("t", 3); t = A[tn][:, 0:nt]
                    P.add("dve", lambda e: e.tensor_tensor(out=t, in0=gt, in1=ps[:, bank, 0:nt], op=ALU.mult),
                          reads=[A["b_" + sn]], writes=[self.psb[bank], A["b_" + tn]])
                    P.add("pool", lambda e: e.tensor_tensor(out=y3[:, q, :], in0=y3[:, q, :], in1=t, op=ALU.add),
                          reads=[A["b_" + tn]], writes=[A["b_y"]])
            self.gemm_fm("act", 8, nt, D[wn][l], 0, [k * 128 for k in range(c.DC)], ev)
        for k in range(c.DC):
            P.add("act", lambda e, k=k: e.copy(out=h3[:, k, :], in_=y3[:, k, :]), reads=[A["b_y"]], writes=[A["b_h"]])

        def ev2(q, col0, bank):
            P.add("act", lambda e: e.copy(out=y3[:, q, :], in_=ps[:, bank, 0:nt]), writes=[self.psb[bank], A["b_y"]])
        self.gemm_fm("h", c.DC, nt, D["w_out"][l], 0, [k * 128 for k in range(c.DC)], ev2)
        self.resid(l, 1, g, nt)

    def mixers(self, l):
        c, P = self.cfg, self.P
        self.mix_setup(l)
        mark = self.aoff
        for h in range(NA_HEADS):
            self.aoff = mark
            self.na_head(l, h)
            P.barrier()
        P.barrier()
        for h in range(DIFF_HEADS):
            self.aoff = mark
            self.diff_head(l, h)
            P.barrier()
        self.aoff = mark
        self.gdn(l)

    def mix_setup(self, l):
        c, P, ps = self.cfg, self.P, self.ps
        D, B = self.dram, self.dbuf
        PC = c.PAST // 128
        M = {}
        self.M = M
        M["b"] = Buf("mixconst")
        o = self.alloc(8 * c.PAST // 2); M["KcNA"] = self.bf16v(o, 8 * c.PAST).rearrange("p (h t) -> p h t", h=8)
        o = self.alloc(8 * c.PAST // 2); M["KcD"] = self.bf16v(o, 8 * c.PAST).rearrange("p (h t) -> p h t", h=8)
        o = self.alloc(PC * 1024 // 2); M["VcNA"] = self.bf16v(o, PC * 1024).rearrange("p (k n) -> p k n", k=PC)
        o = self.alloc(PC * 1024 // 2); M["VcD"] = self.bf16v(o, PC * 1024).rearrange("p (k n) -> p k n", k=PC)
        o = self.alloc(8); M["lam"] = self.f32v(o, 1)
        o = self.alloc(8); M["nlam"] = self.f32v(o, 1)
        o = self.alloc(8); M["dnw"] = self.f32v(o, 2)
        mark = self.aoff
        o = self.alloc(1024); stage = self.f32v(o, 1024)
        bst = Buf("stage")
        for name, src in (("KcNA", "cna_k"), ("KcD", "cd_k")):
            for k in range(PC):
                P.dma("sp", stage, D[src][l, k * 128:(k + 1) * 128, :], writes=[bst])
                for h in range(8):
                    bank = 6 + (h % 2)
                    P.add("pe", lambda e, h=h, bank=bank: e.transpose(out=ps[:, bank, 0:128], in_=stage[:, h * 128:(h + 1) * 128],
                                                                      identity=self.identF),
                          reads=[bst, self.b_const], writes=[self.psb[bank]])
                    P.add("act", lambda e, h=h, bank=bank, k=k, name=name: e.copy(out=M[name][:, h, k * 128:(k + 1) * 128],
                                                                                 in_=ps[:, bank, 0:128]),
                          writes=[self.psb[bank], M["b"]])
        for name, src in (("VcNA", "cna_v"), ("VcD", "cd_v")):
            P.dma("pool", M[name], D[src][l].rearrange("(k p) n -> p k n", p=128), writes=[M["b"]], pool="w")
        lam_init = 0.8 - 0.6 * math.exp(-0.3 * l)
        o = self.alloc(8); lt = self.f32v(o, 4)
        o = self.alloc(8); pr = self.f32v(o, 2)
        for i_ in range(4):
            P.dma("sp", lt[:, i_:i_ + 1], D["diff_lambda"][l, i_].rearrange("(p o) -> p o", o=1), writes=[bst],
                  allow_slow_non_contiguous=True)
        P.add("dve", lambda e: e.tensor_tensor(out=pr[:, 0:1], in0=lt[:, 0:1], in1=lt[:, 1:2], op=ALU.mult), reads=[bst], writes=[bst])
        P.add("dve", lambda e: e.tensor_tensor(out=pr[:, 1:2], in0=lt[:, 2:3], in1=lt[:, 3:4], op=ALU.mult), reads=[bst], writes=[bst])
        P.add("pe", lambda e: e.matmul(ps[:, 6, 0:2], lhsT=self.onesF, rhs=pr, start=True, stop=True),
              reads=[bst, self.b_const], writes=[self.psb[6]])
        P.add("act", lambda e: e.activation(out=pr, in_=ps[:, 6, 0:2], func=AF.Exp), writes=[self.psb[6], bst])
        P.add("dve", lambda e: e.scalar_tensor_tensor(out=M["lam"], in0=pr[:, 0:1], scalar=lam_init, in1=pr[:, 1:2],
                                                      op0=ALU.add, op1=ALU.subtract), reads=[bst], writes=[M["b"]])
        P.add("dve", lambda e: e.tensor_scalar(out=M["nlam"], in0=M["lam"], scalar1=-1.0, scalar2=None, op0=ALU.mult),
              writes=[M["b"]])
        for d_ in range(2):
            P.dma("sp", M["dnw"][:, d_:d_ + 1], D["diff_norm"][l, d_ * 128:(d_ + 1) * 128].rearrange("(p o) -> p o", o=1),
                  writes=[M["b"]], allow_slow_non_contiguous=True)
        P.add("dve", lambda e: e.tensor_scalar(out=M["dnw"], in0=M["dnw"], scalar1=1.0 - lam_init, scalar2=None, op0=ALU.mult),
              writes=[M["b"]])
        P.barrier()
        self.aoff = mark

    def seqs(self):
        c = self.cfg
        out = [(0, c.TL, True)]
        for b_ in range(c.NB):
            out.append((c.TL + b_ * c.S, c.S, False))
        return out

    def na_head(self, l, h):
        c, P, ps, M = self.cfg, self.P, self.ps, self.M
        D, B = self.dram, self.dbuf
        scale = HEAD_DIM ** -0.5
        TLc = c.TL // 128
        PC = c.PAST // 128
        Tmax = c.TL
        o = self.alloc(Tmax // 2); QT = self.bf16v(o, Tmax); bQ = Buf("QT")
        o = self.alloc(Tmax // 2); KT = self.bf16v(o, Tmax); bK = Buf("KT")
        o = self.alloc(Tmax // 2); Vt = self.bf16v(o, Tmax).rearrange("p (k d) -> p k d", d=128); bV = Buf("Vt")
        o = self.alloc(Tmax // 2); ost = self.bf16v(o, Tmax); bO = Buf("ost")
        o = self.alloc(self.nty * 128); bias = self.f32v(o, self.nty * 128).rearrange("p (t q) -> p t q", q=128); bB = Buf("bias")
        pTs, bP = [], []
        for i in range(2):
            o = self.alloc(8 * 128 // 2); pTs.append(self.bf16v(o, 1024)); bP.append(Buf("pT%d" % i))
        o = self.alloc(128); rden = self.f32v(o, 128); bR = Buf("rden")
        for ty0 in range(0, self.nty, 4):
            ty1 = min(self.nty, ty0 + 4)
            P.dma("sp", bias[:, ty0:ty1, :], D["nab"][l, h, ty0:ty1].rearrange("t k q -> k t q"), writes=[bB])
        P.add("act", lambda e: e.mul(out=bias, in_=bias, mul=1.0 / scale), writes=[bB])
        it = 0
        for (t0, T, lat) in self.seqs():
            nqt = T // 128
            r0 = h * 128
            P.dma("sp", QT[:, 0:T], D["QNA"][r0:r0 + 128, t0:t0 + T], reads=[B["QNA"]], writes=[bQ])
            P.dma("sp", KT[:, 0:T], D["KNA"][r0:r0 + 128, t0:t0 + T], reads=[B["KNA"]], writes=[bK])
            for k0 in range(0, nqt, 8):
                k1 = min(nqt, k0 + 8)
                P.dma("sp", Vt[:, k0:k1, :], D["VNA"][t0 + k0 * 128:t0 + k1 * 128, r0:r0 + 128].rearrange("(k p) d -> p k d", p=128),
                      reads=[B["VNA"]], writes=[bV])
            for j in range(nqt):
                if lat:
                    chunks = [("l", m, ty) for (m, ty) in self.plan[j]] + [("c", k, None) for k in range(PC)]
                else:
                    chunks = [("l", k, None) for k in range(nqt)]
                n = len(chunks)
                assert n <= 8
                sb0 = (it % 2) * 2
                S2 = ps[:, sb0:sb0 + 2, :].rearrange("p b f -> p (b f)")
                sbufs = [self.psb[sb0], self.psb[sb0 + 1]]
                q_ap = QT[:, j * 128:(j + 1) * 128]
                for i, (kind, kc, ty) in enumerate(chunks):
                    k_ap = KT[:, kc * 128:(kc + 1) * 128] if kind == "l" else M["KcNA"][:, h, kc * 128:(kc + 1) * 128]
                    P.add("pe", lambda e, i=i, k_ap=k_ap, ty=ty, S2=S2, q_ap=q_ap: e.matmul(
                        S2[:, i * 128:(i + 1) * 128], lhsT=k_ap, rhs=q_ap, start=True, stop=(ty is None)),
                        reads=[bK, bQ, M["b"]], writes=sbufs)
                    if ty is not None:
                        P.add("pe", lambda e, i=i, ty=ty, S2=S2: e.matmul(
                            S2[:, i * 128:(i + 1) * 128], lhsT=self.identF, rhs=bias[:, ty, :], start=False, stop=True),
                            reads=[bB, self.b_const], writes=sbufs)
                pi = it % 2
                pT = pTs[pi]
                P.add("act", lambda e, pT=pT, S2=S2, n=n: e.activation(out=pT[:, 0:n * 128], in_=S2[:, 0:n * 128],
                                                                      func=AF.Exp, scale=scale),
                      writes=sbufs + [bP[pi]])
                ob = 4 + (it % 2)
                for i, (kind, kc, ty) in enumerate(chunks):
                    v_ap = Vt[:, kc, :] if kind == "l" else M["VcNA"][:, kc, h * 128:(h + 1) * 128]
                    P.add("pe", lambda e, i=i, v_ap=v_ap, pT=pT, ob=ob, n=n: e.matmul(
                        ps[:, ob, 0:128], lhsT=v_ap, rhs=pT[:, i * 128:(i + 1) * 128], start=(i == 0), stop=(i == n - 1)),
                        reads=[bV, bP[pi], M["b"]], writes=[self.psb[ob]])
                for i in range(n):
                    P.add("pe", lambda e, i=i, pT=pT, ob=ob, n=n: e.matmul(
                        ps[:, ob, 128:256], lhsT=self.onesB, rhs=pT[:, i * 128:(i + 1) * 128], start=(i == 0), stop=(i == n - 1)),
                        reads=[bP[pi], self.b_const], writes=[self.psb[ob]])
                P.add("dve", lambda e, ob=ob: e.reciprocal(out=rden, in_=ps[:, ob, 128:256]), writes=[self.psb[ob], bR])
                P.add("dve", lambda e, ob=ob, j=j: e.tensor_tensor(out=ost[:, j * 128:(j + 1) * 128], in0=ps[:, ob, 0:128],
                                                                  in1=rden, op=ALU.mult),
                      reads=[bR], writes=[self.psb[ob], bO])
                it += 1
            P.dma("sp", D["ONA"][r0:r0 + 128, t0:t0 + T], ost[:, 0:T], reads=[bO], writes=[B["ONA"]], pool="st")

    def diff_head(self, l, h):
        c, P, ps, M = self.cfg, self.P, self.ps, self.M
        D, B = self.dram, self.dbuf
        scale = HEAD_DIM ** -0.5
        PC = c.PAST // 128
        Tmax = c.TL
        QT, KT, bQ, bK = [], [], Buf("dQT"), Buf("dKT")
        for m in range(2):
            o = self.alloc(Tmax // 2); QT.append(self.bf16v(o, Tmax))
            o = self.alloc(Tmax // 2); KT.append(self.bf16v(o, Tmax))
        o = self.alloc(Tmax); Vt = self.bf16v(o, Tmax * 2).rearrange("p (k d) -> p k d", d=256); bV = Buf("dVt")
        osts, bO = [], Buf("dost")
        for d_ in range(2):
            o = self.alloc(Tmax // 2); osts.append(self.bf16v(o, Tmax))
        NKmax = (c.TL + c.PAST) // 128
        pTs, bP = [], []
        for i in range(2):
            o = self.alloc(NKmax * 256); pTs.append(self.bf16v(o, NKmax * 512).rearrange("p (k q) -> p k q", q=512)); bP.append(Buf("dpT%d" % i))
        o = self.alloc(512); rden = self.f32v(o, 512); bR = Buf("drden")
        od, bod = [], Buf("od")
        for d_ in range(2):
            o = self.alloc(256); od.append(self.f32v(o, 256))
        o = self.alloc(256); t1 = self.f32v(o, 256); bt1 = Buf("dt1")
        sq, bsq = [], []
        for d_ in range(2):
            o = self.alloc(256); sq.append(self.f32v(o, 256)); bsq.append(Buf("dsq%d" % d_))
        o = self.alloc(256); rstd = self.f32v(o, 256); brs = Buf("drstd")
        it = 0
        qit = 0
        for (t0, T, lat) in self.seqs():
            nkl = T // 128
            QB = min(256, T)
            for m in range(2):
                r0 = (h * 2 + m) * 128
                P.dma("sp", QT[m][:, 0:T], D["DQ"][r0:r0 + 128, t0:t0 + T], reads=[B["DQ"]], writes=[bQ])
                P.dma("sp", KT[m][:, 0:T], D["DK"][r0:r0 + 128, t0:t0 + T], reads=[B["DK"]], writes=[bK])
            for k0 in range(0, nkl, 8):
                k1 = min(nkl, k0 + 8)
                P.dma("sp", Vt[:, k0:k1, :], D["DV"][t0 + k0 * 128:t0 + k1 * 128, h * 256:(h + 1) * 256].rearrange("(k p) d -> p k d", p=128),
                      reads=[B["DV"]], writes=[bV])
            chunks = [("l", k) for k in range(nkl)] + ([("c", k) for k in range(PC)] if lat else [])
            n = len(chunks)
            for qb in range(T // QB):
                acc = (2, 3, 4) if qit % 2 == 0 else (5, 6, 7)
                accb = [self.psb[a] for a in acc]
                pi = qit % 2
                pT = pTs[pi]
                for i, (kind, kc) in enumerate(chunks):
                    sbk = it % 2
                    for m in range(2):
                        k_ap = KT[m][:, kc * 128:(kc + 1) * 128] if kind == "l" else M["KcD"][:, h * 2 + m, kc * 128:(kc + 1) * 128]
                        P.add("pe", lambda e, m=m, k_ap=k_ap, sbk=sbk, qb=qb, QB=QB: e.matmul(
                            ps[:, sbk, m * QB:(m + 1) * QB], lhsT=k_ap, rhs=QT[m][:, qb * QB:(qb + 1) * QB], start=True, stop=True),
                            reads=[bK, bQ, M["b"]], writes=[self.psb[sbk]])
                    P.add("act", lambda e, pT=pT, sbk=sbk, QB=QB, i=i: e.activation(out=pT[:, i, 0:2 * QB], in_=ps[:, sbk, 0:2 * QB],
                                                                                   func=AF.Exp, scale=scale),
                          writes=[self.psb[sbk], bP[pi]])
                    it += 1
                for m in range(2):
                    for d_ in range(2):
                        for i, (kind, kc) in enumerate(chunks):
                            v_ap = Vt[:, kc, d_ * 128:(d_ + 1) * 128] if kind == "l" else M["VcD"][:, kc, h * 256 + d_ * 128:h * 256 + (d_ + 1) * 128]
                            P.add("pe", lambda e, m=m, d_=d_, v_ap=v_ap, pT=pT, i=i, acc=acc, QB=QB, n=n: e.matmul(
                                ps[:, acc[d_], m * QB:(m + 1) * QB], lhsT=v_ap, rhs=pT[:, i, m * QB:(m + 1) * QB],
                                start=(i == 0), stop=(i == n - 1)),
                                reads=[bV, bP[pi], M["b"]], writes=[accb[d_]])
                    for i in range(n):
                        P.add("pe", lambda e, m=m, pT=pT, i=i, acc=acc, QB=QB, n=n: e.matmul(
                            ps[:, acc[2], m * QB:(m + 1) * QB], lhsT=self.onesB, rhs=pT[:, i, m * QB:(m + 1) * QB],
                            start=(i == 0), stop=(i == n - 1)),
                            reads=[bP[pi], self.b_const], writes=[accb[2]])
                P.add("dve", lambda e, acc=acc, QB=QB: e.reciprocal(out=rden[:, 0:2 * QB], in_=ps[:, acc[2], 0:2 * QB]),
                      writes=[accb[2], bR])
                for d_ in range(2):
                    P.add("dve", lambda e, d_=d_, acc=acc, QB=QB: e.tensor_tensor(out=od[d_][:, 0:QB], in0=ps[:, acc[d_], 0:QB],
                                                                                 in1=rden[:, 0:QB], op=ALU.mult),
                          reads=[bR], writes=[accb[d_], bod])
                    P.add("dve", lambda e, d_=d_, acc=acc, QB=QB: e.tensor_tensor(out=t1[:, 0:QB], in0=ps[:, acc[d_], QB:2 * QB],
                                                                                 in1=rden[:, QB:2 * QB], op=ALU.mult),
                          reads=[bR], writes=[accb[d_], bt1])
                    P.add("dve", lambda e, d_=d_, QB=QB: e.scalar_tensor_tensor(out=od[d_][:, 0:QB], in0=t1[:, 0:QB], scalar=M["nlam"][:, 0:1],
                                                                               in1=od[d_][:, 0:QB], op0=ALU.mult, op1=ALU.add),
                          reads=[bt1, M["b"]], writes=[bod])
                    P.add("act", lambda e, d_=d_, QB=QB: e.activation(out=sq[d_][:, 0:QB], in_=od[d_][:, 0:QB], func=AF.Square),
                          reads=[bod], writes=[bsq[d_]])
                nb = it % 2
                for d_ in range(2):
                    P.add("pe", lambda e, d_=d_, nb=nb, QB=QB: e.matmul(ps[:, nb, 0:QB], lhsT=self.onesF, rhs=sq[d_][:, 0:QB],
                                                                       start=(d_ == 0), stop=(d_ == 1)),
                          reads=[bsq[d_], self.b_const], writes=[self.psb[nb]])
                it += 1
                P.add("act", lambda e, nb=nb, QB=QB: e.activation(out=rstd[:, 0:QB], in_=ps[:, nb, 0:QB], func=AF.Ln, bias=self.epsT,
                                                                 scale=1.0 / 256.0), reads=[self.b_const], writes=[self.psb[nb], brs])
                P.add("act", lambda e, QB=QB: e.activation(out=rstd[:, 0:QB], in_=rstd[:, 0:QB], func=AF.Exp, scale=-0.5), writes=[brs])
                for d_ in range(2):
                    P.add("dve", lambda e, d_=d_, qb=qb, QB=QB: e.scalar_tensor_tensor(
                        out=osts[d_][:, qb * QB:(qb + 1) * QB], in0=od[d_][:, 0:QB], scalar=M["dnw"][:, d_:d_ + 1], in1=rstd[:, 0:QB],
                        op0=ALU.mult, op1=ALU.mult), reads=[bod, brs, M["b"]], writes=[bO])
                qit += 1
            for d_ in range(2):
                r0 = h * 256 + d_ * 128
                P.dma("sp", D["OD"][r0:r0 + 128, t0:t0 + T], osts[d_][:, 0:T], reads=[bO], writes=[B["OD"]], pool="st")

    def gdn(self, l):
        self.gdn_prep(l)
        self.P.barrier()
        self.aoff = self.gdn_mark
        self.gdn_scan(l)

    def bank1(self):
        b = self.g_b1 % 8
        self.g_b1 += 1
        return b

    def bank2(self):
        b = (self.g_b2 % 4) * 2
        self.g_b2 += 1
        return b

    def gdn_prep(self, l):
        c, P, ps = self.cfg, self.P, self.ps
        D, B = self.dram, self.dbuf
        self.g_b1, self.g_b2 = 0, 0
        self.gdn_mark = self.aoff
        G = {}
        self.G = G
        G["b"] = Buf("gconst")
        o = self.alloc(72); cw = self.f32v(o, 72).rearrange("p (j k) -> p j k", j=3)
        for j in range(3):
            P.dma("sp", cw[:, j, :], D["gdn_conv"][l, j].rearrange("(k p) -> p k", p=128), writes=[G["b"]],
                  allow_slow_non_contiguous=True)
        o = self.alloc(16); dtb = self.f32v(o, 16)
        o = self.alloc(16); nea = self.f32v(o, 16)
        o = self.alloc(128); G["gnw"] = self.f32v(o, 128)
        P.dma("sp", dtb, D["gdn_dt_bias"][l:l + 1].rearrange("o d h -> o (d h)").partition_broadcast(128), writes=[G["b"]])
        P.dma("sp", nea, D["gdn_a_log"][l:l + 1].rearrange("o d h -> o (d h)").partition_broadcast(128), writes=[G["b"]])
        P.dma("sp", G["gnw"], D["gdn_norm"][l:l + 1, :].partition_broadcast(128), writes=[G["b"]])
        P.add("act", lambda e: e.activation(out=nea, in_=nea, func=AF.Exp), writes=[G["b"]])
        P.add("dve", lambda e: e.tensor_scalar(out=nea, in0=nea, scalar1=-1.0, scalar2=None, op0=ALU.mult), writes=[G["b"]])
        o = self.alloc(8); oneT = self.f32v(o, 1)
        P.add("dve", lambda e: e.memset(oneT, 1.0), writes=[G["b"]])
        self.gdn_mark = self.aoff
        NT = 512
        xh, bxh = [], []
        for i in range(2):
            o = self.alloc(NT + 8); xh.append(self.f32v(o, NT + 2)); bxh.append(Buf("xh%d" % i))
        o = self.alloc(NT); t = self.f32v(o, NT); bt = Buf("gt")
        ys, bys = [], []
        for i in range(2):
            o = self.alloc(NT); ys.append(self.f32v(o, NT)); bys.append(Buf("ys%d" % i))
        o = self.alloc(NT); sq = self.f32v(o, NT); bsq = Buf("gsq")
        o = self.alloc(NT); rinv = self.f32v(o, NT); bri = Buf("grinv")
        yn, byn = [], []
        for i in range(2):
            o = self.alloc(NT); yn.append(self.f32v(o, NT)); byn.append(Buf("yn%d" % i))
        tst, btst = [], []
        for i in range(2):
            o = self.alloc(128); tst.append(self.f32v(o, 128)); btst.append(Buf("tst%d" % i))
        ab, bab = [], []
        for i in range(2):
            o = self.alloc(32); ab.append(self.f32v(o, 32)); bab.append(Buf("ab%d" % i))
        o = self.alloc(16); xe = self.f32v(o, 16); bxe = Buf("xe")
        it = 0
        for (t0, T, lat) in self.seqs():
            for b0 in range(0, T, NT):
                nt = min(NT, T - b0)
                for fc in range(24):
                    kind = fc // 8
                    src = ("GQ", "GK", "GV")[kind]
                    r0 = (fc % 8) * 128
                    xi = it % 2
                    x_ = xh[xi]
                    lo = max(0, b0 - 1)
                    hi = min(T, b0 + nt + 1)
                    off = lo - (b0 - 1)
                    if b0 == 0:
                        P.add("pool", lambda e, x_=x_: e.memset(x_[:, 0:1], 0.0), writes=[bxh[xi]])
                    if b0 + nt == T:
                        P.add("pool", lambda e, x_=x_, nt=nt: e.memset(x_[:, nt + 1:nt + 2], 0.0), writes=[bxh[xi]])
                    P.dma("sp", x_[:, off:off + hi - lo], D[src][r0:r0 + 128, t0 + lo:t0 + hi], reads=[B[src]], writes=[bxh[xi]])
                    P.add("dve", lambda e, x_=x_, nt=nt, fc=fc: e.tensor_scalar(out=t[:, 0:nt], in0=x_[:, 0:nt], scalar1=cw[:, 0, fc:fc + 1],
                                                                                 scalar2=None, op0=ALU.mult),
                          reads=[bxh[xi], G["b"]], writes=[bt])
                    P.add("dve", lambda e, x_=x_, nt=nt, fc=fc: e.scalar_tensor_tensor(out=t[:, 0:nt], in0=x_[:, 1:nt + 1], scalar=cw[:, 1, fc:fc + 1],
                                                                                        in1=t[:, 0:nt], op0=ALU.mult, op1=ALU.add),
                          reads=[bxh[xi], G["b"]], writes=[bt])
                    P.add("dve", lambda e, x_=x_, nt=nt, fc=fc: e.scalar_tensor_tensor(out=t[:, 0:nt], in0=x_[:, 2:nt + 2], scalar=cw[:, 2, fc:fc + 1],
                                                                                        in1=t[:, 0:nt], op0=ALU.mult, op1=ALU.add),
                          reads=[bxh[xi], G["b"]], writes=[bt])
                    y_ = ys[xi]
                    P.add("act", lambda e, y_=y_, nt=nt: e.activation(out=y_[:, 0:nt], in_=t[:, 0:nt], func=AF.Silu),
                          reads=[bt], writes=[bys[xi]])
                    fin, bfin = y_, bys[xi]
                    if kind < 2:
                        P.add("act", lambda e, y_=y_, nt=nt: e.activation(out=sq[:, 0:nt], in_=y_[:, 0:nt], func=AF.Square),
                              reads=[bys[xi]], writes=[bsq])
                        bk = self.bank1()
                        P.add("pe", lambda e, bk=bk, nt=nt: e.matmul(ps[:, bk, 0:nt], lhsT=self.onesF, rhs=sq[:, 0:nt], start=True, stop=True),
                              reads=[bsq, self.b_const], writes=[self.psb[bk]])
                        P.add("act", lambda e, bk=bk, nt=nt: e.activation(out=rinv[:, 0:nt], in_=ps[:, bk, 0:nt], func=AF.Ln, bias=self.epsT, scale=1.0),
                              reads=[self.b_const], writes=[self.psb[bk], bri])
                        P.add("act", lambda e, nt=nt: e.activation(out=rinv[:, 0:nt], in_=rinv[:, 0:nt], func=AF.Exp, scale=-0.5), writes=[bri])
                        n_ = yn[xi]
                        sc = HEAD_DIM ** -0.5 if kind == 0 else 1.0
                        P.add("dve", lambda e, n_=n_, y_=y_, nt=nt, sc=sc: e.scalar_tensor_tensor(out=n_[:, 0:nt], in0=y_[:, 0:nt], scalar=sc, in1=rinv[:, 0:nt],
                                                                                                op0=ALU.mult, op1=ALU.mult),
                              reads=[bys[xi], bri], writes=[byn[xi]])
                        fin, bfin = n_, byn[xi]
                        dst = "GQN" if kind == 0 else "GKN"
                        P.dma("sp", D[dst][r0:r0 + 128, t0 + b0:t0 + b0 + nt], fin[:, 0:nt], reads=[bfin], writes=[B[dst]], pool="st")
                    if kind >= 1:
                        dstT = "GKT" if kind == 1 else "GVT"
                        for tt in range(nt // 128 if nt >= 128 else 1):
                            w_ = min(128, nt)
                            bk = self.bank1()
                            P.add("pe", lambda e, bk=bk, fin=fin, tt=tt, w_=w_: e.transpose(out=ps[0:w_, bk, 0:128], in_=fin[:, tt * 128:tt * 128 + w_],
                                                                                             identity=self.identF),
                                  reads=[bfin, self.b_const], writes=[self.psb[bk]])
                            si = (it + tt) % 2
                            P.add("act", lambda e, bk=bk, si=si, w_=w_: e.copy(out=tst[si][0:w_, :], in_=ps[0:w_, bk, 0:128]),
                                  writes=[self.psb[bk], btst[si]])
                            tok = t0 + b0 + tt * 128
                            P.dma("sp", D[dstT][tok:tok + w_, r0:r0 + 128], tst[si][0:w_, :], reads=[btst[si]], writes=[B[dstT]], pool="st")
                    it += 1
                for tt in range(max(1, nt // 128)):
                    w_ = min(128, nt)
                    tok = t0 + b0 + tt * 128
                    ai = (it + tt) % 2
                    a_ = ab[ai]
                    P.dma("sp", a_[0:w_, :], D["GAB"][tok:tok + w_, :], reads=[B["GAB"]], writes=[bab[ai]])
                    P.add("act", lambda e, a_=a_, w_=w_: e.activation(out=a_[0:w_, 0:16], in_=a_[0:w_, 0:16], func=AF.Sigmoid), writes=[bab[ai]])
                    P.add("dve", lambda e, a_=a_, w_=w_: e.tensor_tensor(out=xe[0:w_, :], in0=a_[0:w_, 16:32], in1=dtb[0:w_, :], op=ALU.add),
                          reads=[bab[ai], G["b"]], writes=[bxe])
                    P.add("act", lambda e, w_=w_: e.activation(out=xe[0:w_, :], in_=xe[0:w_, :], func=AF.Exp), writes=[bxe])
                    P.add("act", lambda e, w_=w_: e.activation(out=xe[0:w_, :], in_=xe[0:w_, :], func=AF.Ln, bias=oneT[0:w_, :], scale=1.0),
                          reads=[G["b"]], writes=[bxe])
                    P.add("dve", lambda e, a_=a_, w_=w_: e.tensor_tensor(out=a_[0:w_, 16:32], in0=xe[0:w_, :], in1=nea[0:w_, :], op=ALU.mult),
                          reads=[bxe, G["b"]], writes=[bab[ai]])
                    P.dma("sp", D["GAB2"][tok:tok + w_, :], a_[0:w_, :], reads=[bab[ai]], writes=[B["GAB2"]], pool="st")
                it += 1

    def gdn_scan(self, l):
        c, P, ps = self.cfg, self.P, self.ps
        D, B, G = self.dram, self.dbuf, self.G
        C = GDN_CHUNK
        H = GDN_HEADS

        def f3(n, inner):
            o = self.alloc(n)
            return self.f32v(o, n).rearrange("p (h x) -> p h x", x=inner)

        def f2(n):
            return self.f32v(self.alloc(n), n)

        o = self.alloc(6 * 64); msk = self.f32v(o, 384).rearrange("p (m j) -> p m j", m=6)
        o = self.alloc(2 * 64); tri = self.f32v(o, 128).rearrange("p (m j) -> p m j", m=2)
        P.dma("sp", msk[0:64], D["k_masks"].rearrange("m i j -> i m j"), writes=[G["b"]])
        P.dma("sp", tri[0:64], D["k_tri"].rearrange("m i j -> i m j"), writes=[G["b"]])
        id64 = self.identF[0:64, 0:64]

        def bc_h(ap2):
            return ap2.unsqueeze(1).to_broadcast([ap2.shape[0], H, ap2.shape[1]])

        def bc_x(ap2, n):
            return ap2.unsqueeze(2).to_broadcast([ap2.shape[0], H, n])

        LD = []
        for i in range(2):
            d = {"Kf": f3(512, 64), "Qf": f3(512, 64), "Kt": f3(1024, 128), "Vt": f3(1024, 128), "ab": f2(32), "b": Buf("gld%d" % i)}
            LD.append(d)
        T_ = {k: f3(512, 64) for k in ("R", "Rb", "D", "E", "ET", "EM", "ETM", "ETI", "X", "XT", "Y0", "Y1", "YT0", "YT1", "nBb")}
        T_["gam"] = f2(8); T_["tot"] = f2(8); T_["kdsc"] = f2(8); T_["bes"] = f2(8); T_["nbeta"] = f2(8)
        T_["bv"] = f3(1024, 128); T_["Rk"] = f3(1024, 128)
        Tb = {k: Buf("g_" + k) for k in T_}
        OUT = []
        for i in range(2):
            d = {"TT": f3(512, 64), "pT": f3(512, 64), "u": f3(1024, 128), "wkT": f3(512, 64), "kd": f3(1024, 128),
                 "egam": f2(8), "gl": f2(8)}
            d["b"] = {k: Buf("go%d_%s" % (i, k)) for k in d}
            OUT.append(d)
        S_ = f3(1024, 128); bS = Buf("gS")
        w_ = f3(1024, 128); bw = Buf("gw")
        zs = f3(1024, 128); bzs = Buf("gzs")
        ot = f3(1024, 128); bot = Buf("got")
        of_ = f3(1024, 128); bof = Buf("gof")
        gz = f3(1024, 128); bgz = Buf("ggz")
        tmp = f3(1024, 128); btmp = Buf("gtmp")
        ssq = f2(8); bssq = Buf("gssq")
        o = self.alloc(256); ogT = self.bf16v(o, 512).rearrange("p (h t) -> p h t", t=64); bogT = Buf("gogT")

        def psv(bank, parts, n, inner):
            return ps[0:parts, bank, 0:n].rearrange("p (h x) -> p h x", x=inner)

        def ps2v(bank, parts, inner):
            return ps[0:parts, bank:bank + 2, :].rearrange("p b f -> p (b f)").rearrange("p (h x) -> p h x", x=inner)

        def prep(t0c, dr, li, oi):
            L_, O_ = LD[li], OUT[oi]
            ob = O_["b"]
            P.dma("sp", L_["Kf"], D["GKN"][:, t0c:t0c + C].rearrange("(h p) t -> p h t", p=128), reads=[B["GKN"]], writes=[L_["b"]])
            P.dma("sp", L_["Qf"], D["GQN"][:, t0c:t0c + C].rearrange("(h p) t -> p h t", p=128), reads=[B["GQN"]], writes=[L_["b"]])
            P.dma("sp", L_["Kt"][0:C], D["GKT"][t0c:t0c + C, :].rearrange("t (h d) -> t h d", d=128), reads=[B["GKT"]], writes=[L_["b"]])
            P.dma("sp", L_["Vt"][0:C], D["GVT"][t0c:t0c + C, :].rearrange("t (h d) -> t h d", d=128), reads=[B["GVT"]], writes=[L_["b"]])
            P.dma("sp", L_["ab"][0:C], D["GAB2"][t0c:t0c + C, :], reads=[B["GAB2"]], writes=[L_["b"]])
            beta = L_["ab"][0:C, dr * 8:dr * 8 + 8]
            la = L_["ab"][0:C, 16 + dr * 8:16 + dr * 8 + 8]
            gam, tot, kdsc, bes, nbeta = (T_[k][0:C] for k in ("gam", "tot", "kdsc", "bes", "nbeta"))
            b0 = self.bank1()
            P.add("pe", lambda e: e.matmul(ps[0:C, b0, 0:8], lhsT=tri[0:C, dr, :], rhs=la, start=True, stop=True),
                  reads=[L_["b"], G["b"]], writes=[self.psb[b0]])
            P.add("pe", lambda e: e.matmul(ps[:, b0, 8:16], lhsT=self.onesF[0:C, :], rhs=la, start=True, stop=True),
                  reads=[L_["b"], self.b_const], writes=[self.psb[b0]])
            P.add("act", lambda e: e.copy(out=gam, in_=ps[0:C, b0, 0:8]), writes=[self.psb[b0], Tb["gam"]])
            P.add("act", lambda e: e.activation(out=O_["egam"][0:C], in_=ps[0:C, b0, 0:8], func=AF.Exp), writes=[self.psb[b0], ob["egam"]])
            P.add("act", lambda e: e.activation(out=O_["gl"], in_=ps[:, b0, 8:16], func=AF.Exp), writes=[self.psb[b0], ob["gl"]])
            P.add("act", lambda e: e.copy(out=tot, in_=ps[0:C, b0, 8:16]), writes=[self.psb[b0], Tb["tot"]])
            P.add("dve", lambda e: e.tensor_tensor(out=kdsc, in0=tot, in1=gam, op=ALU.subtract), reads=[Tb["tot"], Tb["gam"]], writes=[Tb["kdsc"]])
            P.add("act", lambda e: e.activation(out=kdsc, in_=kdsc, func=AF.Exp), writes=[Tb["kdsc"]])
            P.add("dve", lambda e: e.tensor_tensor(out=bes, in0=beta, in1=O_["egam"][0:C], op=ALU.mult), reads=[L_["b"], ob["egam"]], writes=[Tb["bes"]])
            P.add("dve", lambda e: e.tensor_scalar(out=nbeta, in0=beta, scalar1=-1.0, scalar2=None, op0=ALU.mult), reads=[L_["b"]], writes=[Tb["nbeta"]])
            R, Rb = T_["R"][0:C], T_["Rb"][0:C]
            P.add("dve", lambda e: e.tensor_tensor(out=R, in0=bc_h(id64), in1=bc_x(gam, C), op=ALU.mult), reads=[Tb["gam"], self.b_const], writes=[Tb["R"]])
            P.add("pool", lambda e: e.tensor_tensor(out=Rb, in0=bc_h(id64), in1=bc_x(nbeta, C), op=ALU.mult), reads=[Tb["nbeta"], self.b_const], writes=[Tb["Rb"]])
            bG, bBb = self.bank1(), self.bank1()
            P.add("pe", lambda e: e.matmul(ps[0:C, bG, :], lhsT=self.onesF[0:C, 0:C], rhs=R.rearrange("p h x -> p (h x)"), start=True, stop=True),
                  reads=[Tb["R"], self.b_const], writes=[self.psb[bG]])
            P.add("pe", lambda e: e.matmul(ps[0:C, bBb, :], lhsT=self.onesF[0:C, 0:C], rhs=Rb.rearrange("p h x -> p (h x)"), start=True, stop=True),
                  reads=[Tb["Rb"], self.b_const], writes=[self.psb[bBb]])
            Dm, E, ET, EM, ETM, ETI, nBb = (T_[k][0:C] for k in ("D", "E", "ET", "EM", "ETM", "ETI", "nBb"))
            Gb = psv(bG, C, 512, 64)
            P.add("dve", lambda e: e.tensor_tensor(out=Dm, in0=bc_x(gam, C), in1=Gb, op=ALU.subtract), reads=[Tb["gam"]], writes=[self.psb[bG], Tb["D"]])
            P.add("dve", lambda e: e.tensor_scalar(out=Dm, in0=Dm, scalar1=0.0, scalar2=None, op0=ALU.min), writes=[Tb["D"]])
            P.add("act", lambda e: e.activation(out=E, in_=Dm, func=AF.Exp), reads=[Tb["D"]], writes=[Tb["E"]])
            P.add("dve", lambda e: e.tensor_tensor(out=ET, in0=Gb, in1=bc_x(gam, C), op=ALU.subtract), reads=[Tb["gam"]], writes=[self.psb[bG], Tb["ET"]])
            P.add("dve", lambda e: e.tensor_scalar(out=ET, in0=ET, scalar1=0.0, scalar2=None, op0=ALU.min), writes=[Tb["ET"]])
            P.add("act", lambda e: e.activation(out=ET, in_=ET, func=AF.Exp), writes=[Tb["ET"]])
            P.add("act", lambda e: e.copy(out=nBb, in_=psv(bBb, C, 512, 64)), writes=[self.psb[bBb], Tb["nBb"]])
            m0 = dr * 3
            P.add("pool", lambda e: e.tensor_tensor(out=EM, in0=E, in1=bc_h(msk[0:C, m0, :]), op=ALU.mult), reads=[Tb["E"], G["b"]], writes=[Tb["EM"]])
            P.add("pool", lambda e: e.tensor_tensor(out=ETM, in0=ET, in1=bc_h(msk[0:C, m0 + 1, :]), op=ALU.mult), reads=[Tb["ET"], G["b"]], writes=[Tb["ETM"]])
            P.add("pool", lambda e: e.tensor_tensor(out=ETI, in0=ET, in1=bc_h(msk[0:C, m0 + 2, :]), op=ALU.mult), reads=[Tb["ET"], G["b"]], writes=[Tb["ETI"]])
            bkk, bqk = self.bank1(), self.bank1()
            for h in range(H):
                P.add("pe", lambda e, h=h: e.matmul(ps[0:C, bkk, h * 64:(h + 1) * 64], lhsT=L_["Kf"][:, h, :], rhs=L_["Kf"][:, h, :], start=True, stop=True),
                      reads=[L_["b"]], writes=[self.psb[bkk]])
            for h in range(H):
                P.add("pe", lambda e, h=h: e.matmul(ps[0:C, bqk, h * 64:(h + 1) * 64], lhsT=L_["Kf"][:, h, :], rhs=L_["Qf"][:, h, :], start=True, stop=True),
                      reads=[L_["b"]], writes=[self.psb[bqk]])
            X, XT = T_["X"][0:C], T_["XT"][0:C]
            kkv, qkv = psv(bkk, C, 512, 64), psv(bqk, C, 512, 64)
            P.add("dve", lambda e: e.tensor_tensor(out=X, in0=EM, in1=kkv, op=ALU.mult), reads=[Tb["EM"]], writes=[self.psb[bkk], Tb["X"]])
            P.add("dve", lambda e: e.tensor_tensor(out=X, in0=X, in1=bc_x(nbeta, C), op=ALU.mult), reads=[Tb["nbeta"]], writes=[Tb["X"]])
            P.add("dve", lambda e: e.tensor_tensor(out=XT, in0=ETM, in1=kkv, op=ALU.mult), reads=[Tb["ETM"]], writes=[self.psb[bkk], Tb["XT"]])
            P.add("dve", lambda e: e.tensor_tensor(out=XT, in0=XT, in1=nBb, op=ALU.mult), reads=[Tb["nBb"]], writes=[Tb["XT"]])
            pT = O_["pT"][0:C]
            P.add("dve", lambda e: e.tensor_tensor(out=pT, in0=ETI, in1=qkv, op=ALU.mult), reads=[Tb["ETI"]], writes=[self.psb[bqk], ob["pT"]])
            TT = O_["TT"][0:C]
            P.add("pool", lambda e: e.tensor_tensor(out=TT, in0=XT, in1=bc_h(id64), op=ALU.add), reads=[Tb["XT"], self.b_const], writes=[ob["TT"]])
            Y, YT, bY, bYT = X, XT, Tb["X"], Tb["XT"]
            for k in range(1, 6):
                Yn, bYn = T_["Y%d" % (k % 2)][0:C], Tb["Y%d" % (k % 2)]
                YTn, bYTn = T_["YT%d" % (k % 2)][0:C], Tb["YT%d" % (k % 2)]
                ba = self.bank1()
                for h in range(H):
                    P.add("pe", lambda e, h=h, ba=ba, Y=Y, YT=YT: e.matmul(ps[0:C, ba, h * 64:(h + 1) * 64], lhsT=YT[:, h, :], rhs=Y[:, h, :], start=True, stop=True),
                          reads=[bY, bYT], writes=[self.psb[ba]])
                if k < 5:
                    bb_ = self.bank1()
                    for h in range(H):
                        P.add("pe", lambda e, h=h, bb_=bb_, Y=Y, YT=YT: e.matmul(ps[0:C, bb_, h * 64:(h + 1) * 64], lhsT=Y[:, h, :], rhs=YT[:, h, :], start=True, stop=True),
                              reads=[bY, bYT], writes=[self.psb[bb_]])
                P.add("act", lambda e, Yn=Yn, ba=ba: e.copy(out=Yn, in_=psv(ba, C, 512, 64)), writes=[self.psb[ba], bYn])
                if k < 5:
                    P.add("act", lambda e, YTn=YTn, bb_=bb_: e.copy(out=YTn, in_=psv(bb_, C, 512, 64)), writes=[self.psb[bb_], bYTn])
                bc_ = self.bank1()
                for h in range(H):
                    P.add("pe", lambda e, h=h, bc_=bc_, Yn=Yn: e.matmul(ps[0:C, bc_, h * 64:(h + 1) * 64], lhsT=Yn[:, h, :], rhs=TT[:, h, :], start=True, stop=True),
                          reads=[bYn, ob["TT"]], writes=[self.psb[bc_]])
                P.add("dve", lambda e, bc_=bc_: e.tensor_tensor(out=TT, in0=TT, in1=psv(bc_, C, 512, 64), op=ALU.add), writes=[self.psb[bc_], ob["TT"]])
                Y, YT, bY, bYT = Yn, YTn, bYn, bYTn
            bv, Rk, kd = T_["bv"][0:C], T_["Rk"][0:C], O_["kd"][0:C]
            P.add("pool", lambda e: e.tensor_tensor(out=bv, in0=L_["Vt"][0:C], in1=bc_x(beta, 128), op=ALU.mult), reads=[L_["b"]], writes=[Tb["bv"]])
            P.add("pool", lambda e: e.tensor_tensor(out=Rk, in0=L_["Kt"][0:C], in1=bc_x(bes, 128), op=ALU.mult), reads=[L_["b"], Tb["bes"]], writes=[Tb["Rk"]])
            P.add("pool", lambda e: e.tensor_tensor(out=kd, in0=L_["Kt"][0:C], in1=bc_x(kdsc, 128), op=ALU.mult), reads=[L_["b"], Tb["kdsc"]], writes=[ob["kd"]])
            bu = self.bank2()
            uv = ps2v(bu, C, 128)
            for h in range(H):
                P.add("pe", lambda e, h=h: e.matmul(uv[:, h, :], lhsT=TT[:, h, :], rhs=bv[:, h, :], start=True, stop=True),
                      reads=[ob["TT"], Tb["bv"]], writes=[self.psb[bu], self.psb[bu + 1]])
            P.add("act", lambda e: e.copy(out=O_["u"][0:C], in_=uv), writes=[self.psb[bu], self.psb[bu + 1], ob["u"]])
            bwk = self.bank1()
            for h in range(H):
                P.add("pe", lambda e, h=h: e.matmul(ps[:, bwk, h * 64:(h + 1) * 64], lhsT=Rk[:, h, :], rhs=TT[:, h, :], start=True, stop=True),
                      reads=[ob["TT"], Tb["Rk"]], writes=[self.psb[bwk]])
            P.add("act", lambda e: e.copy(out=O_["wkT"], in_=psv(bwk, 128, 512, 64)), writes=[self.psb[bwk], ob["wkT"]])

        def scan(t0c, dr, li, oi, final):
            L_, O_ = LD[li], OUT[oi]
            ob = O_["b"]
            b1 = self.bank2()
            wkS = ps2v(b1, C, 128)
            for h in range(H):
                P.add("pe", lambda e, h=h: e.matmul(wkS[:, h, :], lhsT=O_["wkT"][:, h, :], rhs=S_[:, h, :], start=True, stop=True),
                      reads=[ob["wkT"], bS], writes=[self.psb[b1], self.psb[b1 + 1]])
            P.add("dve", lambda e: e.tensor_tensor(out=w_[0:C], in0=O_["u"][0:C], in1=wkS, op=ALU.subtract),
                  reads=[ob["u"]], writes=[self.psb[b1], self.psb[b1 + 1], bw])
            b2 = self.bank2()
            zv = ps2v(b2, C, 128)
            for h in range(H):
                P.add("pe", lambda e, h=h: e.matmul(zv[:, h, :], lhsT=L_["Qf"][:, h, :], rhs=S_[:, h, :], start=True, stop=True),
                      reads=[L_["b"], bS], writes=[self.psb[b2], self.psb[b2 + 1]])
            P.add("dve", lambda e: e.tensor_tensor(out=zs[0:C], in0=zv, in1=bc_x(O_["egam"][0:C], 128), op=ALU.mult),
                  reads=[ob["egam"]], writes=[self.psb[b2], self.psb[b2 + 1], bzs])
            b3 = self.bank2()
            pwv = ps2v(b3, C, 128)
            for h in range(H):
                P.add("pe", lambda e, h=h: e.matmul(pwv[:, h, :], lhsT=O_["pT"][0:C, h, :], rhs=w_[0:C, h, :], start=True, stop=True),
                      reads=[ob["pT"], bw], writes=[self.psb[b3], self.psb[b3 + 1]])
            P.add("dve", lambda e: e.tensor_tensor(out=ot[0:C], in0=zs[0:C], in1=pwv, op=ALU.add),
                  reads=[bzs], writes=[self.psb[b3], self.psb[b3 + 1], bot])
            b4 = self.bank2()
            sup = ps2v(b4, 128, 128)
            for h in range(H):
                P.add("pe", lambda e, h=h: e.matmul(sup[:, h, :], lhsT=O_["kd"][0:C, h, :], rhs=w_[0:C, h, :], start=True, stop=True),
                      reads=[ob["kd"], bw], writes=[self.psb[b4], self.psb[b4 + 1]])
            P.add("pool", lambda e: e.tensor_tensor(out=S_, in0=S_, in1=bc_x(O_["gl"], 128), op=ALU.mult), reads=[ob["gl"]], writes=[bS])
            P.add("dve", lambda e: e.tensor_tensor(out=S_, in0=S_, in1=sup, op=ALU.add), writes=[self.psb[b4], self.psb[b4 + 1], bS])
            ofl = "(h d)"
            if dr == 0:
                P.dma("sp", D["GOF"][t0c:t0c + C, :].rearrange("t (h d) -> t h d", d=128), ot[0:C], reads=[bot], writes=[B["GOF"]], pool="st")
            else:
                P.dma("sp", of_[0:C], D["GOF"][t0c:t0c + C, :].rearrange("t (h d) -> t h d", d=128), reads=[B["GOF"]], writes=[bof])
                P.dma("sp", gz[0:C], D["GZT"][t0c:t0c + C, :].rearrange("t (h d) -> t h d", d=128), reads=[B["GZT"]], writes=[bgz])
                P.add("dve", lambda e: e.tensor_tensor(out=ot[0:C], in0=ot[0:C], in1=of_[0:C], op=ALU.add), reads=[bof], writes=[bot])
                P.add("pool", lambda e: e.tensor_tensor(out=tmp[0:C], in0=ot[0:C], in1=ot[0:C], op=ALU.mult), reads=[bot], writes=[btmp])
                P.add("dve", lambda e: e.tensor_reduce(out=ssq[0:C], in_=tmp[0:C], axis=AX.X, op=ALU.add), reads=[btmp], writes=[bssq])
                P.add("act", lambda e: e.activation(out=ssq[0:C], in_=ssq[0:C], func=AF.Ln, bias=self.epsT[0:C], scale=1.0 / 128.0),
                      reads=[self.b_const], writes=[bssq])
                P.add("act", lambda e: e.activation(out=ssq[0:C], in_=ssq[0:C], func=AF.Exp, scale=-0.5), writes=[bssq])
                P.add("dve", lambda e: e.tensor_tensor(out=ot[0:C], in0=ot[0:C], in1=bc_x(ssq[0:C], 128), op=ALU.mult), reads=[bssq], writes=[bot])
                P.add("pool", lambda e: e.tensor_tensor(out=gz[0:C], in0=gz[0:C], in1=G["gnw"][0:C].unsqueeze(1).to_broadcast([C, H, 128]), op=ALU.mult),
                      reads=[G["b"]], writes=[bgz])
                P.add("dve", lambda e: e.tensor_tensor(out=ot[0:C], in0=ot[0:C], in1=gz[0:C], op=ALU.mult), reads=[bgz], writes=[bot])
                b5 = self.bank1()
                for h in range(H):
                    P.add("pe", lambda e, h=h: e.transpose(out=ps[:, b5, h * 64:(h + 1) * 64], in_=ot[0:C, h, :], identity=id64),
                          reads=[bot, self.b_const], writes=[self.psb[b5]])
                P.add("act", lambda e: e.copy(out=ogT, in_=psv(b5, 128, 512, 64)), writes=[self.psb[b5], bogT])
                P.dma("sp", D["OG"][:, t0c:t0c + C].rearrange("(h p) t -> p h t", p=128), ogT, reads=[bogT], writes=[B["OG"]], pool="st")

        for si, (t0, T, lat) in enumerate(self.seqs()):
            ncks = T // C
            for dr in range(2):
                if lat:
                    P.dma("sp", S_, D["sgdn"][l, dr].rearrange("h k v -> k h v"), writes=[bS])
                else:
                    P.add("dve", lambda e: e.memset(S_, 0.0), writes=[bS])
                order = list(range(ncks)) if dr == 0 else list(range(ncks - 1, -1, -1))
                prep(t0 + order[0] * C, dr, 0, 0)
                for n_, ci in enumerate(order):
                    if n_ + 1 < ncks:
                        prep(t0 + order[n_ + 1] * C, dr, (n_ + 1) % 2, (n_ + 1) % 2)
                    scan(t0 + ci * C, dr, n_ % 2, n_ % 2, n_ == ncks - 1)
                if not lat:
                    b_ = si - 1
                    P.dma("sp", D["o_gdn"][b_, l, dr].rearrange("h k v -> k h v"), S_, reads=[bS], writes=[B["o_gdn"]], pool="st")


def host_constants(cfg, na_rpb, plan_tiles):
    ident = np.eye(128, dtype=np.float32)
    perm = np.zeros((128, 128), np.float32)
    for m in range(128):
        perm[m ^ 32, m] = 1.0
    t = np.arange(cfg.TL)
    pos = np.stack([t // GRID_W, t % GRID_W], -1).astype(np.float32)
    nq = HEAD_DIM // 4
    inv = (ROPE_BASE ** (-np.arange(nq, dtype=np.float32) / nq)).astype(np.float32)
    cos = np.zeros((128, cfg.TL), np.float32)
    sin = np.zeros((128, cfg.TL), np.float32)
    for p in range(128):
        a, half, f = p // 64, (p % 64) // 32, p % 32
        ang = pos[:, a] * inv[f]
        cos[p] = np.cos(ang)
        sin[p] = np.sin(ang) * (-1.0 if half == 0 else 1.0)
    i = np.arange(64)[:, None]
    j = np.arange(64)[None, :]
    masks = np.stack([(i > j), (i > j).T, (i >= j).T, (i < j), (i < j).T, (i <= j).T]).astype(np.float32)
    tri = np.stack([(i <= j), (i >= j)]).astype(np.float32)
    L = na_rpb.shape[0]
    nty = len(plan_tiles)
    nab = np.empty((L, NA_HEADS, nty, 128, 128), np.float32)
    for ti, (dr, dc, valid) in enumerate(plan_tiles):
        gathered = na_rpb[:, :, dr, dc]
        nab[:, :, ti] = np.where(valid[None, None], gathered, np.float32(-30000.0))
    return {"k_ident": ident, "k_perm": perm, "k_cos": cos, "k_sin": sin, "k_masks": masks, "k_tri": tri, "nab": nab}


def make_in_maps(cfg, inputs, n_cores, plan_tiles):
    consts = host_constants(cfg, np.asarray(inputs["na_rpb"], np.float32), plan_tiles)
    shared = {}
    for k in ("w_mod", "b_mod", "norm_pre", "norm_post", "ffn1_w_gu", "ffn1_w_dn", "ffn2_w_gu", "ffn2_w_dn", "w_in",
              "gdn_conv", "gdn_a_log", "gdn_dt_bias", "gdn_norm", "diff_lambda", "diff_norm", "w_branch_na",
              "w_branch_gdn", "w_branch_diff", "w_out"):
        shared[k] = np.ascontiguousarray(inputs[k], dtype=np.float32)
    shared.update(consts)
    maps = []
    L = cfg.L
    for i in range(n_cores):
        m = dict(shared)
        m["xs"] = np.ascontiguousarray(inputs["x_sample"][i])
        m["xp"] = np.ascontiguousarray(inputs["x_prompt"][i * cfg.NB:(i + 1) * cfg.NB]).reshape(cfg.NB * cfg.S, cfg.D)
        m["cna_k"] = np.ascontiguousarray(inputs["cache_na_k"][i]).reshape(L, cfg.PAST, cfg.NAW)
        m["cna_v"] = np.ascontiguousarray(inputs["cache_na_v"][i]).reshape(L, cfg.PAST, cfg.NAW)
        m["sgdn"] = np.ascontiguousarray(inputs["state_gdn"][i])
        m["cd_k"] = np.ascontiguousarray(inputs["cache_diff_k"][i]).reshape(L, cfg.PAST, cfg.DW)
        m["cd_v"] = np.ascontiguousarray(inputs["cache_diff_v"][i]).reshape(L, cfg.PAST, cfg.DW)
        m["cvec"] = np.ascontiguousarray(np.stack([inputs["c"][i], inputs["c_ctx"]]))
        maps.append(m)
    return maps


def assemble(cfg, results, n_cores):
    L = cfg.L
    y_s = np.stack([r["y_s"] for r in results])
    y_p = np.concatenate([r["y_p"].reshape(cfg.NB, cfg.S, cfg.D) for r in results])
    nk = np.concatenate([r["o_na_k"].reshape(cfg.NB, L, cfg.S, NA_HEADS, 128) for r in results])
    nv = np.concatenate([r["o_na_v"].reshape(cfg.NB, L, cfg.S, NA_HEADS, 128) for r in results])
    gs = np.concatenate([r["o_gdn"] for r in results])
    dk = np.concatenate([r["o_d_k"].reshape(cfg.NB, L, cfg.S, DIFF_HEADS, 2, 128) for r in results])
    dv = np.concatenate([r["o_d_v"].reshape(cfg.NB, L, cfg.S, DIFF_HEADS, 256) for r in results])
    return tuple(np.asarray(a, np.float32) for a in (y_s, y_p, nk, nv, gs, dk, dv))


def kernel(**inputs):
    inputs = {k: np.asarray(v) for k, v in inputs.items()}
    n = 8
    cfg = Cfg()
    b = Builder(cfg)
    nc = b.build()
    maps = make_in_maps(cfg, inputs, n, b.na_tiles)
    res = run_bass_kernel_spmd(nc, maps, core_ids=list(range(n)))
    y_s, y_p, nk, nv, gs, dk, dv = assemble(cfg, res.results, n)
    return (y_p, y_s, nk, nv, gs, dk, dv)
```

```python
import math
from contextlib import ExitStack
import numpy as np
import concourse.bass as bass
import concourse.mybir as mybir
from concourse.bass_utils import run_bass_kernel_spmd

F32 = mybir.dt.float32
BF16 = mybir.dt.bfloat16
AF = mybir.ActivationFunctionType
ALU = mybir.AluOpType
AX = mybir.AxisListType

HEAD_DIM = 128
NA_HEADS = 8
GDN_HEADS = 8
DIFF_HEADS = 4
GRID_W = 64
NA_WIN_R = 8
NA_WIN_C = 16
GDN_CHUNK = 64
EPS = 1e-6
N_MOD = 9
ROPE_BASE = 10000.0


class Buf:
    __slots__ = ("name", "last_w", "readers")

    def __init__(self, name=""):
        self.name = name
        self.last_w = None
        self.readers = []


class Op:
    __slots__ = ("eng", "fn", "deps", "sig", "vc", "waits", "dma", "pool", "needed")

    def __init__(self, eng, fn, dma, pool):
        self.eng = eng
        self.fn = fn
        self.deps = set()
        self.sig = None
        self.vc = None
        self.waits = None
        self.dma = dma
        self.pool = pool
        self.needed = False


class Prog:
    ENGS = ("pe", "act", "dve", "pool", "sp")

    def __init__(self, nc, dma_pools):
        self.nc = nc
        self.ops = []
        self.dma_pools = dma_pools
        self.nosync_same = {"pe"}
        self.last_on = {}
        self.pending_dma = []

    def add(self, eng, fn, reads=(), writes=(), dma=False, pool=None, extra=()):
        op = Op(eng, fn, dma, pool)
        op.deps.update(extra)
        for b in reads:
            if b.last_w is not None:
                op.deps.add(b.last_w)
            b.readers.append(op)
        for b in writes:
            if b.last_w is not None:
                op.deps.add(b.last_w)
            for r in b.readers:
                if r is not op:
                    op.deps.add(r)
            b.last_w = op
            b.readers = []
        self.ops.append(op)
        self.last_on[(eng, pool if dma else None)] = op
        if dma:
            self.pending_dma.append(op)
        return op

    def dma(self, eng, out, in_, reads=(), writes=(), pool="ld", extra=(), **kw):
        return self.add(eng, lambda e: e.dma_start(out=out, in_=in_, **kw), reads, writes, dma=True, pool=pool, extra=extra)

    def barrier(self):
        lasts = list(self.last_on.values())
        pend = self.pending_dma
        self.pending_dma = []
        for e in self.ENGS:
            op = self.add(e, lambda e_: None)
            for l in lasts:
                if l is not op:
                    op.deps.add(l)
            for d in pend:
                op.deps.add(d)

    def finalize(self, stack):
        nc = self.nc
        for op in self.ops:
            for d in op.deps:
                d.needed = True
        self.eng_sem = {e: stack.enter_context(nc.semaphore("s_" + e)) for e in self.ENGS}
        self.pool_sems = {}
        for pname, n in self.dma_pools.items():
            self.pool_sems[pname] = [stack.enter_context(nc.semaphore("d_%s%d" % (pname, i))) for i in range(n)]
        pool_rr = {p: 0 for p in self.dma_pools}
        pool_state = {p: [[0, None] for _ in range(n)] for p, n in self.dma_pools.items()}
        cnt = {e: 0 for e in self.ENGS}
        clock = {e: {} for e in self.ENGS}
        nwaits = 0
        for op in self.ops:
            E = op.eng
            ck = clock[E]
            if op.dma:
                p = op.pool
                k = pool_rr[p]
                pool_rr[p] = (k + 1) % len(pool_state[p])
                st = pool_state[p][k]
                if st[1] is not None:
                    op.deps.add(st[1])
                st[0] += 16
                st[1] = op
                op.sig = ((p, k), st[0])
            elif op.needed:
                cnt[E] += 1
                op.sig = (E, cnt[E])
            waits = {}
            for d in op.deps:
                k, v = d.sig
                if ck.get(k, 0) >= v:
                    continue
                if (not d.dma) and d.eng == E and E in self.nosync_same:
                    continue
                if waits.get(k, 0) < v:
                    waits[k] = v
            for d in op.deps:
                k, v = d.sig
                if k in waits and waits[k] >= v and d.vc is not None:
                    for kk, vv in d.vc.items():
                        if ck.get(kk, 0) < vv:
                            ck[kk] = vv
            for k, v in waits.items():
                if ck.get(k, 0) < v:
                    ck[k] = v
            op.waits = waits
            nwaits += len(waits)
            if op.sig is not None:
                vc = dict(ck)
                vc[op.sig[0]] = op.sig[1]
                op.vc = vc
            op.deps = None
        self.nwaits = nwaits

    def _sem(self, key):
        if isinstance(key, tuple):
            return self.pool_sems[key[0]][key[1]]
        return self.eng_sem[key]

    def emit(self, block):
        per = {e: [] for e in self.ENGS}
        for op in self.ops:
            per[op.eng].append(op)

        def run(eng_handle, ops):
            for op in ops:
                for k, v in op.waits.items():
                    eng_handle.wait_ge(self._sem(k), v)
                ins = op.fn(eng_handle)
                if op.sig is not None:
                    if ins is None:
                        ins = eng_handle.engine_nop() if hasattr(eng_handle, "engine_nop") else None
                    if ins is not None:
                        ins.then_inc(self._sem(op.sig[0]), 16 if op.dma else 1)
                    else:
                        eng_handle.sem_inc(self._sem(op.sig[0]), 1)

        if per["sp"]:
            block.sync(lambda e: run(e, per["sp"]))
        if per["pe"]:
            block.tensor(lambda e: run(e, per["pe"]))
        if per["act"]:
            block.scalar(lambda e: run(e, per["act"]))
        if per["dve"]:
            block.vector(lambda e: run(e, per["dve"]))
        if per["pool"]:
            block.gpsimd(lambda e: run(e, per["pool"]))


class Cfg:
    def __init__(self, D=2048, DFF=5632, TL=4096, S=256, NB=4, PAST=256, L=2):
        self.D, self.DFF, self.TL, self.S, self.NB, self.PAST, self.L = D, DFF, TL, S, NB, PAST, L
        self.DC = D // 128
        self.FC = DFF // 128
        self.TT = TL + NB * S
        self.NAW = NA_HEADS * HEAD_DIM
        self.GW = GDN_HEADS * HEAD_DIM
        self.DW = DIFF_HEADS * 2 * HEAD_DIM
        sp = (self.NAW, self.NAW, self.NAW, self.GW, self.GW, self.GW, self.GW, 4 * GDN_HEADS,
              self.DW, self.DW, self.DW, 3 * D)
        self.offs = [0] + list(np.cumsum(sp))
        self.DIN = int(self.offs[-1])
        self.blocks = []
        for t in range(0, TL, 512):
            self.blocks.append((t, min(512, TL - t), 0))
        ctx_tok = NB * S
        step = 512 if ctx_tok >= 512 else ctx_tok
        for t in range(0, ctx_tok, step):
            self.blocks.append((TL + t, step, 1))


def na_tile_plan(rows):
    wr = min(NA_WIN_R, rows)
    col = np.arange(GRID_W)
    cs = np.clip(col - NA_WIN_C // 2, 0, GRID_W - NA_WIN_C)
    types = {}
    tiles = []
    plan = []
    for j in range(rows // 2):
        qr = np.array([2 * j, 2 * j + 1])
        rs = np.clip(qr - NA_WIN_R // 2, 0, rows - wr)
        need = set()
        for a in range(2):
            for r in range(rs[a], rs[a] + wr):
                need.add(r // 2)
        lst = []
        for m in sorted(need):
            kr = np.array([2 * m, 2 * m + 1])
            KR = np.repeat(kr, GRID_W)[:, None]
            KC = np.tile(col, 2)[:, None]
            QR = np.repeat(qr, GRID_W)[None, :]
            QC = np.tile(col, 2)[None, :]
            RS = np.repeat(rs, GRID_W)[None, :]
            CS = np.tile(cs, 2)[None, :]
            valid = (KR >= RS) & (KR < RS + wr) & (KC >= CS) & (KC < CS + NA_WIN_C)
            dr = KR - QR + NA_WIN_R - 1
            dc = KC - QC + NA_WIN_C - 1
            dr = np.where(valid, dr, 0)
            dc = np.where(valid, dc, 0)
            key = (dr.tobytes(), dc.tobytes(), valid.tobytes())
            if key not in types:
                types[key] = len(tiles)
                tiles.append((dr, dc, valid))
            lst.append((m, types[key]))
        plan.append(lst)
    return plan, tiles


class Builder:
    def __init__(self, cfg):
        self.cfg = cfg
        self.nc = bass.Bass("TRN2", target_bir_lowering=False)
        self.P = Prog(self.nc, {"ld": 8, "w": 12, "st": 8})
        self.dram = {}
        self.dbuf = {}

    def din(self, name, shape, dt=F32):
        self.dram[name] = self.nc.dram_tensor(name, list(shape), dt, kind="ExternalInput").ap()
        self.dbuf[name] = Buf(name)
        return self.dram[name]

    def dout(self, name, shape, dt=F32):
        self.dram[name] = self.nc.dram_tensor(name, list(shape), dt, kind="ExternalOutput").ap()
        self.dbuf[name] = Buf(name)
        return self.dram[name]

    def dscr(self, name, shape, dt):
        self.dram[name] = self.nc.dram_tensor(name, list(shape), dt, kind="Internal").ap()
        self.dbuf[name] = Buf(name)
        return self.dram[name]

    def reset_arena(self):
        self.aoff = self.const_end

    def alloc(self, words, name=""):
        off = self.aoff
        self.aoff += (words + 7) // 8 * 8
        assert self.aoff <= getattr(self, "topoff", self.AW), ("sbuf arena overflow", name, self.aoff)
        return off

    def f32v(self, off, n):
        return self.arena[:, off:off + n]

    def bf16v(self, off, nbf):
        return self.arena[:, off:off + (nbf + 1) // 2].bitcast(BF16)

    def declare(self):
        c = self.cfg
        L = c.L
        d = self.din
        d("xs", [c.TL, c.D]); d("xp", [c.NB * c.S, c.D])
        d("cna_k", [L, c.PAST, c.NAW]); d("cna_v", [L, c.PAST, c.NAW])
        d("sgdn", [L, 2, GDN_HEADS, 128, 128])
        d("cd_k", [L, c.PAST, c.DW]); d("cd_v", [L, c.PAST, c.DW])
        d("cvec", [2, c.D])
        d("w_mod", [L, c.D, 9 * c.D]); d("b_mod", [L, 9 * c.D])
        d("norm_pre", [L, 3, c.D]); d("norm_post", [L, 3, c.D])
        d("ffn1_w_gu", [L, c.D, 2 * c.DFF]); d("ffn1_w_dn", [L, c.DFF, c.D])
        d("ffn2_w_gu", [L, c.D, 2 * c.DFF]); d("ffn2_w_dn", [L, c.DFF, c.D])
        d("w_in", [L, c.D, c.DIN])
        d("gdn_conv", [L, 3, 3 * c.GW]); d("gdn_a_log", [L, 2, 8]); d("gdn_dt_bias", [L, 2, 8])
        d("gdn_norm", [L, 128]); d("diff_lambda", [L, 4, 128]); d("diff_norm", [L, 256])
        d("w_branch_na", [L, c.NAW, c.D]); d("w_branch_gdn", [L, c.GW, c.D]); d("w_branch_diff", [L, c.DW, c.D])
        d("w_out", [L, c.D, c.D])
        d("k_ident", [128, 128]); d("k_perm", [128, 128])
        d("k_cos", [128, c.TL]); d("k_sin", [128, c.TL])
        d("k_masks", [6, 64, 64]); d("k_tri", [2, 64, 64])
        d("nab", [L, NA_HEADS, self.nty, 128, 128])
        o = self.dout
        o("y_s", [c.TL, c.D]); o("y_p", [c.NB * c.S, c.D])
        o("o_na_k", [c.NB, L, c.S, c.NAW]); o("o_na_v", [c.NB, L, c.S, c.NAW])
        o("o_gdn", [c.NB, L, 2, GDN_HEADS, 128, 128])
        o("o_d_k", [c.NB, L, c.S, c.DW]); o("o_d_v", [c.NB, L, c.S, c.DW])
        s = self.dscr
        s("XR", [c.D, c.TT], F32)
        s("QNA", [c.NAW, c.TT], BF16); s("KNA", [c.NAW, c.TT], BF16); s("VNA", [c.TT, c.NAW], BF16)
        s("GQ", [c.GW, c.TT], F32); s("GK", [c.GW, c.TT], F32); s("GV", [c.GW, c.TT], F32)
        s("GZT", [c.TT, c.GW], F32); s("GAB", [c.TT, 32], F32)
        s("DQ", [c.DW, c.TT], BF16); s("DK", [c.DW, c.TT], BF16); s("DV", [c.TT, c.DW], BF16)
        s("GATE", [3 * c.D, c.TT], BF16)
        s("ONA", [c.NAW, c.TT], BF16); s("OG", [c.GW, c.TT], BF16); s("OD", [c.DW, c.TT], BF16)
        s("GQN", [c.GW, c.TT], F32); s("GKN", [c.GW, c.TT], F32)
        s("GKT", [c.TT, c.GW], F32); s("GVT", [c.TT, c.GW], F32)
        s("GAB2", [c.TT, 32], F32); s("GOF", [c.TT, c.GW], F32)

    def build(self):
        c = self.cfg
        nc = self.nc
        P = self.P
        self.plan, self.na_tiles = na_tile_plan(c.TL // GRID_W)
        self.nty = len(self.na_tiles)
        self.declare()
        with ExitStack() as st:
            self.AW = 46800
            self.arena = st.enter_context(nc.sbuf_tensor("arena", [128, self.AW], F32))
            self.ps = st.enter_context(nc.psum_tensor("ps", [128, 8, 512], F32))
            self.psb = [Buf("ps%d" % i) for i in range(8)]
            self.aoff = 0
            self.const_end = 0
            self.consts()
            self.const_end = self.aoff
            self.modulation()
            self.const_end = self.aoff
            for seg in range(c.L + 1):
                if seg >= getattr(self, "seg_limit", 99):
                    break
                P.barrier()
                self.reset_arena()
                self.dense_segment(seg - 1 if seg > 0 else None, seg if seg < c.L else None)
                if seg < c.L:
                    P.barrier()
                    self.reset_arena()
                    self.mixers(seg)
            P.barrier()
            P.finalize(st)
            with nc.Block() as block:
                P.emit(block)
        return nc

    def consts(self):
        c, P = self.cfg, self.P
        self.b_const = Buf("const")
        o = self.alloc(128); self.identF = self.f32v(o, 128)
        o = self.alloc(128); self.permF = self.f32v(o, 128)
        o = self.alloc(128); self.onesF = self.f32v(o, 128)
        o = self.alloc(64); self.identB = self.bf16v(o, 128)
        o = self.alloc(64); self.onesB = self.bf16v(o, 128)
        P.dma("sp", self.identF, self.dram["k_ident"], writes=[self.b_const])
        P.dma("sp", self.permF, self.dram["k_perm"], writes=[self.b_const])
        o = self.alloc(8); self.epsT = self.f32v(o, 1)
        P.add("dve", lambda e: e.memset(self.epsT, EPS), writes=[self.b_const])
        P.add("dve", lambda e: e.memset(self.onesF, 1.0), writes=[self.b_const])
        P.add("dve", lambda e: e.memset(self.onesB, 1.0), writes=[self.b_const])
        P.add("dve", lambda e: e.tensor_copy(out=self.identB, in_=self.identF), writes=[self.b_const])

    def wpanel_load(self, slot, w2d, r0, kc, col0, ncols):
        P = self.P
        view = self.wp[slot][:, 0:kc * ncols].rearrange("p (k n) -> p k n", k=kc)
        step = max(1, 512 // max(1, (ncols * 4) // 512)) if False else 4
        for k0 in range(0, kc, step):
            k1 = min(kc, k0 + step)
            P.dma("pool", view[:, k0:k1, :],
                  w2d[r0 + k0 * 128:r0 + k1 * 128, col0:col0 + ncols].rearrange("(k p) n -> p k n", p=128),
                  writes=[self.wpb[slot][k0 // 4]], pool="w", extra=self.wextra)
        return view

    def next_wslot(self):
        s = self.wslot
        self.wslot = (self.wslot + 1) % len(self.wp)
        extra = set()
        for b in self.wpb[s]:
            if b.last_w is not None:
                extra.add(b.last_w)
            extra.update(b.readers)
        self.wpb[s] = [Buf("wp%d_%d" % (s, q)) for q in range(8)]
        self.wextra = extra
        return s

    def next_bank(self):
        b = self.banks[self.bank_i % len(self.banks)]
        self.bank_i += 1
        return b

    def modulation(self):
        c, P, ps = self.cfg, self.P, self.ps
        DC = c.DC
        b_m = Buf("modtmp")
        save = self.aoff
        o = self.alloc(DC * 2); cT = self.f32v(o, DC * 2).rearrange("p (k g) -> p k g", g=2)
        o = self.alloc(DC); cTb = self.bf16v(o, DC * 2).rearrange("p (k g) -> p k g", g=2)
        o = self.alloc(9 * c.D); brow = self.arena[0:1, o:o + 9 * c.D]
        o = self.alloc(2); ones2 = self.arena[0:1, o:o + 2]
        self.wp = []
        self.wpb = []
        for i in range(3):
            o = self.alloc(DC * 512 // 2)
            self.wp.append(self.bf16v(o, DC * 512)); self.wpb.append([Buf("wp%d_%d" % (i, q)) for q in range(8)])
        self.wslot = 0
        npre = []
        self.modtab = {}
        keep = []
        P.add("dve", lambda e: e.memset(ones2, 1.0), writes=[b_m])
        for g_ in range(2):
            P.dma("sp", cT[:, :, g_], self.dram["cvec"][g_].rearrange("(k p) -> p k", p=128), writes=[b_m],
                  allow_slow_non_contiguous=True)
        P.add("act", lambda e: e.activation(out=cTb, in_=cT, func=AF.Silu), reads=[b_m], writes=[b_m])
        self.aoff_keep = None
        tabs = {}
        for l in range(c.L):
            P.dma("sp", brow, self.dram["b_mod"][l:l + 1, :], writes=[b_m])
            nch = 9 * DC
            modps = ps[:, 7, 0:nch * 2].rearrange("p (n g) -> p n g", g=2)
            bps = self.psb[7]
            for n0 in range(0, 9 * c.D, 512):
                slot = self.next_wslot()
                ncl = min(512, 9 * c.D - n0)
                wv = self.wpanel_load(slot, self.dram["w_mod"][l], 0, DC, n0, ncl)
                for j in range(ncl // 128):
                    n = n0 // 128 + j
                    for k in range(DC):
                        P.add("pe", lambda e, wv=wv, k=k, j=j, n=n: e.matmul(
                            modps[:, n, :], lhsT=wv[:, k, j * 128:(j + 1) * 128], rhs=cTb[:, k, :],
                            start=(k == 0), stop=False), reads=[self.wpb[slot][k // 4], b_m], writes=[bps])
                    P.add("pe", lambda e, n=n: e.matmul(
                        modps[:, n, :], lhsT=brow[0:1, n * 128:(n + 1) * 128], rhs=ones2,
                        start=False, stop=True), reads=[b_m], writes=[bps])
            tabs[l] = None
            o = self._top_alloc(9 * DC * 2)
            mv = self.f32v(o, 9 * DC * 2).rearrange("p (i k g) -> p i k g", i=9, g=2)
            P.add("act", lambda e, mv=mv, modps=modps: e.copy(out=mv, in_=modps.rearrange("p (i k) g -> p i k g", i=9)),
                  reads=[], writes=[bps, self.b_const])
            o = self._top_alloc(3 * DC); npre_l = self.f32v(o, 3 * DC).rearrange("p (i k) -> p i k", i=3)
            o = self._top_alloc(3 * DC); npost_l = self.f32v(o, 3 * DC).rearrange("p (i k) -> p i k", i=3)
            for i_ in range(3):
                P.dma("sp", npre_l[:, i_, :], self.dram["norm_pre"][l, i_].rearrange("(k p) -> p k", p=128),
                      writes=[self.b_const], allow_slow_non_contiguous=True)
                P.dma("sp", npost_l[:, i_, :], self.dram["norm_post"][l, i_].rearrange("(k p) -> p k", p=128),
                      writes=[self.b_const], allow_slow_non_contiguous=True)
            for g in range(2):
                o = self._top_alloc(3 * DC); A = self.f32v(o, 3 * DC).rearrange("p (i k) -> p i k", i=3)
                o = self._top_alloc(3 * DC); Bt = self.f32v(o, 3 * DC).rearrange("p (i k) -> p i k", i=3)
                o = self._top_alloc(3 * DC); G = self.f32v(o, 3 * DC).rearrange("p (i k) -> p i k", i=3)
                for i in range(3):
                    coef = 1.0 if i == 1 else 0.5
                    P.add("dve", lambda e, A=A, mv=mv, npre_l=npre_l, i=i, g=g: e.scalar_tensor_tensor(
                        out=A[:, i, :], in0=mv[:, 3 * i + 1, :, g], scalar=1.0, in1=npre_l[:, i, :],
                        op0=ALU.add, op1=ALU.mult), reads=[self.b_const], writes=[self.b_const])
                    P.add("dve", lambda e, Bt=Bt, mv=mv, i=i, g=g: e.tensor_copy(out=Bt[:, i, :], in_=mv[:, 3 * i, :, g]),
                          reads=[self.b_const], writes=[self.b_const])
                    P.add("dve", lambda e, G=G, mv=mv, npost_l=npost_l, i=i, g=g, coef=coef: e.scalar_tensor_tensor(
                        out=G[:, i, :], in0=mv[:, 3 * i + 2, :, g], scalar=coef, in1=npost_l[:, i, :],
                        op0=ALU.mult, op1=ALU.mult), reads=[self.b_const], writes=[self.b_const])
                self.modtab[(l, g)] = (A, Bt, G)
        self.P.barrier()
        self.aoff = save

    def _top_alloc(self, words):
        if not hasattr(self, "topoff"):
            self.topoff = self.AW
        self.topoff -= (words + 7) // 8 * 8
        self.AW_eff = self.topoff
        return self.topoff

    def dense_alloc(self):
        c = self.cfg
        DC, FC = c.DC, c.FC
        A = {}
        def f32(name, n):
            A[name] = self.f32v(self.alloc(n, name), n); A["b_" + name] = Buf(name)
        def b16(name, n):
            A[name] = self.bf16v(self.alloc((n + 1) // 2, name), n); A["b_" + name] = Buf(name)
        f32("x", DC * 512)
        b16("h", DC * 512)
        b16("act", max((FC + 1) // 2, 8) * 512)
        f32("y", DC * 512)
        for i in range(3):
            f32("t%d" % i, 512)
        f32("rstd", 512)
        for i in range(4):
            b16("sb%d" % i, 512)
            f32("sf%d" % i, 512)
        f32("cos", 512); f32("sin", 512)
        self.wp, self.wpb = [], []
        for i in range(3):
            o = self.alloc(8192 // 2)
            self.wp.append(self.bf16v(o, 8192)); self.wpb.append([Buf("wp%d_%d" % (i, q)) for q in range(8)])
        self.wslot = 0
        self.banks = [2, 3, 4, 5, 6, 7]
        self.bank_i = 0
        self.A = A
        self.rr = {"t": 0, "sb": 0, "sf": 0}
        return A

    def rot(self, kind, n):
        i = self.rr[kind]
        self.rr[kind] = (i + 1) % n
        return "%s%d" % (kind, i)

    def x3(self, name, nt):
        c = self.cfg
        return self.A[name][:, 0:c.DC * nt].rearrange("p (k t) -> p k t", t=nt)

    def norm_stats(self, src3, nt, srcbuf, kc, scale):
        P, ps, A = self.P, self.ps, self.A
        bank = 0 if self.bank_i % 2 == 0 else 1
        for k in range(kc):
            tn = self.rot("t", 3)
            t = A[tn][:, 0:nt]
            P.add("act", lambda e, t=t, k=k: e.activation(out=t, in_=src3[:, k, :], func=AF.Square),
                  reads=[srcbuf], writes=[A["b_" + tn]])
            P.add("pe", lambda e, t=t, k=k, bank=bank: e.matmul(ps[:, bank, 0:nt], lhsT=self.onesF, rhs=t,
                                                                start=(k == 0), stop=(k == kc - 1)),
                  reads=[A["b_" + tn], self.b_const], writes=[self.psb[bank]])
        rstd = A["rstd"][:, 0:nt]
        P.add("act", lambda e: e.activation(out=rstd, in_=ps[:, bank, 0:nt], func=AF.Ln, bias=self.epsT, scale=scale),
              reads=[self.b_const], writes=[self.psb[bank], A["b_rstd"]])
        P.add("act", lambda e: e.activation(out=rstd, in_=rstd, func=AF.Exp, scale=-0.5), writes=[A["b_rstd"]])
        return rstd

    def prenorm(self, l, i, g, nt):
        c, P, A = self.cfg, self.P, self.A
        x3, h3 = self.x3("x", nt), self.x3("h", nt)
        rstd = self.norm_stats(x3, nt, A["b_x"], c.DC, 1.0 / c.D)
        At, Bt, _ = self.modtab[(l, g)]
        for k in range(c.DC):
            tn = self.rot("t", 3)
            t = A[tn][:, 0:nt]
            P.add("dve", lambda e, t=t, k=k: e.scalar_tensor_tensor(out=t, in0=x3[:, k, :], scalar=At[:, i, k:k + 1],
                                                                   in1=rstd, op0=ALU.mult, op1=ALU.mult),
                  reads=[A["b_x"], A["b_rstd"], self.b_const], writes=[A["b_" + tn]])
            P.add("act", lambda e, t=t, k=k: e.activation(out=h3[:, k, :], in_=t, func=AF.Identity,
                                                          bias=Bt[:, i, k:k + 1], scale=1.0),
                  reads=[A["b_" + tn], self.b_const], writes=[A["b_h"]])

    def resid(self, l, i, g, nt):
        c, P, A = self.cfg, self.P, self.A
        x3, y3 = self.x3("x", nt), self.x3("y", nt)
        rstd = self.norm_stats(y3, nt, A["b_y"], c.DC, 1.0 / c.D)
        _, _, G = self.modtab[(l, g)]
        for k in range(c.DC):
            tn = self.rot("t", 3)
            t = A[tn][:, 0:nt]
            P.add("dve", lambda e, t=t, k=k: e.scalar_tensor_tensor(out=t, in0=y3[:, k, :], scalar=G[:, i, k:k + 1],
                                                                   in1=rstd, op0=ALU.mult, op1=ALU.mult),
                  reads=[A["b_y"], A["b_rstd"], self.b_const], writes=[A["b_" + tn]])
            P.add("pool", lambda e, t=t, k=k: e.tensor_tensor(out=x3[:, k, :], in0=x3[:, k, :], in1=t, op=ALU.add),
                  reads=[A["b_" + tn]], writes=[A["b_x"]])

    def gemm_fm(self, xname, kc, nt, w2d, r0, cols, evac):
        P, ps, A = self.P, self.ps, self.A
        x3 = A[xname][:, 0:kc * nt].rearrange("p (k t) -> p k t", t=nt)
        pc = min(512, (8192 // kc) // 128 * 128)
        i = 0
        while i < len(cols):
            j = i
            while j + 1 < len(cols) and cols[j + 1] == cols[j] + 128 and (cols[j + 1] + 128 - cols[i]) <= pc:
                j += 1
            slot = self.next_wslot()
            ncols = cols[j] + 128 - cols[i]
            wv = self.wpanel_load(slot, w2d, r0, kc, cols[i], ncols)
            for q in range(i, j + 1):
                bank = self.next_bank()
                off = cols[q] - cols[i]
                for k in range(kc):
                    P.add("pe", lambda e, wv=wv, k=k, off=off, bank=bank: e.matmul(
                        ps[:, bank, 0:nt], lhsT=wv[:, k, off:off + 128], rhs=x3[:, k, :],
                        start=(k == 0), stop=(k == kc - 1)),
                        reads=[self.wpb[slot][k // 4], A["b_" + xname]], writes=[self.psb[bank]])
                evac(q, cols[q], bank)
            i = j + 1

    def gemm_tm(self, xname, kc, nt, w2d, r0, col0, ncols, evac):
        P, ps, A = self.P, self.ps, self.A
        x3 = A[xname][:, 0:kc * nt].rearrange("p (k t) -> p k t", t=nt)
        for c0 in range(col0, col0 + ncols, 512):
            ncl = min(512, col0 + ncols - c0)
            slot = self.next_wslot()
            wv = self.wpanel_load(slot, w2d, r0, kc, c0, ncl)
            for tt in range(nt // 128):
                bank = self.next_bank()
                for k in range(kc):
                    P.add("pe", lambda e, wv=wv, k=k, tt=tt, bank=bank, ncl=ncl: e.matmul(
                        ps[:, bank, 0:ncl], lhsT=x3[:, k, tt * 128:(tt + 1) * 128], rhs=wv[:, k, :],
                        start=(k == 0), stop=(k == kc - 1)),
                        reads=[self.wpb[slot][k // 4], A["b_" + xname]], writes=[self.psb[bank]])
                evac(tt, c0, ncl, bank)

    def ffn(self, l, which, nt):
        c, P, ps, A = self.cfg, self.P, self.ps, self.A
        wgu = self.dram["ffn%d_w_gu" % which][l]
        wdn = self.dram["ffn%d_w_dn" % which][l]
        FH = (c.FC + 1) // 2
        act3 = A["act"][:, 0:FH * nt].rearrange("p (k t) -> p k t", t=nt)
        h3 = self.x3("h", nt)
        y3 = self.x3("y", nt)
        for hf in range(2):
            jlo, jhi = hf * FH, min(c.FC, (hf + 1) * FH)
            for j0 in range(jlo, jhi, 2):
                nj = min(2, jhi - j0)
                slot = self.next_wslot()
                wv = self.wp[slot][:, 0:c.DC * 2 * nj * 128].rearrange("p (k n) -> p k n", k=c.DC)
                for half in range(2):
                    for k0 in range(0, c.DC, 4):
                        k1 = min(c.DC, k0 + 4)
                        P.dma("pool", wv[:, k0:k1, half * nj * 128:(half + 1) * nj * 128],
                              wgu[k0 * 128:k1 * 128, half * c.DFF + j0 * 128: half * c.DFF + (j0 + nj) * 128]
                              .rearrange("(k p) n -> p k n", p=128), writes=[self.wpb[slot][half * 4 + k0 // 4]], pool="w", extra=self.wextra)
                for jj in range(nj):
                    j = j0 + jj
                    bg, bu = self.next_bank(), self.next_bank()
                    for half, bank in ((0, bg), (1, bu)):
                        off = half * nj * 128 + jj * 128
                        for k in range(c.DC):
                            P.add("pe", lambda e, wv=wv, k=k, off=off, bank=bank: e.matmul(
                                ps[:, bank, 0:nt], lhsT=wv[:, k, off:off + 128], rhs=h3[:, k, :],
                                start=(k == 0), stop=(k == c.DC - 1)),
                                reads=[self.wpb[slot][half * 4 + k // 4], A["b_h"]], writes=[self.psb[bank]])
                    sn = self.rot("sf", 4)
                    sg = A[sn][:, 0:nt]
                    P.add("act", lambda e, sg=sg, bg=bg: e.activation(out=sg, in_=ps[:, bg, 0:nt], func=AF.Silu),
                          writes=[self.psb[bg], A["b_" + sn]])
                    P.add("dve", lambda e, sg=sg, bu=bu, j=j - jlo: e.tensor_tensor(out=act3[:, j, :], in0=sg,
                                                                            in1=ps[:, bu, 0:nt], op=ALU.mult),
                          reads=[A["b_" + sn]], writes=[self.psb[bu], A["b_act"]])

            def ev(q, col0, bank, hf=hf):
                if hf == 0:
                    P.add("act", lambda e: e.copy(out=y3[:, q, :], in_=ps[:, bank, 0:nt]),
                          writes=[self.psb[bank], A["b_y"]])
                else:
                    P.add("dve", lambda e: e.tensor_tensor(out=y3[:, q, :], in0=y3[:, q, :], in1=ps[:, bank, 0:nt],
                                                           op=ALU.add), writes=[self.psb[bank], A["b_y"]])
            self.gemm_fm("act", jhi - jlo, nt, wdn, jlo * 128, [k * 128 for k in range(c.DC)], ev)

    def load_x_input(self, blk):
        c, P, ps, A = self.cfg, self.P, self.ps, self.A
        t0, nt, g = blk
        src = self.dram["xs"] if g == 0 else self.dram["xp"]
        sb = self.dbuf["xs" if g == 0 else "xp"]
        r0 = t0 if g == 0 else t0 - c.TL
        x3 = self.x3("x", nt)
        stage = A["y"][:, 0:c.D]
        for tt in range(nt // 128):
            P.dma("sp", stage, src[r0 + tt * 128:r0 + (tt + 1) * 128, :], reads=[sb], writes=[A["b_y"]])
            for k in range(c.DC):
                bank = self.next_bank()
                P.add("pe", lambda e, k=k, bank=bank: e.transpose(out=ps[:, bank, 0:128], in_=stage[:, k * 128:(k + 1) * 128],
                                                                  identity=self.identF),
                      reads=[A["b_y"], self.b_const], writes=[self.psb[bank]])
                P.add("act" if k % 2 else "dve",
                      (lambda e, k=k, bank=bank, tt=tt: e.copy(out=x3[:, k, tt * 128:(tt + 1) * 128], in_=ps[:, bank, 0:128])) if k % 2 else
                      (lambda e, k=k, bank=bank, tt=tt: e.tensor_copy(out=x3[:, k, tt * 128:(tt + 1) * 128], in_=ps[:, bank, 0:128])),
                      writes=[self.psb[bank], A["b_x"]])

    def store_y_output(self, blk):
        c, P, ps, A = self.cfg, self.P, self.ps, self.A
        t0, nt, g = blk
        dst = self.dram["y_s"] if g == 0 else self.dram["y_p"]
        db = self.dbuf["y_s" if g == 0 else "y_p"]
        r0 = t0 if g == 0 else t0 - c.TL
        x3 = self.x3("x", nt)
        stage = A["y"][:, 0:c.D]
        for tt in range(nt // 128):
            for k in range(c.DC):
                bank = self.next_bank()
                P.add("pe", lambda e, k=k, bank=bank, tt=tt: e.transpose(out=ps[:, bank, 0:128], in_=x3[:, k, tt * 128:(tt + 1) * 128],
                                                                         identity=self.identF),
                      reads=[A["b_x"], self.b_const], writes=[self.psb[bank]])
                P.add("act", lambda e, k=k, bank=bank: e.copy(out=stage[:, k * 128:(k + 1) * 128], in_=ps[:, bank, 0:128]),
                      writes=[self.psb[bank], A["b_y"]])
            P.dma("sp", dst[r0 + tt * 128:r0 + (tt + 1) * 128, :], stage, reads=[A["b_y"]], writes=[db], pool="st")

    def dense_segment(self, lp, ln):
        c, P, A = self.cfg, self.P, self.dense_alloc()
        for blk in c.blocks:
            t0, nt, g = blk
            if lp is None:
                self.load_x_input(blk)
            else:
                P.dma("sp", self.x3("x", nt), self.dram["XR"][:, t0:t0 + nt].rearrange("(k p) t -> p k t", p=128),
                      reads=[self.dbuf["XR"]], writes=[A["b_x"]])
                self.merge(lp, blk)
                self.prenorm(lp, 2, g, nt)
                self.ffn(lp, 2, nt)
                self.resid(lp, 2, g, nt)
            if ln is not None:
                self.prenorm(ln, 0, g, nt)
                self.ffn(ln, 1, nt)
                self.resid(ln, 0, g, nt)
                self.prenorm(ln, 1, g, nt)
                self.proj(ln, blk)
                P.dma("sp", self.dram["XR"][:, t0:t0 + nt].rearrange("(k p) t -> p k t", p=128), self.x3("x", nt),
                      reads=[A["b_x"]], writes=[self.dbuf["XR"]], pool="st")
            else:
                self.store_y_output(blk)

    def proj(self, l, blk):
        c, P, ps, A = self.cfg, self.P, self.ps, self.A
        t0, nt, g = blk
        w = self.dram["w_in"][l]
        o = c.offs
        D = self.dram
        B = self.dbuf

        def fm_store(dst, dt_bf16, col_base, func=None):
            def ev(q, col0, bank):
                r = col0 - col_base
                sn = self.rot("sb", 4) if dt_bf16 else self.rot("sf", 4)
                sv = A[sn][:, 0:nt]
                if func is None:
                    P.add("act", lambda e: e.copy(out=sv, in_=ps[:, bank, 0:nt]), writes=[self.psb[bank], A["b_" + sn]])
                else:
                    P.add("act", lambda e: e.activation(out=sv, in_=ps[:, bank, 0:nt], func=func),
                          writes=[self.psb[bank], A["b_" + sn]])
                P.dma("sp", D[dst][r:r + 128, t0:t0 + nt], sv, reads=[A["b_" + sn]], writes=[B[dst]], pool="st")
            return ev

        def rope_store(dst, col_base):
            cos, sin = A["cos"][:, 0:nt], A["sin"][:, 0:nt]

            def ev(q, col0, bank):
                r = col0 - col_base
                fn_ = self.rot("sf", 4); qf = A[fn_][:, 0:nt]
                P.add("act", lambda e: e.copy(out=qf, in_=ps[:, bank, 0:nt]), writes=[self.psb[bank], A["b_" + fn_]])
                b2 = self.next_bank()
                P.add("pe", lambda e: e.matmul(ps[:, b2, 0:nt], lhsT=self.permF, rhs=qf, start=True, stop=True),
                      reads=[A["b_" + fn_], self.b_const], writes=[self.psb[b2]])
                tn = self.rot("t", 3); t = A[tn][:, 0:nt]
                P.add("dve", lambda e: e.tensor_tensor(out=t, in0=ps[:, b2, 0:nt], in1=sin, op=ALU.mult),
                      reads=[A["b_sin"]], writes=[self.psb[b2], A["b_" + tn]])
                P.add("pool", lambda e: e.tensor_tensor(out=qf, in0=qf, in1=cos, op=ALU.mult),
                      reads=[A["b_cos"]], writes=[A["b_" + fn_]])
                sn = self.rot("sb", 4); sv = A[sn][:, 0:nt]
                P.add("dve", lambda e: e.tensor_tensor(out=sv, in0=qf, in1=t, op=ALU.add),
                      reads=[A["b_" + fn_], A["b_" + tn]], writes=[A["b_" + sn]])
                P.dma("sp", D[dst][r:r + 128, t0:t0 + nt], sv, reads=[A["b_" + sn]], writes=[B[dst]], pool="st")
            return ev

        def chunks(i):
            return list(range(int(o[i]), int(o[i + 1]), 128))

        self.gemm_fm("h", c.DC, nt, w, 0, chunks(0), fm_store("QNA", True, int(o[0])))
        self.gemm_fm("h", c.DC, nt, w, 0, chunks(1), fm_store("KNA", True, int(o[1])))
        self.gemm_fm("h", c.DC, nt, w, 0, chunks(3), fm_store("GQ", False, int(o[3])))
        self.gemm_fm("h", c.DC, nt, w, 0, chunks(4), fm_store("GK", False, int(o[4])))
        self.gemm_fm("h", c.DC, nt, w, 0, chunks(5), fm_store("GV", False, int(o[5])))
        if g == 0:
            P.dma("sp", A["cos"][:, 0:nt], D["k_cos"][:, t0:t0 + nt], writes=[A["b_cos"]])
            P.dma("sp", A["sin"][:, 0:nt], D["k_sin"][:, t0:t0 + nt], writes=[A["b_sin"]])
            self.gemm_fm("h", c.DC, nt, w, 0, chunks(8), rope_store("DQ", int(o[8])))
            self.gemm_fm("h", c.DC, nt, w, 0, chunks(9), rope_store("DK", int(o[9])))
        else:
            self.gemm_fm("h", c.DC, nt, w, 0, chunks(8), fm_store("DQ", True, int(o[8])))
            self.gemm_fm("h", c.DC, nt, w, 0, chunks(9), fm_store("DK", True, int(o[9])))
        self.gemm_fm("h", c.DC, nt, w, 0, chunks(11), fm_store("GATE", True, int(o[11]), AF.Sigmoid))

        def tm_store(dst, col_base, bf, func=None, cache=None):
            def ev(tt, c0, ncl, bank):
                r = c0 - col_base
                tok = t0 + tt * 128
                fn_ = self.rot("sf", 4); sv = A[fn_][:, 0:ncl]
                if func is None:
                    P.add("act", lambda e: e.copy(out=sv, in_=ps[:, bank, 0:ncl]), writes=[self.psb[bank], A["b_" + fn_]])
                else:
                    P.add("act", lambda e: e.activation(out=sv, in_=ps[:, bank, 0:ncl], func=func),
                          writes=[self.psb[bank], A["b_" + fn_]])
                if dst is not None:
                    if bf:
                        sn = self.rot("sb", 4); sb_ = A[sn][:, 0:ncl]
                        P.add("dve", lambda e: e.tensor_copy(out=sb_, in_=sv), reads=[A["b_" + fn_]], writes=[A["b_" + sn]])
                        P.dma("sp", D[dst][tok:tok + 128, r:r + ncl], sb_, reads=[A["b_" + sn]], writes=[B[dst]], pool="st")
                    else:
                        P.dma("sp", D[dst][tok:tok + 128, r:r + ncl], sv, reads=[A["b_" + fn_]], writes=[B[dst]], pool="st")
                if cache is not None and g == 1:
                    ct = tok - c.TL
                    b_, s_ = ct // c.S, ct % c.S
                    P.dma("sp", D[cache][b_, l, s_:s_ + 128, r:r + ncl], sv, reads=[A["b_" + fn_]], writes=[B[cache]], pool="st")
            return ev

        self.gemm_tm("h", c.DC, nt, w, 0, int(o[2]), c.NAW, tm_store("VNA", int(o[2]), True, cache="o_na_v"))
        self.gemm_tm("h", c.DC, nt, w, 0, int(o[6]), c.GW, tm_store("GZT", int(o[6]), False, func=AF.Silu))
        self.gemm_tm("h", c.DC, nt, w, 0, int(o[7]), 32, tm_store("GAB", int(o[7]), False))
        self.gemm_tm("h", c.DC, nt, w, 0, int(o[10]), c.DW, tm_store("DV", int(o[10]), True, cache="o_d_v"))
        if g == 1:
            self.gemm_tm("h", c.DC, nt, w, 0, int(o[1]), c.NAW, tm_store(None, int(o[1]), False, cache="o_na_k"))
            self.gemm_tm("h", c.DC, nt, w, 0, int(o[9]), c.DW, tm_store(None, int(o[9]), False, cache="o_d_k"))

    def merge(self, l, blk):
        c, P, ps, A = self.cfg, self.P, self.ps, self.A
        t0, nt, g = blk
        D, B = self.dram, self.dbuf
        y3, h3 = self.x3("y", nt), self.x3("h", nt)
        o3 = A["act"][:, 0:8 * nt].rearrange("p (k t) -> p k t", t=nt)
        for br, (src, wn) in enumerate((("ONA", "w_branch_na"), ("OG", "w_branch_gdn"), ("OD", "w_branch_diff"))):
            P.dma("sp", o3, D[src][:, t0:t0 + nt].rearrange("(k p) t -> p k t", p=128), reads=[B[src]], writes=[A["b_act"]])

            def ev(q, col0, bank, br=br):
                sn = self.rot("sb", 4); gt = A[sn][:, 0:nt]
                r = br * c.D + col0
                P.dma("sp", gt, D["GATE"][r:r + 128, t0:t0 + nt], reads=[B["GATE"]], writes=[A["b_" + sn]])
                if br == 0:
                    P.add("dve", lambda e: e.tensor_tensor(out=y3[:, q, :], in0=gt, in1=ps[:, bank, 0:nt], op=ALU.mult),
                          reads=[A["b_" + sn]], writes=[self.psb[bank], A["b_y"]])
                else:
                    tn = self.rot("t", 3); t = A[tn][:, 0:nt]
                    P.add("dve", lambda e: e.tensor_tensor(out=t, in0=gt, in1=ps[:, bank, 0:nt], op=ALU.mult),
                          reads=[A["b_" + sn]], writes=[self.psb[bank], A["b_" + tn]])
                    P.add("pool", lambda e: e.tensor_tensor(out=y3[:, q, :], in0=y3[:, q, :], in1=t, op=ALU.add),
                          reads=[A["b_" + tn]], writes=[A["b_y"]])
            self.gemm_fm("act", 8, nt, D[wn][l], 0, [k * 128 for k in range(c.DC)], ev)
        for k in range(c.DC):
            P.add("act", lambda e, k=k: e.copy(out=h3[:, k, :], in_=y3[:, k, :]), reads=[A["b_y"]], writes=[A["b_h"]])

        def ev2(q, col0, bank):
            P.add("act", lambda e: e.copy(out=y3[:, q, :], in_=ps[:, bank, 0:nt]), writes=[self.psb[bank], A["b_y"]])
        self.gemm_fm("h", c.DC, nt, D["w_out"][l], 0, [k * 128 for k in range(c.DC)], ev2)
        self.resid(l, 1, g, nt)

    def mixers(self, l):
        c, P = self.cfg, self.P
        self.mix_setup(l)
        mark = self.aoff
        for h in range(NA_HEADS):
            self.aoff = mark
            self.na_head(l, h)
            P.barrier()
        P.barrier()
        for h in range(DIFF_HEADS):
            self.aoff = mark
            self.diff_head(l, h)
            P.barrier()
        self.aoff = mark
        self.gdn(l)

    def mix_setup(self, l):
        c, P, ps = self.cfg, self.P, self.ps
        D, B = self.dram, self.dbuf
        PC = c.PAST // 128
        M = {}
        self.M = M
        M["b"] = Buf("mixconst")
        o = self.alloc(8 * c.PAST // 2); M["KcNA"] = self.bf16v(o, 8 * c.PAST).rearrange("p (h t) -> p h t", h=8)
        o = self.alloc(8 * c.PAST // 2); M["KcD"] = self.bf16v(o, 8 * c.PAST).rearrange("p (h t) -> p h t", h=8)
        o = self.alloc(PC * 1024 // 2); M["VcNA"] = self.bf16v(o, PC * 1024).rearrange("p (k n) -> p k n", k=PC)
        o = self.alloc(PC * 1024 // 2); M["VcD"] = self.bf16v(o, PC * 1024).rearrange("p (k n) -> p k n", k=PC)
        o = self.alloc(8); M["lam"] = self.f32v(o, 1)
        o = self.alloc(8); M["nlam"] = self.f32v(o, 1)
        o = self.alloc(8); M["dnw"] = self.f32v(o, 2)
        mark = self.aoff
        o = self.alloc(1024); stage = self.f32v(o, 1024)
        bst = Buf("stage")
        for name, src in (("KcNA", "cna_k"), ("KcD", "cd_k")):
            for k in range(PC):
                P.dma("sp", stage, D[src][l, k * 128:(k + 1) * 128, :], writes=[bst])
                for h in range(8):
                    bank = 6 + (h % 2)
                    P.add("pe", lambda e, h=h, bank=bank: e.transpose(out=ps[:, bank, 0:128], in_=stage[:, h * 128:(h + 1) * 128],
                                                                      identity=self.identF),
                          reads=[bst, self.b_const], writes=[self.psb[bank]])
                    P.add("act", lambda e, h=h, bank=bank, k=k, name=name: e.copy(out=M[name][:, h, k * 128:(k + 1) * 128],
                                                                                 in_=ps[:, bank, 0:128]),
                          writes=[self.psb[bank], M["b"]])
        for name, src in (("VcNA", "cna_v"), ("VcD", "cd_v")):
            P.dma("pool", M[name], D[src][l].rearrange("(k p) n -> p k n", p=128), writes=[M["b"]], pool="w")
        lam_init = 0.8 - 0.6 * math.exp(-0.3 * l)
        o = self.alloc(8); lt = self.f32v(o, 4)
        o = self.alloc(8); pr = self.f32v(o, 2)
        for i_ in range(4):
            P.dma("sp", lt[:, i_:i_ + 1], D["diff_lambda"][l, i_].rearrange("(p o) -> p o", o=1), writes=[bst],
                  allow_slow_non_contiguous=True)
        P.add("dve", lambda e: e.tensor_tensor(out=pr[:, 0:1], in0=lt[:, 0:1], in1=lt[:, 1:2], op=ALU.mult), reads=[bst], writes=[bst])
        P.add("dve", lambda e: e.tensor_tensor(out=pr[:, 1:2], in0=lt[:, 2:3], in1=lt[:, 3:4], op=ALU.mult), reads=[bst], writes=[bst])
        P.add("pe", lambda e: e.matmul(ps[:, 6, 0:2], lhsT=self.onesF, rhs=pr, start=True, stop=True),
              reads=[bst, self.b_const], writes=[self.psb[6]])
        P.add("act", lambda e: e.activation(out=pr, in_=ps[:, 6, 0:2], func=AF.Exp), writes=[self.psb[6], bst])
        P.add("dve", lambda e: e.scalar_tensor_tensor(out=M["lam"], in0=pr[:, 0:1], scalar=lam_init, in1=pr[:, 1:2],
                                                      op0=ALU.add, op1=ALU.subtract), reads=[bst], writes=[M["b"]])
        P.add("dve", lambda e: e.tensor_scalar(out=M["nlam"], in0=M["lam"], scalar1=-1.0, scalar2=None, op0=ALU.mult),
              writes=[M["b"]])
        for d_ in range(2):
            P.dma("sp", M["dnw"][:, d_:d_ + 1], D["diff_norm"][l, d_ * 128:(d_ + 1) * 128].rearrange("(p o) -> p o", o=1),
                  writes=[M["b"]], allow_slow_non_contiguous=True)
        P.add("dve", lambda e: e.tensor_scalar(out=M["dnw"], in0=M["dnw"], scalar1=1.0 - lam_init, scalar2=None, op0=ALU.mult),
              writes=[M["b"]])
        P.barrier()
        self.aoff = mark

    def seqs(self):
        c = self.cfg
        out = [(0, c.TL, True)]
        for b_ in range(c.NB):
            out.append((c.TL + b_ * c.S, c.S, False))
        return out

    def na_head(self, l, h):
        c, P, ps, M = self.cfg, self.P, self.ps, self.M
        D, B = self.dram, self.dbuf
        scale = HEAD_DIM ** -0.5
        TLc = c.TL // 128
        PC = c.PAST // 128
        Tmax = c.TL
        o = self.alloc(Tmax // 2); QT = self.bf16v(o, Tmax); bQ = Buf("QT")
        o = self.alloc(Tmax // 2); KT = self.bf16v(o, Tmax); bK = Buf("KT")
        o = self.alloc(Tmax // 2); Vt = self.bf16v(o, Tmax).rearrange("p (k d) -> p k d", d=128); bV = Buf("Vt")
        o = self.alloc(Tmax // 2); ost = self.bf16v(o, Tmax); bO = Buf("ost")
        o = self.alloc(self.nty * 128); bias = self.f32v(o, self.nty * 128).rearrange("p (t q) -> p t q", q=128); bB = Buf("bias")
        pTs, bP = [], []
        for i in range(2):
            o = self.alloc(8 * 128 // 2); pTs.append(self.bf16v(o, 1024)); bP.append(Buf("pT%d" % i))
        o = self.alloc(128); rden = self.f32v(o, 128); bR = Buf("rden")
        for ty0 in range(0, self.nty, 4):
            ty1 = min(self.nty, ty0 + 4)
            P.dma("sp", bias[:, ty0:ty1, :], D["nab"][l, h, ty0:ty1].rearrange("t k q -> k t q"), writes=[bB])
        P.add("act", lambda e: e.mul(out=bias, in_=bias, mul=1.0 / scale), writes=[bB])
        it = 0
        for (t0, T, lat) in self.seqs():
            nqt = T // 128
            r0 = h * 128
            P.dma("sp", QT[:, 0:T], D["QNA"][r0:r0 + 128, t0:t0 + T], reads=[B["QNA"]], writes=[bQ])
            P.dma("sp", KT[:, 0:T], D["KNA"][r0:r0 + 128, t0:t0 + T], reads=[B["KNA"]], writes=[bK])
            for k0 in range(0, nqt, 8):
                k1 = min(nqt, k0 + 8)
                P.dma("sp", Vt[:, k0:k1, :], D["VNA"][t0 + k0 * 128:t0 + k1 * 128, r0:r0 + 128].rearrange("(k p) d -> p k d", p=128),
                      reads=[B["VNA"]], writes=[bV])
            for j in range(nqt):
                if lat:
                    chunks = [("l", m, ty) for (m, ty) in self.plan[j]] + [("c", k, None) for k in range(PC)]
                else:
                    chunks = [("l", k, None) for k in range(nqt)]
                n = len(chunks)
                assert n <= 8
                sb0 = (it % 2) * 2
                S2 = ps[:, sb0:sb0 + 2, :].rearrange("p b f -> p (b f)")
                sbufs = [self.psb[sb0], self.psb[sb0 + 1]]
                q_ap = QT[:, j * 128:(j + 1) * 128]
                for i, (kind, kc, ty) in enumerate(chunks):
                    k_ap = KT[:, kc * 128:(kc + 1) * 128] if kind == "l" else M["KcNA"][:, h, kc * 128:(kc + 1) * 128]
                    P.add("pe", lambda e, i=i, k_ap=k_ap, ty=ty, S2=S2, q_ap=q_ap: e.matmul(
                        S2[:, i * 128:(i + 1) * 128], lhsT=k_ap, rhs=q_ap, start=True, stop=(ty is None)),
                        reads=[bK, bQ, M["b"]], writes=sbufs)
                    if ty is not None:
                        P.add("pe", lambda e, i=i, ty=ty, S2=S2: e.matmul(
                            S2[:, i * 128:(i + 1) * 128], lhsT=self.identF, rhs=bias[:, ty, :], start=False, stop=True),
                            reads=[bB, self.b_const], writes=sbufs)
                pi = it % 2
                pT = pTs[pi]
                P.add("act", lambda e, pT=pT, S2=S2, n=n: e.activation(out=pT[:, 0:n * 128], in_=S2[:, 0:n * 128],
                                                                      func=AF.Exp, scale=scale),
                      writes=sbufs + [bP[pi]])
                ob = 4 + (it % 2)
                for i, (kind, kc, ty) in enumerate(chunks):
                    v_ap = Vt[:, kc, :] if kind == "l" else M["VcNA"][:, kc, h * 128:(h + 1) * 128]
                    P.add("pe", lambda e, i=i, v_ap=v_ap, pT=pT, ob=ob, n=n: e.matmul(
                        ps[:, ob, 0:128], lhsT=v_ap, rhs=pT[:, i * 128:(i + 1) * 128], start=(i == 0), stop=(i == n - 1)),
                        reads=[bV, bP[pi], M["b"]], writes=[self.psb[ob]])
                for i in range(n):
                    P.add("pe", lambda e, i=i, pT=pT, ob=ob, n=n: e.matmul(
                        ps[:, ob, 128:256], lhsT=self.onesB, rhs=pT[:, i * 128:(i + 1) * 128], start=(i == 0), stop=(i == n - 1)),
                        reads=[bP[pi], self.b_const], writes=[self.psb[ob]])
                P.add("dve", lambda e, ob=ob: e.reciprocal(out=rden, in_=ps[:, ob, 128:256]), writes=[self.psb[ob], bR])
                P.add("dve", lambda e, ob=ob, j=j: e.tensor_tensor(out=ost[:, j * 128:(j + 1) * 128], in0=ps[:, ob, 0:128],
                                                                  in1=rden, op=ALU.mult),
                      reads=[bR], writes=[self.psb[ob], bO])
                it += 1
            P.dma("sp", D["ONA"][r0:r0 + 128, t0:t0 + T], ost[:, 0:T], reads=[bO], writes=[B["ONA"]], pool="st")

    def diff_head(self, l, h):
        c, P, ps, M = self.cfg, self.P, self.ps, self.M
        D, B = self.dram, self.dbuf
        scale = HEAD_DIM ** -0.5
        PC = c.PAST // 128
        Tmax = c.TL
        QT, KT, bQ, bK = [], [], Buf("dQT"), Buf("dKT")
        for m in range(2):
            o = self.alloc(Tmax // 2); QT.append(self.bf16v(o, Tmax))
            o = self.alloc(Tmax // 2); KT.append(self.bf16v(o, Tmax))
        o = self.alloc(Tmax); Vt = self.bf16v(o, Tmax * 2).rearrange("p (k d) -> p k d", d=256); bV = Buf("dVt")
        osts, bO = [], Buf("dost")
        for d_ in range(2):
            o = self.alloc(Tmax // 2); osts.append(self.bf16v(o, Tmax))
        NKmax = (c.TL + c.PAST) // 128
        pTs, bP = [], []
        for i in range(2):
            o = self.alloc(NKmax * 256); pTs.append(self.bf16v(o, NKmax * 512).rearrange("p (k q) -> p k q", q=512)); bP.append(Buf("dpT%d" % i))
        o = self.alloc(512); rden = self.f32v(o, 512); bR = Buf("drden")
        od, bod = [], Buf("od")
        for d_ in range(2):
            o = self.alloc(256); od.append(self.f32v(o, 256))
        o = self.alloc(256); t1 = self.f32v(o, 256); bt1 = Buf("dt1")
        sq, bsq = [], []
        for d_ in range(2):
            o = self.alloc(256); sq.append(self.f32v(o, 256)); bsq.append(Buf("dsq%d" % d_))
        o = self.alloc(256); rstd = self.f32v(o, 256); brs = Buf("drstd")
        it = 0
        qit = 0
        for (t0, T, lat) in self.seqs():
            nkl = T // 128
            QB = min(256, T)
            for m in range(2):
                r0 = (h * 2 + m) * 128
                P.dma("sp", QT[m][:, 0:T], D["DQ"][r0:r0 + 128, t0:t0 + T], reads=[B["DQ"]], writes=[bQ])
                P.dma("sp", KT[m][:, 0:T], D["DK"][r0:r0 + 128, t0:t0 + T], reads=[B["DK"]], writes=[bK])
            for k0 in range(0, nkl, 8):
                k1 = min(nkl, k0 + 8)
                P.dma("sp", Vt[:, k0:k1, :], D["DV"][t0 + k0 * 128:t0 + k1 * 128, h * 256:(h + 1) * 256].rearrange("(k p) d -> p k d", p=128),
                      reads=[B["DV"]], writes=[bV])
            chunks = [("l", k) for k in range(nkl)] + ([("c", k) for k in range(PC)] if lat else [])
            n = len(chunks)
            for qb in range(T // QB):
                acc = (2, 3, 4) if qit % 2 == 0 else (5, 6, 7)
                accb = [self.psb[a] for a in acc]
                pi = qit % 2
                pT = pTs[pi]
                for i, (kind, kc) in enumerate(chunks):
                    sbk = it % 2
                    for m in range(2):
                        k_ap = KT[m][:, kc * 128:(kc + 1) * 128] if kind == "l" else M["KcD"][:, h * 2 + m, kc * 128:(kc + 1) * 128]
                        P.add("pe", lambda e, m=m, k_ap=k_ap, sbk=sbk, qb=qb, QB=QB: e.matmul(
                            ps[:, sbk, m * QB:(m + 1) * QB], lhsT=k_ap, rhs=QT[m][:, qb * QB:(qb + 1) * QB], start=True, stop=True),
                            reads=[bK, bQ, M["b"]], writes=[self.psb[sbk]])
                    P.add("act", lambda e, pT=pT, sbk=sbk, QB=QB, i=i: e.activation(out=pT[:, i, 0:2 * QB], in_=ps[:, sbk, 0:2 * QB],
                                                                                   func=AF.Exp, scale=scale),
                          writes=[self.psb[sbk], bP[pi]])
                    it += 1
                for m in range(2):
                    for d_ in range(2):
                        for i, (kind, kc) in enumerate(chunks):
                            v_ap = Vt[:, kc, d_ * 128:(d_ + 1) * 128] if kind == "l" else M["VcD"][:, kc, h * 256 + d_ * 128:h * 256 + (d_ + 1) * 128]
                            P.add("pe", lambda e, m=m, d_=d_, v_ap=v_ap, pT=pT, i=i, acc=acc, QB=QB, n=n: e.matmul(
                                ps[:, acc[d_], m * QB:(m + 1) * QB], lhsT=v_ap, rhs=pT[:, i, m * QB:(m + 1) * QB],
                                start=(i == 0), stop=(i == n - 1)),
                                reads=[bV, bP[pi], M["b"]], writes=[accb[d_]])
                    for i in range(n):
                        P.add("pe", lambda e, m=m, pT=pT, i=i, acc=acc, QB=QB, n=n: e.matmul(
                            ps[:, acc[2], m * QB:(m + 1) * QB], lhsT=self.onesB, rhs=pT[:, i, m * QB:(m + 1) * QB],
                            start=(i == 0), stop=(i == n - 1)),
                            reads=[bP[pi], self.b_const], writes=[accb[2]])
                P.add("dve", lambda e, acc=acc, QB=QB: e.reciprocal(out=rden[:, 0:2 * QB], in_=ps[:, acc[2], 0:2 * QB]),
                      writes=[accb[2], bR])
                for d_ in range(2):
                    P.add("dve", lambda e, d_=d_, acc=acc, QB=QB: e.tensor_tensor(out=od[d_][:, 0:QB], in0=ps[:, acc[d_], 0:QB],
                                                                                 in1=rden[:, 0:QB], op=ALU.mult),
                          reads=[bR], writes=[accb[d_], bod])
                    P.add("dve", lambda e, d_=d_, acc=acc, QB=QB: e.tensor_tensor(out=t1[:, 0:QB], in0=ps[:, acc[d_], QB:2 * QB],
                                                                                 in1=rden[:, QB:2 * QB], op=ALU.mult),
                          reads=[bR], writes=[accb[d_], bt1])
                    P.add("dve", lambda e, d_=d_, QB=QB: e.scalar_tensor_tensor(out=od[d_][:, 0:QB], in0=t1[:, 0:QB], scalar=M["nlam"][:, 0:1],
                                                                               in1=od[d_][:, 0:QB], op0=ALU.mult, op1=ALU.add),
                          reads=[bt1, M["b"]], writes=[bod])
                    P.add("act", lambda e, d_=d_, QB=QB: e.activation(out=sq[d_][:, 0:QB], in_=od[d_][:, 0:QB], func=AF.Square),
                          reads=[bod], writes=[bsq[d_]])
                nb = it % 2
                for d_ in range(2):
                    P.add("pe", lambda e, d_=d_, nb=nb, QB=QB: e.matmul(ps[:, nb, 0:QB], lhsT=self.onesF, rhs=sq[d_][:, 0:QB],
                                                                       start=(d_ == 0), stop=(d_ == 1)),
                          reads=[bsq[d_], self.b_const], writes=[self.psb[nb]])
                it += 1
                P.add("act", lambda e, nb=nb, QB=QB: e.activation(out=rstd[:, 0:QB], in_=ps[:, nb, 0:QB], func=AF.Ln, bias=self.epsT,
                                                                 scale=1.0 / 256.0), reads=[self.b_const], writes=[self.psb[nb], brs])
                P.add("act", lambda e, QB=QB: e.activation(out=rstd[:, 0:QB], in_=rstd[:, 0:QB], func=AF.Exp, scale=-0.5), writes=[brs])
                for d_ in range(2):
                    P.add("dve", lambda e, d_=d_, qb=qb, QB=QB: e.scalar_tensor_tensor(
                        out=osts[d_][:, qb * QB:(qb + 1) * QB], in0=od[d_][:, 0:QB], scalar=M["dnw"][:, d_:d_ + 1], in1=rstd[:, 0:QB],
                        op0=ALU.mult, op1=ALU.mult), reads=[bod, brs, M["b"]], writes=[bO])
                qit += 1
            for d_ in range(2):
                r0 = h * 256 + d_ * 128
                P.dma("sp", D["OD"][r0:r0 + 128, t0:t0 + T], osts[d_][:, 0:T], reads=[bO], writes=[B["OD"]], pool="st")

    def gdn(self, l):
        self.gdn_prep(l)
        self.P.barrier()
        self.aoff = self.gdn_mark
        self.gdn_scan(l)

    def bank1(self):
        b = self.g_b1 % 8
        self.g_b1 += 1
        return b

    def bank2(self):
        b = (self.g_b2 % 4) * 2
        self.g_b2 += 1
        return b

    def gdn_prep(self, l):
        c, P, ps = self.cfg, self.P, self.ps
        D, B = self.dram, self.dbuf
        self.g_b1, self.g_b2 = 0, 0
        self.gdn_mark = self.aoff
        G = {}
        self.G = G
        G["b"] = Buf("gconst")
        o = self.alloc(72); cw = self.f32v(o, 72).rearrange("p (j k) -> p j k", j=3)
        for j in range(3):
            P.dma("sp", cw[:, j, :], D["gdn_conv"][l, j].rearrange("(k p) -> p k", p=128), writes=[G["b"]],
                  allow_slow_non_contiguous=True)
        o = self.alloc(16); dtb = self.f32v(o, 16)
        o = self.alloc(16); nea = self.f32v(o, 16)
        o = self.alloc(128); G["gnw"] = self.f32v(o, 128)
        P.dma("sp", dtb, D["gdn_dt_bias"][l:l + 1].rearrange("o d h -> o (d h)").partition_broadcast(128), writes=[G["b"]])
        P.dma("sp", nea, D["gdn_a_log"][l:l + 1].rearrange("o d h -> o (d h)").partition_broadcast(128), writes=[G["b"]])
        P.dma("sp", G["gnw"], D["gdn_norm"][l:l + 1, :].partition_broadcast(128), writes=[G["b"]])
        P.add("act", lambda e: e.activation(out=nea, in_=nea, func=AF.Exp), writes=[G["b"]])
        P.add("dve", lambda e: e.tensor_scalar(out=nea, in0=nea, scalar1=-1.0, scalar2=None, op0=ALU.mult), writes=[G["b"]])
        o = self.alloc(8); oneT = self.f32v(o, 1)
        P.add("dve", lambda e: e.memset(oneT, 1.0), writes=[G["b"]])
        self.gdn_mark = self.aoff
        NT = 512
        xh, bxh = [], []
        for i in range(2):
            o = self.alloc(NT + 8); xh.append(self.f32v(o, NT + 2)); bxh.append(Buf("xh%d" % i))
        o = self.alloc(NT); t = self.f32v(o, NT); bt = Buf("gt")
        ys, bys = [], []
        for i in range(2):
            o = self.alloc(NT); ys.append(self.f32v(o, NT)); bys.append(Buf("ys%d" % i))
        o = self.alloc(NT); sq = self.f32v(o, NT); bsq = Buf("gsq")
        o = self.alloc(NT); rinv = self.f32v(o, NT); bri = Buf("grinv")
        yn, byn = [], []
        for i in range(2):
            o = self.alloc(NT); yn.append(self.f32v(o, NT)); byn.append(Buf("yn%d" % i))
        tst, btst = [], []
        for i in range(2):
            o = self.alloc(128); tst.append(self.f32v(o, 128)); btst.append(Buf("tst%d" % i))
        ab, bab = [], []
        for i in range(2):
            o = self.alloc(32); ab.append(self.f32v(o, 32)); bab.append(Buf("ab%d" % i))
        o = self.alloc(16); xe = self.f32v(o, 16); bxe = Buf("xe")
        it = 0
        for (t0, T, lat) in self.seqs():
            for b0 in range(0, T, NT):
                nt = min(NT, T - b0)
                for fc in range(24):
                    kind = fc // 8
                    src = ("GQ", "GK", "GV")[kind]
                    r0 = (fc % 8) * 128
                    xi = it % 2
                    x_ = xh[xi]
                    lo = max(0, b0 - 1)
                    hi = min(T, b0 + nt + 1)
                    off = lo - (b0 - 1)
                    if b0 == 0:
                        P.add("pool", lambda e, x_=x_: e.memset(x_[:, 0:1], 0.0), writes=[bxh[xi]])
                    if b0 + nt == T:
                        P.add("pool", lambda e, x_=x_, nt=nt: e.memset(x_[:, nt + 1:nt + 2], 0.0), writes=[bxh[xi]])
                    P.dma("sp", x_[:, off:off + hi - lo], D[src][r0:r0 + 128, t0 + lo:t0 + hi], reads=[B[src]], writes=[bxh[xi]])
                    P.add("dve", lambda e, x_=x_, nt=nt, fc=fc: e.tensor_scalar(out=t[:, 0:nt], in0=x_[:, 0:nt], scalar1=cw[:, 0, fc:fc + 1],
                                                                                 scalar2=None, op0=ALU.mult),
                          reads=[bxh[xi], G["b"]], writes=[bt])
                    P.add("dve", lambda e, x_=x_, nt=nt, fc=fc: e.scalar_tensor_tensor(out=t[:, 0:nt], in0=x_[:, 1:nt + 1], scalar=cw[:, 1, fc:fc + 1],
                                                                                        in1=t[:, 0:nt], op0=ALU.mult, op1=ALU.add),
                          reads=[bxh[xi], G["b"]], writes=[bt])
                    P.add("dve", lambda e, x_=x_, nt=nt, fc=fc: e.scalar_tensor_tensor(out=t[:, 0:nt], in0=x_[:, 2:nt + 2], scalar=cw[:, 2, fc:fc + 1],
                                                                                        in1=t[:, 0:nt], op0=ALU.mult, op1=ALU.add),
                          reads=[bxh[xi], G["b"]], writes=[bt])
                    y_ = ys[xi]
                    P.add("act", lambda e, y_=y_, nt=nt: e.activation(out=y_[:, 0:nt], in_=t[:, 0:nt], func=AF.Silu),
                          reads=[bt], writes=[bys[xi]])
                    fin, bfin = y_, bys[xi]
                    if kind < 2:
                        P.add("act", lambda e, y_=y_, nt=nt: e.activation(out=sq[:, 0:nt], in_=y_[:, 0:nt], func=AF.Square),
                              reads=[bys[xi]], writes=[bsq])
                        bk = self.bank1()
                        P.add("pe", lambda e, bk=bk, nt=nt: e.matmul(ps[:, bk, 0:nt], lhsT=self.onesF, rhs=sq[:, 0:nt], start=True, stop=True),
                              reads=[bsq, self.b_const], writes=[self.psb[bk]])
                        P.add("act", lambda e, bk=bk, nt=nt: e.activation(out=rinv[:, 0:nt], in_=ps[:, bk, 0:nt], func=AF.Ln, bias=self.epsT, scale=1.0),
                              reads=[self.b_const], writes=[self.psb[bk], bri])
                        P.add("act", lambda e, nt=nt: e.activation(out=rinv[:, 0:nt], in_=rinv[:, 0:nt], func=AF.Exp, scale=-0.5), writes=[bri])
                        n_ = yn[xi]
                        sc = HEAD_DIM ** -0.5 if kind == 0 else 1.0
                        P.add("dve", lambda e, n_=n_, y_=y_, nt=nt, sc=sc: e.scalar_tensor_tensor(out=n_[:, 0:nt], in0=y_[:, 0:nt], scalar=sc, in1=rinv[:, 0:nt],
                                                                                                op0=ALU.mult, op1=ALU.mult),
                              reads=[bys[xi], bri], writes=[byn[xi]])
                        fin, bfin = n_, byn[xi]
                        dst = "GQN" if kind == 0 else "GKN"
                        P.dma("sp", D[dst][r0:r0 + 128, t0 + b0:t0 + b0 + nt], fin[:, 0:nt], reads=[bfin], writes=[B[dst]], pool="st")
                    if kind >= 1:
                        dstT = "GKT" if kind == 1 else "GVT"
                        for tt in range(nt // 128 if nt >= 128 else 1):
                            w_ = min(128, nt)
                            bk = self.bank1()
                            P.add("pe", lambda e, bk=bk, fin=fin, tt=tt, w_=w_: e.transpose(out=ps[0:w_, bk, 0:128], in_=fin[:, tt * 128:tt * 128 + w_],
                                                                                             identity=self.identF),
                                  reads=[bfin, self.b_const], writes=[self.psb[bk]])
                            si = (it + tt) % 2
                            P.add("act", lambda e, bk=bk, si=si, w_=w_: e.copy(out=tst[si][0:w_, :], in_=ps[0:w_, bk, 0:128]),
                                  writes=[self.psb[bk], btst[si]])
                            tok = t0 + b0 + tt * 128
                            P.dma("sp", D[dstT][tok:tok + w_, r0:r0 + 128], tst[si][0:w_, :], reads=[btst[si]], writes=[B[dstT]], pool="st")
                    it += 1
                for tt in range(max(1, nt // 128)):
                    w_ = min(128, nt)
                    tok = t0 + b0 + tt * 128
                    ai = (it + tt) % 2
                    a_ = ab[ai]
                    P.dma("sp", a_[0:w_, :], D["GAB"][tok:tok + w_, :], reads=[B["GAB"]], writes=[bab[ai]])
                    P.add("act", lambda e, a_=a_, w_=w_: e.activation(out=a_[0:w_, 0:16], in_=a_[0:w_, 0:16], func=AF.Sigmoid), writes=[bab[ai]])
                    P.add("dve", lambda e, a_=a_, w_=w_: e.tensor_tensor(out=xe[0:w_, :], in0=a_[0:w_, 16:32], in1=dtb[0:w_, :], op=ALU.add),
                          reads=[bab[ai], G["b"]], writes=[bxe])
                    P.add("act", lambda e, w_=w_: e.activation(out=xe[0:w_, :], in_=xe[0:w_, :], func=AF.Exp), writes=[bxe])
                    P.add("act", lambda e, w_=w_: e.activation(out=xe[0:w_, :], in_=xe[0:w_, :], func=AF.Ln, bias=oneT[0:w_, :], scale=1.0),
                          reads=[G["b"]], writes=[bxe])
                    P.add("dve", lambda e, a_=a_, w_=w_: e.tensor_tensor(out=a_[0:w_, 16:32], in0=xe[0:w_, :], in1=nea[0:w_, :], op=ALU.mult),
                          reads=[bxe, G["b"]], writes=[bab[ai]])
                    P.dma("sp", D["GAB2"][tok:tok + w_, :], a_[0:w_, :], reads=[bab[ai]], writes=[B["GAB2"]], pool="st")
                it += 1

    def gdn_scan(self, l):
        c, P, ps = self.cfg, self.P, self.ps
        D, B, G = self.dram, self.dbuf, self.G
        C = GDN_CHUNK
        H = GDN_HEADS

        def f3(n, inner):
            o = self.alloc(n)
            return self.f32v(o, n).rearrange("p (h x) -> p h x", x=inner)

        def f2(n):
            return self.f32v(self.alloc(n), n)

        o = self.alloc(6 * 64); msk = self.f32v(o, 384).rearrange("p (m j) -> p m j", m=6)
        o = self.alloc(2 * 64); tri = self.f32v(o, 128).rearrange("p (m j) -> p m j", m=2)
        P.dma("sp", msk[0:64], D["k_masks"].rearrange("m i j -> i m j"), writes=[G["b"]])
        P.dma("sp", tri[0:64], D["k_tri"].rearrange("m i j -> i m j"), writes=[G["b"]])
        id64 = self.identF[0:64, 0:64]

        def bc_h(ap2):
            return ap2.unsqueeze(1).to_broadcast([ap2.shape[0], H, ap2.shape[1]])

        def bc_x(ap2, n):
            return ap2.unsqueeze(2).to_broadcast([ap2.shape[0], H, n])

        LD = []
        for i in range(2):
            d = {"Kf": f3(512, 64), "Qf": f3(512, 64), "Kt": f3(1024, 128), "Vt": f3(1024, 128), "ab": f2(32), "b": Buf("gld%d" % i)}
            LD.append(d)
        T_ = {k: f3(512, 64) for k in ("R", "Rb", "D", "E", "ET", "EM", "ETM", "ETI", "X", "XT", "Y0", "Y1", "YT0", "YT1", "nBb")}
        T_["gam"] = f2(8); T_["tot"] = f2(8); T_["kdsc"] = f2(8); T_["bes"] = f2(8); T_["nbeta"] = f2(8)
        T_["bv"] = f3(1024, 128); T_["Rk"] = f3(1024, 128)
        Tb = {k: Buf("g_" + k) for k in T_}
        OUT = []
        for i in range(2):
            d = {"TT": f3(512, 64), "pT": f3(512, 64), "u": f3(1024, 128), "wkT": f3(512, 64), "kd": f3(1024, 128),
                 "egam": f2(8), "gl": f2(8)}
            d["b"] = {k: Buf("go%d_%s" % (i, k)) for k in d}
            OUT.append(d)
        S_ = f3(1024, 128); bS = Buf("gS")
        w_ = f3(1024, 128); bw = Buf("gw")
        zs = f3(1024, 128); bzs = Buf("gzs")
        ot = f3(1024, 128); bot = Buf("got")
        of_ = f3(1024, 128); bof = Buf("gof")
        gz = f3(1024, 128); bgz = Buf("ggz")
        tmp = f3(1024, 128); btmp = Buf("gtmp")
        ssq = f2(8); bssq = Buf("gssq")
        o = self.alloc(256); ogT = self.bf16v(o, 512).rearrange("p (h t) -> p h t", t=64); bogT = Buf("gogT")

        def psv(bank, parts, n, inner):
            return ps[0:parts, bank, 0:n].rearrange("p (h x) -> p h x", x=inner)

        def ps2v(bank, parts, inner):
            return ps[0:parts, bank:bank + 2, :].rearrange("p b f -> p (b f)").rearrange("p (h x) -> p h x", x=inner)

        def prep(t0c, dr, li, oi):
            L_, O_ = LD[li], OUT[oi]
            ob = O_["b"]
            P.dma("sp", L_["Kf"], D["GKN"][:, t0c:t0c + C].rearrange("(h p) t -> p h t", p=128), reads=[B["GKN"]], writes=[L_["b"]])
            P.dma("sp", L_["Qf"], D["GQN"][:, t0c:t0c + C].rearrange("(h p) t -> p h t", p=128), reads=[B["GQN"]], writes=[L_["b"]])
            P.dma("sp", L_["Kt"][0:C], D["GKT"][t0c:t0c + C, :].rearrange("t (h d) -> t h d", d=128), reads=[B["GKT"]], writes=[L_["b"]])
            P.dma("sp", L_["Vt"][0:C], D["GVT"][t0c:t0c + C, :].rearrange("t (h d) -> t h d", d=128), reads=[B["GVT"]], writes=[L_["b"]])
            P.dma("sp", L_["ab"][0:C], D["GAB2"][t0c:t0c + C, :], reads=[B["GAB2"]], writes=[L_["b"]])
            beta = L_["ab"][0:C, dr * 8:dr * 8 + 8]
            la = L_["ab"][0:C, 16 + dr * 8:16 + dr * 8 + 8]
            gam, tot, kdsc, bes, nbeta = (T_[k][0:C] for k in ("gam", "tot", "kdsc", "bes", "nbeta"))
            b0 = self.bank1()
            P.add("pe", lambda e: e.matmul(ps[0:C, b0, 0:8], lhsT=tri[0:C, dr, :], rhs=la, start=True, stop=True),
                  reads=[L_["b"], G["b"]], writes=[self.psb[b0]])
            P.add("pe", lambda e: e.matmul(ps[:, b0, 8:16], lhsT=self.onesF[0:C, :], rhs=la, start=True, stop=True),
                  reads=[L_["b"], self.b_const], writes=[self.psb[b0]])
            P.add("act", lambda e: e.copy(out=gam, in_=ps[0:C, b0, 0:8]), writes=[self.psb[b0], Tb["gam"]])
            P.add("act", lambda e: e.activation(out=O_["egam"][0:C], in_=ps[0:C, b0, 0:8], func=AF.Exp), writes=[self.psb[b0], ob["egam"]])
            P.add("act", lambda e: e.activation(out=O_["gl"], in_=ps[:, b0, 8:16], func=AF.Exp), writes=[self.psb[b0], ob["gl"]])
            P.add("act", lambda e: e.copy(out=tot, in_=ps[0:C, b0, 8:16]), writes=[self.psb[b0], Tb["tot"]])
            P.add("dve", lambda e: e.tensor_tensor(out=kdsc, in0=tot, in1=gam, op=ALU.subtract), reads=[Tb["tot"], Tb["gam"]], writes=[Tb["kdsc"]])
            P.add("act", lambda e: e.activation(out=kdsc, in_=kdsc, func=AF.Exp), writes=[Tb["kdsc"]])
            P.add("dve", lambda e: e.tensor_tensor(out=bes, in0=beta, in1=O_["egam"][0:C], op=ALU.mult), reads=[L_["b"], ob["egam"]], writes=[Tb["bes"]])
            P.add("dve", lambda e: e.tensor_scalar(out=nbeta, in0=beta, scalar1=-1.0, scalar2=None, op0=ALU.mult), reads=[L_["b"]], writes=[Tb["nbeta"]])
            R, Rb = T_["R"][0:C], T_["Rb"][0:C]
            P.add("dve", lambda e: e.tensor_tensor(out=R, in0=bc_h(id64), in1=bc_x(gam, C), op=ALU.mult), reads=[Tb["gam"], self.b_const], writes=[Tb["R"]])
            P.add("pool", lambda e: e.tensor_tensor(out=Rb, in0=bc_h(id64), in1=bc_x(nbeta, C), op=ALU.mult), reads=[Tb["nbeta"], self.b_const], writes=[Tb["Rb"]])
            bG, bBb = self.bank1(), self.bank1()
            P.add("pe", lambda e: e.matmul(ps[0:C, bG, :], lhsT=self.onesF[0:C, 0:C], rhs=R.rearrange("p h x -> p (h x)"), start=True, stop=True),
                  reads=[Tb["R"], self.b_const], writes=[self.psb[bG]])
            P.add("pe", lambda e: e.matmul(ps[0:C, bBb, :], lhsT=self.onesF[0:C, 0:C], rhs=Rb.rearrange("p h x -> p (h x)"), start=True, stop=True),
                  reads=[Tb["Rb"], self.b_const], writes=[self.psb[bBb]])
            Dm, E, ET, EM, ETM, ETI, nBb = (T_[k][0:C] for k in ("D", "E", "ET", "EM", "ETM", "ETI", "nBb"))
            Gb = psv(bG, C, 512, 64)
            P.add("dve", lambda e: e.tensor_tensor(out=Dm, in0=bc_x(gam, C), in1=Gb, op=ALU.subtract), reads=[Tb["gam"]], writes=[self.psb[bG], Tb["D"]])
            P.add("dve", lambda e: e.tensor_scalar(out=Dm, in0=Dm, scalar1=0.0, scalar2=None, op0=ALU.min), writes=[Tb["D"]])
            P.add("act", lambda e: e.activation(out=E, in_=Dm, func=AF.Exp), reads=[Tb["D"]], writes=[Tb["E"]])
            P.add("dve", lambda e: e.tensor_tensor(out=ET, in0=Gb, in1=bc_x(gam, C), op=ALU.subtract), reads=[Tb["gam"]], writes=[self.psb[bG], Tb["ET"]])
            P.add("dve", lambda e: e.tensor_scalar(out=ET, in0=ET, scalar1=0.0, scalar2=None, op0=ALU.min), writes=[Tb["ET"]])
            P.add("act", lambda e: e.activation(out=ET, in_=ET, func=AF.Exp), writes=[Tb["ET"]])
            P.add("act", lambda e: e.copy(out=nBb, in_=psv(bBb, C, 512, 64)), writes=[self.psb[bBb], Tb["nBb"]])
            m0 = dr * 3
            P.add("pool", lambda e: e.tensor_tensor(out=EM, in0=E, in1=bc_h(msk[0:C, m0, :]), op=ALU.mult), reads=[Tb["E"], G["b"]], writes=[Tb["EM"]])
            P.add("pool", lambda e: e.tensor_tensor(out=ETM, in0=ET, in1=bc_h(msk[0:C, m0 + 1, :]), op=ALU.mult), reads=[Tb["ET"], G["b"]], writes=[Tb["ETM"]])
            P.add("pool", lambda e: e.tensor_tensor(out=ETI, in0=ET, in1=bc_h(msk[0:C, m0 + 2, :]), op=ALU.mult), reads=[Tb["ET"], G["b"]], writes=[Tb["ETI"]])
            bkk, bqk = self.bank1(), self.bank1()
            for h in range(H):
                P.add("pe", lambda e, h=h: e.matmul(ps[0:C, bkk, h * 64:(h + 1) * 64], lhsT=L_["Kf"][:, h, :], rhs=L_["Kf"][:, h, :], start=True, stop=True),
                      reads=[L_["b"]], writes=[self.psb[bkk]])
            for h in range(H):
                P.add("pe", lambda e, h=h: e.matmul(ps[0:C, bqk, h * 64:(h + 1) * 64], lhsT=L_["Kf"][:, h, :], rhs=L_["Qf"][:, h, :], start=True, stop=True),
                      reads=[L_["b"]], writes=[self.psb[bqk]])
            X, XT = T_["X"][0:C], T_["XT"][0:C]
            kkv, qkv = psv(bkk, C, 512, 64), psv(bqk, C, 512, 64)
            P.add("dve", lambda e: e.tensor_tensor(out=X, in0=EM, in1=kkv, op=ALU.mult), reads=[Tb["EM"]], writes=[self.psb[bkk], Tb["X"]])
            P.add("dve", lambda e: e.tensor_tensor(out=X, in0=X, in1=bc_x(nbeta, C), op=ALU.mult), reads=[Tb["nbeta"]], writes=[Tb["X"]])
            P.add("dve", lambda e: e.tensor_tensor(out=XT, in0=ETM, in1=kkv, op=ALU.mult), reads=[Tb["ETM"]], writes=[self.psb[bkk], Tb["XT"]])
            P.add("dve", lambda e: e.tensor_tensor(out=XT, in0=XT, in1=nBb, op=ALU.mult), reads=[Tb["nBb"]], writes=[Tb["XT"]])
            pT = O_["pT"][0:C]
            P.add("dve", lambda e: e.tensor_tensor(out=pT, in0=ETI, in1=qkv, op=ALU.mult), reads=[Tb["ETI"]], writes=[self.psb[bqk], ob["pT"]])
            TT = O_["TT"][0:C]
            P.add("pool", lambda e: e.tensor_tensor(out=TT, in0=XT, in1=bc_h(id64), op=ALU.add), reads=[Tb["XT"], self.b_const], writes=[ob["TT"]])
            Y, YT, bY, bYT = X, XT, Tb["X"], Tb["XT"]
            for k in range(1, 6):
                Yn, bYn = T_["Y%d" % (k % 2)][0:C], Tb["Y%d" % (k % 2)]
                YTn, bYTn = T_["YT%d" % (k % 2)][0:C], Tb["YT%d" % (k % 2)]
                ba = self.bank1()
                for h in range(H):
                    P.add("pe", lambda e, h=h, ba=ba, Y=Y, YT=YT: e.matmul(ps[0:C, ba, h * 64:(h + 1) * 64], lhsT=YT[:, h, :], rhs=Y[:, h, :], start=True, stop=True),
                          reads=[bY, bYT], writes=[self.psb[ba]])
                if k < 5:
                    bb_ = self.bank1()
                    for h in range(H):
                        P.add("pe", lambda e, h=h, bb_=bb_, Y=Y, YT=YT: e.matmul(ps[0:C, bb_, h * 64:(h + 1) * 64], lhsT=Y[:, h, :], rhs=YT[:, h, :], start=True, stop=True),
                              reads=[bY, bYT], writes=[self.psb[bb_]])
                P.add("act", lambda e, Yn=Yn, ba=ba: e.copy(out=Yn, in_=psv(ba, C, 512, 64)), writes=[self.psb[ba], bYn])
                if k < 5:
                    P.add("act", lambda e, YTn=YTn, bb_=bb_: e.copy(out=YTn, in_=psv(bb_, C, 512, 64)), writes=[self.psb[bb_], bYTn])
                bc_ = self.bank1()
                for h in range(H):
                    P.add("pe", lambda e, h=h, bc_=bc_, Yn=Yn: e.matmul(ps[0:C, bc_, h * 64:(h + 1) * 64], lhsT=Yn[:, h, :], rhs=TT[:, h, :], start=True, stop=True),
                          reads=[bYn, ob["TT"]], writes=[self.psb[bc_]])
                P.add("dve", lambda e, bc_=bc_: e.tensor_tensor(out=TT, in0=TT, in1=psv(bc_, C, 512, 64), op=ALU.add), writes=[self.psb[bc_], ob["TT"]])
                Y, YT, bY, bYT = Yn, YTn, bYn, bYTn
            bv, Rk, kd = T_["bv"][0:C], T_["Rk"][0:C], O_["kd"][0:C]
            P.add("pool", lambda e: e.tensor_tensor(out=bv, in0=L_["Vt"][0:C], in1=bc_x(beta, 128), op=ALU.mult), reads=[L_["b"]], writes=[Tb["bv"]])
            P.add("pool", lambda e: e.tensor_tensor(out=Rk, in0=L_["Kt"][0:C], in1=bc_x(bes, 128), op=ALU.mult), reads=[L_["b"], Tb["bes"]], writes=[Tb["Rk"]])
            P.add("pool", lambda e: e.tensor_tensor(out=kd, in0=L_["Kt"][0:C], in1=bc_x(kdsc, 128), op=ALU.mult), reads=[L_["b"], Tb["kdsc"]], writes=[ob["kd"]])
            bu = self.bank2()
            uv = ps2v(bu, C, 128)
            for h in range(H):
                P.add("pe", lambda e, h=h: e.matmul(uv[:, h, :], lhsT=TT[:, h, :], rhs=bv[:, h, :], start=True, stop=True),
                      reads=[ob["TT"], Tb["bv"]], writes=[self.psb[bu], self.psb[bu + 1]])
            P.add("act", lambda e: e.copy(out=O_["u"][0:C], in_=uv), writes=[self.psb[bu], self.psb[bu + 1], ob["u"]])
            bwk = self.bank1()
            for h in range(H):
                P.add("pe", lambda e, h=h: e.matmul(ps[:, bwk, h * 64:(h + 1) * 64], lhsT=Rk[:, h, :], rhs=TT[:, h, :], start=True, stop=True),
                      reads=[ob["TT"], Tb["Rk"]], writes=[self.psb[bwk]])
            P.add("act", lambda e: e.copy(out=O_["wkT"], in_=psv(bwk, 128, 512, 64)), writes=[self.psb[bwk], ob["wkT"]])

        def scan(t0c, dr, li, oi, final):
            L_, O_ = LD[li], OUT[oi]
            ob = O_["b"]
            b1 = self.bank2()
            wkS = ps2v(b1, C, 128)
            for h in range(H):
                P.add("pe", lambda e, h=h: e.matmul(wkS[:, h, :], lhsT=O_["wkT"][:, h, :], rhs=S_[:, h, :], start=True, stop=True),
                      reads=[ob["wkT"], bS], writes=[self.psb[b1], self.psb[b1 + 1]])
            P.add("dve", lambda e: e.tensor_tensor(out=w_[0:C], in0=O_["u"][0:C], in1=wkS, op=ALU.subtract),
                  reads=[ob["u"]], writes=[self.psb[b1], self.psb[b1 + 1], bw])
            b2 = self.bank2()
            zv = ps2v(b2, C, 128)
            for h in range(H):
                P.add("pe", lambda e, h=h: e.matmul(zv[:, h, :], lhsT=L_["Qf"][:, h, :], rhs=S_[:, h, :], start=True, stop=True),
                      reads=[L_["b"], bS], writes=[self.psb[b2], self.psb[b2 + 1]])
            P.add("dve", lambda e: e.tensor_tensor(out=zs[0:C], in0=zv, in1=bc_x(O_["egam"][0:C], 128), op=ALU.mult),
                  reads=[ob["egam"]], writes=[self.psb[b2], self.psb[b2 + 1], bzs])
            b3 = self.bank2()
            pwv = ps2v(b3, C, 128)
            for h in range(H):
                P.add("pe", lambda e, h=h: e.matmul(pwv[:, h, :], lhsT=O_["pT"][0:C, h, :], rhs=w_[0:C, h, :], start=True, stop=True),
                      reads=[ob["pT"], bw], writes=[self.psb[b3], self.psb[b3 + 1]])
            P.add("dve", lambda e: e.tensor_tensor(out=ot[0:C], in0=zs[0:C], in1=pwv, op=ALU.add),
                  reads=[bzs], writes=[self.psb[b3], self.psb[b3 + 1], bot])
            b4 = self.bank2()
            sup = ps2v(b4, 128, 128)
            for h in range(H):
                P.add("pe", lambda e, h=h: e.matmul(sup[:, h, :], lhsT=O_["kd"][0:C, h, :], rhs=w_[0:C, h, :], start=True, stop=True),
                      reads=[ob["kd"], bw], writes=[self.psb[b4], self.psb[b4 + 1]])
            P.add("pool", lambda e: e.tensor_tensor(out=S_, in0=S_, in1=bc_x(O_["gl"], 128), op=ALU.mult), reads=[ob["gl"]], writes=[bS])
            P.add("dve", lambda e: e.tensor_tensor(out=S_, in0=S_, in1=sup, op=ALU.add), writes=[self.psb[b4], self.psb[b4 + 1], bS])
            ofl = "(h d)"
            if dr == 0:
                P.dma("sp", D["GOF"][t0c:t0c + C, :].rearrange("t (h d) -> t h d", d=128), ot[0:C], reads=[bot], writes=[B["GOF"]], pool="st")
            else:
                P.dma("sp", of_[0:C], D["GOF"][t0c:t0c + C, :].rearrange("t (h d) -> t h d", d=128), reads=[B["GOF"]], writes=[bof])
                P.dma("sp", gz[0:C], D["GZT"][t0c:t0c + C, :].rearrange("t (h d) -> t h d", d=128), reads=[B["GZT"]], writes=[bgz])
                P.add("dve", lambda e: e.tensor_tensor(out=ot[0:C], in0=ot[0:C], in1=of_[0:C], op=ALU.add), reads=[bof], writes=[bot])
                P.add("pool", lambda e: e.tensor_tensor(out=tmp[0:C], in0=ot[0:C], in1=ot[0:C], op=ALU.mult), reads=[bot], writes=[btmp])
                P.add("dve", lambda e: e.tensor_reduce(out=ssq[0:C], in_=tmp[0:C], axis=AX.X, op=ALU.add), reads=[btmp], writes=[bssq])
                P.add("act", lambda e: e.activation(out=ssq[0:C], in_=ssq[0:C], func=AF.Ln, bias=self.epsT[0:C], scale=1.0 / 128.0),
                      reads=[self.b_const], writes=[bssq])
                P.add("act", lambda e: e.activation(out=ssq[0:C], in_=ssq[0:C], func=AF.Exp, scale=-0.5), writes=[bssq])
                P.add("dve", lambda e: e.tensor_tensor(out=ot[0:C], in0=ot[0:C], in1=bc_x(ssq[0:C], 128), op=ALU.mult), reads=[bssq], writes=[bot])
                P.add("pool", lambda e: e.tensor_tensor(out=gz[0:C], in0=gz[0:C], in1=G["gnw"][0:C].unsqueeze(1).to_broadcast([C, H, 128]), op=ALU.mult),
                      reads=[G["b"]], writes=[bgz])
                P.add("dve", lambda e: e.tensor_tensor(out=ot[0:C], in0=ot[0:C], in1=gz[0:C], op=ALU.mult), reads=[bgz], writes=[bot])
                b5 = self.bank1()
                for h in range(H):
                    P.add("pe", lambda e, h=h: e.transpose(out=ps[:, b5, h * 64:(h + 1) * 64], in_=ot[0:C, h, :], identity=id64),
                          reads=[bot, self.b_const], writes=[self.psb[b5]])
                P.add("act", lambda e: e.copy(out=ogT, in_=psv(b5, 128, 512, 64)), writes=[self.psb[b5], bogT])
                P.dma("sp", D["OG"][:, t0c:t0c + C].rearrange("(h p) t -> p h t", p=128), ogT, reads=[bogT], writes=[B["OG"]], pool="st")

        for si, (t0, T, lat) in enumerate(self.seqs()):
            ncks = T // C
            for dr in range(2):
                if lat:
                    P.dma("sp", S_, D["sgdn"][l, dr].rearrange("h k v -> k h v"), writes=[bS])
                else:
                    P.add("dve", lambda e: e.memset(S_, 0.0), writes=[bS])
                order = list(range(ncks)) if dr == 0 else list(range(ncks - 1, -1, -1))
                prep(t0 + order[0] * C, dr, 0, 0)
                for n_, ci in enumerate(order):
                    if n_ + 1 < ncks:
                        prep(t0 + order[n_ + 1] * C, dr, (n_ + 1) % 2, (n_ + 1) % 2)
                    scan(t0 + ci * C, dr, n_ % 2, n_ % 2, n_ == ncks - 1)
                if not lat:
                    b_ = si - 1
                    P.dma("sp", D["o_gdn"][b_, l, dr].rearrange("h k v -> k h v"), S_, reads=[bS], writes=[B["o_gdn"]], pool="st")


def host_constants(cfg, na_rpb, plan_tiles):
    ident = np.eye(128, dtype=np.float32)
    perm = np.zeros((128, 128), np.float32)
    for m in range(128):
        perm[m ^ 32, m] = 1.0
    t = np.arange(cfg.TL)
    pos = np.stack([t // GRID_W, t % GRID_W], -1).astype(np.float32)
    nq = HEAD_DIM // 4
    inv = (ROPE_BASE ** (-np.arange(nq, dtype=np.float32) / nq)).astype(np.float32)
    cos = np.zeros((128, cfg.TL), np.float32)
    sin = np.zeros((128, cfg.TL), np.float32)
    for p in range(128):
        a, half, f = p // 64, (p % 64) // 32, p % 32
        ang = pos[:, a] * inv[f]
        cos[p] = np.cos(ang)
        sin[p] = np.sin(ang) * (-1.0 if half == 0 else 1.0)
    i = np.arange(64)[:, None]
    j = np.arange(64)[None, :]
    masks = np.stack([(i > j), (i > j).T, (i >= j).T, (i < j), (i < j).T, (i <= j).T]).astype(np.float32)
    tri = np.stack([(i <= j), (i >= j)]).astype(np.float32)
    L = na_rpb.shape[0]
    nty = len(plan_tiles)
    nab = np.empty((L, NA_HEADS, nty, 128, 128), np.float32)
    for ti, (dr, dc, valid) in enumerate(plan_tiles):
        gathered = na_rpb[:, :, dr, dc]
        nab[:, :, ti] = np.where(valid[None, None], gathered, np.float32(-30000.0))
    return {"k_ident": ident, "k_perm": perm, "k_cos": cos, "k_sin": sin, "k_masks": masks, "k_tri": tri, "nab": nab}


def make_in_maps(cfg, inputs, n_cores, plan_tiles):
    consts = host_constants(cfg, np.asarray(inputs["na_rpb"], np.float32), plan_tiles)
    shared = {}
    for k in ("w_mod", "b_mod", "norm_pre", "norm_post", "ffn1_w_gu", "ffn1_w_dn", "ffn2_w_gu", "ffn2_w_dn", "w_in",
              "gdn_conv", "gdn_a_log", "gdn_dt_bias", "gdn_norm", "diff_lambda", "diff_norm", "w_branch_na",
              "w_branch_gdn", "w_branch_diff", "w_out"):
        shared[k] = np.ascontiguousarray(inputs[k], dtype=np.float32)
    shared.update(consts)
    maps = []
    L = cfg.L
    for i in range(n_cores):
        m = dict(shared)
        m["xs"] = np.ascontiguousarray(inputs["x_sample"][i])
        m["xp"] = np.ascontiguousarray(inputs["x_prompt"][i * cfg.NB:(i + 1) * cfg.NB]).reshape(cfg.NB * cfg.S, cfg.D)
        m["cna_k"] = np.ascontiguousarray(inputs["cache_na_k"][i]).reshape(L, cfg.PAST, cfg.NAW)
        m["cna_v"] = np.ascontiguousarray(inputs["cache_na_v"][i]).reshape(L, cfg.PAST, cfg.NAW)
        m["sgdn"] = np.ascontiguousarray(inputs["state_gdn"][i])
        m["cd_k"] = np.ascontiguousarray(inputs["cache_diff_k"][i]).reshape(L, cfg.PAST, cfg.DW)
        m["cd_v"] = np.ascontiguousarray(inputs["cache_diff_v"][i]).reshape(L, cfg.PAST, cfg.DW)
        m["cvec"] = np.ascontiguousarray(np.stack([inputs["c"][i], inputs["c_ctx"]]))
        maps.append(m)
    return maps


def assemble(cfg, results, n_cores):
    L = cfg.L
    y_s = np.stack([r["y_s"] for r in results])
    y_p = np.concatenate([r["y_p"].reshape(cfg.NB, cfg.S, cfg.D) for r in results])
    nk = np.concatenate([r["o_na_k"].reshape(cfg.NB, L, cfg.S, NA_HEADS, 128) for r in results])
    nv = np.concatenate([r["o_na_v"].reshape(cfg.NB, L, cfg.S, NA_HEADS, 128) for r in results])
    gs = np.concatenate([r["o_gdn"] for r in results])
    dk = np.concatenate([r["o_d_k"].reshape(cfg.NB, L, cfg.S, DIFF_HEADS, 2, 128) for r in results])
    dv = np.concatenate([r["o_d_v"].reshape(cfg.NB, L, cfg.S, DIFF_HEADS, 256) for r in results])
    return tuple(np.asarray(a, np.float32) for a in (y_s, y_p, nk, nv, gs, dk, dv))


def kernel(**inputs):
    inputs = {k: np.asarray(v) for k, v in inputs.items()}
    n = 8
    cfg = Cfg()
    b = Builder(cfg)
    nc = b.build()
    maps = make_in_maps(cfg, inputs, n, b.na_tiles)
    res = run_bass_kernel_spmd(nc, maps, core_ids=list(range(n)))
    y_s, y_p, nk, nv, gs, dk, dv = assemble(cfg, res.results, n)
    return (y_p, y_s, nk, nv, gs, dk, dv)
```

```python
import math
from contextlib import ExitStack
import numpy as np
import concourse.bass as bass
import concourse.mybir as mybir
from concourse.bass_utils import run_bass_kernel_spmd

F32 = mybir.dt.float32
BF16 = mybir.dt.bfloat16
AF = mybir.ActivationFunctionType
ALU = mybir.AluOpType
AX = mybir.AxisListType

HEAD_DIM = 128
NA_HEADS = 8
GDN_HEADS = 8
DIFF_HEADS = 4
GRID_W = 64
NA_WIN_R = 8
NA_WIN_C = 16
GDN_CHUNK = 64
EPS = 1e-6
N_MOD = 9
ROPE_BASE = 10000.0


class Buf:
    __slots__ = ("name", "last_w", "readers")

    def __init__(self, name=""):
        self.name = name
        self.last_w = None
        self.readers = []


class Op:
    __slots__ = ("eng", "fn", "deps", "sig", "vc", "waits", "dma", "pool", "needed")

    def __init__(self, eng, fn, dma, pool):
        self.eng = eng
        self.fn = fn
        self.deps = set()
        self.sig = None
        self.vc = None
        self.waits = None
        self.dma = dma
        self.pool = pool
        self.needed = False


class Prog:
    ENGS = ("pe", "act", "dve", "pool", "sp")

    def __init__(self, nc, dma_pools):
        self.nc = nc
        self.ops = []
        self.dma_pools = dma_pools
        self.nosync_same = {"pe"}
        self.last_on = {}
        self.pending_dma = []

    def add(self, eng, fn, reads=(), writes=(), dma=False, pool=None, extra=()):
        op = Op(eng, fn, dma, pool)
        op.deps.update(extra)
        for b in reads:
            if b.last_w is not None:
                op.deps.add(b.last_w)
            b.readers.append(op)
        for b in writes:
            if b.last_w is not None:
                op.deps.add(b.last_w)
            for r in b.readers:
                if r is not op:
                    op.deps.add(r)
            b.last_w = op
            b.readers = []
        self.ops.append(op)
        self.last_on[(eng, pool if dma else None)] = op
        if dma:
            self.pending_dma.append(op)
        return op

    def dma(self, eng, out, in_, reads=(), writes=(), pool="ld", extra=(), **kw):
        return self.add(eng, lambda e: e.dma_start(out=out, in_=in_, **kw), reads, writes, dma=True, pool=pool, extra=extra)

    def barrier(self):
        lasts = list(self.last_on.values())
        pend = self.pending_dma
        self.pending_dma = []
        for e in self.ENGS:
            op = self.add(e, lambda e_: None)
            for l in lasts:
                if l is not op:
                    op.deps.add(l)
            for d in pend:
                op.deps.add(d)

    def finalize(self, stack):
        nc = self.nc
        for op in self.ops:
            for d in op.deps:
                d.needed = True
        self.eng_sem = {e: stack.enter_context(nc.semaphore("s_" + e)) for e in self.ENGS}
        self.pool_sems = {}
        for pname, n in self.dma_pools.items():
            self.pool_sems[pname] = [stack.enter_context(nc.semaphore("d_%s%d" % (pname, i))) for i in range(n)]
        pool_rr = {p: 0 for p in self.dma_pools}
        pool_state = {p: [[0, None] for _ in range(n)] for p, n in self.dma_pools.items()}
        cnt = {e: 0 for e in self.ENGS}
        clock = {e: {} for e in self.ENGS}
        nwaits = 0
        for op in self.ops:
            E = op.eng
            ck = clock[E]
            if op.dma:
                p = op.pool
                k = pool_rr[p]
                pool_rr[p] = (k + 1) % len(pool_state[p])
                st = pool_state[p][k]
                if st[1] is not None:
                    op.deps.add(st[1])
                st[0] += 16
                st[1] = op
                op.sig = ((p, k), st[0])
            elif op.needed:
                cnt[E] += 1
                op.sig = (E, cnt[E])
            waits = {}
            for d in op.deps:
                k, v = d.sig
                if ck.get(k, 0) >= v:
                    continue
                if (not d.dma) and d.eng == E and E in self.nosync_same:
                    continue
                if waits.get(k, 0) < v:
                    waits[k] = v
            for d in op.deps:
                k, v = d.sig
                if k in waits and waits[k] >= v and d.vc is not None:
                    for kk, vv in d.vc.items():
                        if ck.get(kk, 0) < vv:
                            ck[kk] = vv
            for k, v in waits.items():
                if ck.get(k, 0) < v:
                    ck[k] = v
            op.waits = waits
            nwaits += len(waits)
            if op.sig is not None:
                vc = dict(ck)
                vc[op.sig[0]] = op.sig[1]
                op.vc = vc
            op.deps = None
        self.nwaits = nwaits

    def _sem(self, key):
        if isinstance(key, tuple):
            return self.pool_sems[key[0]][key[1]]
        return self.eng_sem[key]

    def emit(self, block):
        per = {e: [] for e in self.ENGS}
        for op in self.ops:
            per[op.eng].append(op)

        def run(eng_handle, ops):
            for op in ops:
                for k, v in op.waits.items():
                    eng_handle.wait_ge(self._sem(k), v)
                ins = op.fn(eng_handle)
                if op.sig is not None:
                    if ins is None:
                        ins = eng_handle.engine_nop() if hasattr(eng_handle, "engine_nop") else None
                    if ins is not None:
                        ins.then_inc(self._sem(op.sig[0]), 16 if op.dma else 1)
                    else:
                        eng_handle.sem_inc(self._sem(op.sig[0]), 1)

        if per["sp"]:
            block.sync(lambda e: run(e, per["sp"]))
        if per["pe"]:
            block.tensor(lambda e: run(e, per["pe"]))
        if per["act"]:
            block.scalar(lambda e: run(e, per["act"]))
        if per["dve"]:
            block.vector(lambda e: run(e, per["dve"]))
        if per["pool"]:
            block.gpsimd(lambda e: run(e, per["pool"]))


class Cfg:
    def __init__(self, D=2048, DFF=5632, TL=4096, S=256, NB=4, PAST=256, L=2):
        self.D, self.DFF, self.TL, self.S, self.NB, self.PAST, self.L = D, DFF, TL, S, NB, PAST, L
        self.DC = D // 128
        self.FC = DFF // 128
        self.TT = TL + NB * S
        self.NAW = NA_HEADS * HEAD_DIM
        self.GW = GDN_HEADS * HEAD_DIM
        self.DW = DIFF_HEADS * 2 * HEAD_DIM
        sp = (self.NAW, self.NAW, self.NAW, self.GW, self.GW, self.GW, self.GW, 4 * GDN_HEADS,
              self.DW, self.DW, self.DW, 3 * D)
        self.offs = [0] + list(np.cumsum(sp))
        self.DIN = int(self.offs[-1])
        self.blocks = []
        for t in range(0, TL, 512):
            self.blocks.append((t, min(512, TL - t), 0))
        ctx_tok = NB * S
        step = 512 if ctx_tok >= 512 else ctx_tok
        for t in range(0, ctx_tok, step):
            self.blocks.append((TL + t, step, 1))


def na_tile_plan(rows):
    wr = min(NA_WIN_R, rows)
    col = np.arange(GRID_W)
    cs = np.clip(col - NA_WIN_C // 2, 0, GRID_W - NA_WIN_C)
    types = {}
    tiles = []
    plan = []
    for j in range(rows // 2):
        qr = np.array([2 * j, 2 * j + 1])
        rs = np.clip(qr - NA_WIN_R // 2, 0, rows - wr)
        need = set()
        for a in range(2):
            for r in range(rs[a], rs[a] + wr):
                need.add(r // 2)
        lst = []
        for m in sorted(need):
            kr = np.array([2 * m, 2 * m + 1])
            KR = np.repeat(kr, GRID_W)[:, None]
            KC = np.tile(col, 2)[:, None]
            QR = np.repeat(qr, GRID_W)[None, :]
            QC = np.tile(col, 2)[None, :]
            RS = np.repeat(rs, GRID_W)[None, :]
            CS = np.tile(cs, 2)[None, :]
            valid = (KR >= RS) & (KR < RS + wr) & (KC >= CS) & (KC < CS + NA_WIN_C)
            dr = KR - QR + NA_WIN_R - 1
            dc = KC - QC + NA_WIN_C - 1
            dr = np.where(valid, dr, 0)
            dc = np.where(valid, dc, 0)
            key = (dr.tobytes(), dc.tobytes(), valid.tobytes())
            if key not in types:
                types[key] = len(tiles)
                tiles.append((dr, dc, valid))
            lst.append((m, types[key]))
        plan.append(lst)
    return plan, tiles


class Builder:
    def __init__(self, cfg):
        self.cfg = cfg
        self.nc = bass.Bass("TRN2", target_bir_lowering=False)
        self.P = Prog(self.nc, {"ld": 8, "w": 12, "st": 8})
        self.dram = {}
        self.dbuf = {}

    def din(self, name, shape, dt=F32):
        self.dram[name] = self.nc.dram_tensor(name, list(shape), dt, kind="ExternalInput").ap()
        self.dbuf[name] = Buf(name)
        return self.dram[name]

    def dout(self, name, shape, dt=F32):
        self.dram[name] = self.nc.dram_tensor(name, list(shape), dt, kind="ExternalOutput").ap()
        self.dbuf[name] = Buf(name)
        return self.dram[name]

    def dscr(self, name, shape, dt):
        self.dram[name] = self.nc.dram_tensor(name, list(shape), dt, kind="Internal").ap()
        self.dbuf[name] = Buf(name)
        return self.dram[name]

    def reset_arena(self):
        self.aoff = self.const_end

    def alloc(self, words, name=""):
        off = self.aoff
        self.aoff += (words + 7) // 8 * 8
        assert self.aoff <= getattr(self, "topoff", self.AW), ("sbuf arena overflow", name, self.aoff)
        return off

    def f32v(self, off, n):
        return self.arena[:, off:off + n]

    def bf16v(self, off, nbf):
        return self.arena[:, off:off + (nbf + 1) // 2].bitcast(BF16)

    def declare(self):
        c = self.cfg
        L = c.L
        d = self.din
        d("xs", [c.TL, c.D]); d("xp", [c.NB * c.S, c.D])
        d("cna_k", [L, c.PAST, c.NAW]); d("cna_v", [L, c.PAST, c.NAW])
        d("sgdn", [L, 2, GDN_HEADS, 128, 128])
        d("cd_k", [L, c.PAST, c.DW]); d("cd_v", [L, c.PAST, c.DW])
        d("cvec", [2, c.D])
        d("w_mod", [L, c.D, 9 * c.D]); d("b_mod", [L, 9 * c.D])
        d("norm_pre", [L, 3, c.D]); d("norm_post", [L, 3, c.D])
        d("ffn1_w_gu", [L, c.D, 2 * c.DFF]); d("ffn1_w_dn", [L, c.DFF, c.D])
        d("ffn2_w_gu", [L, c.D, 2 * c.DFF]); d("ffn2_w_dn", [L, c.DFF, c.D])
        d("w_in", [L, c.D, c.DIN])
        d("gdn_conv", [L, 3, 3 * c.GW]); d("gdn_a_log", [L, 2, 8]); d("gdn_dt_bias", [L, 2, 8])
        d("gdn_norm", [L, 128]); d("diff_lambda", [L, 4, 128]); d("diff_norm", [L, 256])
        d("w_branch_na", [L, c.NAW, c.D]); d("w_branch_gdn", [L, c.GW, c.D]); d("w_branch_diff", [L, c.DW, c.D])
        d("w_out", [L, c.D, c.D])
        d("k_ident", [128, 128]); d("k_perm", [128, 128])
        d("k_cos", [128, c.TL]); d("k_sin", [128, c.TL])
        d("k_masks", [6, 64, 64]); d("k_tri", [2, 64, 64])
        d("nab", [L, NA_HEADS, self.nty, 128, 128])
        o = self.dout
        o("y_s", [c.TL, c.D]); o("y_p", [c.NB * c.S, c.D])
        o("o_na_k", [c.NB, L, c.S, c.NAW]); o("o_na_v", [c.NB, L, c.S, c.NAW])
        o("o_gdn", [c.NB, L, 2, GDN_HEADS, 128, 128])
        o("o_d_k", [c.NB, L, c.S, c.DW]); o("o_d_v", [c.NB, L, c.S, c.DW])
        s = self.dscr
        s("XR", [c.D, c.TT], F32)
        s("QNA", [c.NAW, c.TT], BF16); s("KNA", [c.NAW, c.TT], BF16); s("VNA", [c.TT, c.NAW], BF16)
        s("GQ", [c.GW, c.TT], F32); s("GK", [c.GW, c.TT], F32); s("GV", [c.GW, c.TT], F32)
        s("GZT", [c.TT, c.GW], F32); s("GAB", [c.TT, 32], F32)
        s("DQ", [c.DW, c.TT], BF16); s("DK", [c.DW, c.TT], BF16); s("DV", [c.TT, c.DW], BF16)
        s("GATE", [3 * c.D, c.TT], BF16)
        s("ONA", [c.NAW, c.TT], BF16); s("OG", [c.GW, c.TT], BF16); s("OD", [c.DW, c.TT], BF16)
        s("GQN", [c.GW, c.TT], F32); s("GKN", [c.GW, c.TT], F32)
        s("GKT", [c.TT, c.GW], F32); s("GVT", [c.TT, c.GW], F32)
        s("GAB2", [c.TT, 32], F32); s("GOF", [c.TT, c.GW], F32)

    def build(self):
        c = self.cfg
        nc = self.nc
        P = self.P
        self.plan, self.na_tiles = na_tile_plan(c.TL // GRID_W)
        self.nty = len(self.na_tiles)
        self.declare()
        with ExitStack() as st:
            self.AW = 46800
            self.arena = st.enter_context(nc.sbuf_tensor("arena", [128, self.AW], F32))
            self.ps = st.enter_context(nc.psum_tensor("ps", [128, 8, 512], F32))
            self.psb = [Buf("ps%d" % i) for i in range(8)]
            self.aoff = 0
            self.const_end = 0
            self.consts()
            self.const_end = self.aoff
            self.modulation()
            self.const_end = self.aoff
            for seg in range(c.L + 1):
                if seg >= getattr(self, "seg_limit", 99):
                    break
                P.barrier()
                self.reset_arena()
                self.dense_segment(seg - 1 if seg > 0 else None, seg if seg < c.L else None)
                if seg < c.L:
                    P.barrier()
                    self.reset_arena()
                    self.mixers(seg)
            P.barrier()
            P.finalize(st)
            with nc.Block() as block:
                P.emit(block)
        return nc

    def consts(self):
        c, P = self.cfg, self.P
        self.b_const = Buf("const")
        o = self.alloc(128); self.identF = self.f32v(o, 128)
        o = self.alloc(128); self.permF = self.f32v(o, 128)
        o = self.alloc(128); self.onesF = self.f32v(o, 128)
        o = self.alloc(64); self.identB = self.bf16v(o, 128)
        o = self.alloc(64); self.onesB = self.bf16v(o, 128)
        P.dma("sp", self.identF, self.dram["k_ident"], writes=[self.b_const])
        P.dma("sp", self.permF, self.dram["k_perm"], writes=[self.b_const])
        o = self.alloc(8); self.epsT = self.f32v(o, 1)
        P.add("dve", lambda e: e.memset(self.epsT, EPS), writes=[self.b_const])
        P.add("dve", lambda e: e.memset(self.onesF, 1.0), writes=[self.b_const])
        P.add("dve", lambda e: e.memset(self.onesB, 1.0), writes=[self.b_const])
        P.add("dve", lambda e: e.tensor_copy(out=self.identB, in_=self.identF), writes=[self.b_const])

    def wpanel_load(self, slot, w2d, r0, kc, col0, ncols):
        P = self.P
        view = self.wp[slot][:, 0:kc * ncols].rearrange("p (k n) -> p k n", k=kc)
        step = max(1, 512 // max(1, (ncols * 4) // 512)) if False else 4
        for k0 in range(0, kc, step):
            k1 = min(kc, k0 + step)
            P.dma("pool", view[:, k0:k1, :],
                  w2d[r0 + k0 * 128:r0 + k1 * 128, col0:col0 + ncols].rearrange("(k p) n -> p k n", p=128),
                  writes=[self.wpb[slot][k0 // 4]], pool="w", extra=self.wextra)
        return view

    def next_wslot(self):
        s = self.wslot
        self.wslot = (self.wslot + 1) % len(self.wp)
        extra = set()
        for b in self.wpb[s]:
            if b.last_w is not None:
                extra.add(b.last_w)
            extra.update(b.readers)
        self.wpb[s] = [Buf("wp%d_%d" % (s, q)) for q in range(8)]
        self.wextra = extra
        return s

    def next_bank(self):
        b = self.banks[self.bank_i % len(self.banks)]
        self.bank_i += 1
        return b

    def modulation(self):
        c, P, ps = self.cfg, self.P, self.ps
        DC = c.DC
        b_m = Buf("modtmp")
        save = self.aoff
        o = self.alloc(DC * 2); cT = self.f32v(o, DC * 2).rearrange("p (k g) -> p k g", g=2)
        o = self.alloc(DC); cTb = self.bf16v(o, DC * 2).rearrange("p (k g) -> p k g", g=2)
        o = self.alloc(9 * c.D); brow = self.arena[0:1, o:o + 9 * c.D]
        o = self.alloc(2); ones2 = self.arena[0:1, o:o + 2]
        self.wp = []
        self.wpb = []
        for i in range(3):
            o = self.alloc(DC * 512 // 2)
            self.wp.append(self.bf16v(o, DC * 512)); self.wpb.append([Buf("wp%d_%d" % (i, q)) for q in range(8)])
        self.wslot = 0
        npre = []
        self.modtab = {}
        keep = []
        P.add("dve", lambda e: e.memset(ones2, 1.0), writes=[b_m])
        for g_ in range(2):
            P.dma("sp", cT[:, :, g_], self.dram["cvec"][g_].rearrange("(k p) -> p k", p=128), writes=[b_m],
                  allow_slow_non_contiguous=True)
        P.add("act", lambda e: e.activation(out=cTb, in_=cT, func=AF.Silu), reads=[b_m], writes=[b_m])
        self.aoff_keep = None
        tabs = {}
        for l in range(c.L):
            P.dma("sp", brow, self.dram["b_mod"][l:l + 1, :], writes=[b_m])
            nch = 9 * DC
            modps = ps[:, 7, 0:nch * 2].rearrange("p (n g) -> p n g", g=2)
            bps = self.psb[7]
            for n0 in range(0, 9 * c.D, 512):
                slot = self.next_wslot()
                ncl = min(512, 9 * c.D - n0)
                wv = self.wpanel_load(slot, self.dram["w_mod"][l], 0, DC, n0, ncl)
                for j in range(ncl // 128):
                    n = n0 // 128 + j
                    for k in range(DC):
                        P.add("pe", lambda e, wv=wv, k=k, j=j, n=n: e.matmul(
                            modps[:, n, :], lhsT=wv[:, k, j * 128:(j + 1) * 128], rhs=cTb[:, k, :],
                            start=(k == 0), stop=False), reads=[self.wpb[slot][k // 4], b_m], writes=[bps])
                    P.add("pe", lambda e, n=n: e.matmul(
                        modps[:, n, :], lhsT=brow[0:1, n * 128:(n + 1) * 128], rhs=ones2,
                        start=False, stop=True), reads=[b_m], writes=[bps])
            tabs[l] = None
            o = self._top_alloc(9 * DC * 2)
            mv = self.f32v(o, 9 * DC * 2).rearrange("p (i k g) -> p i k g", i=9, g=2)
            P.add("act", lambda e, mv=mv, modps=modps: e.copy(out=mv, in_=modps.rearrange("p (i k) g -> p i k g", i=9)),
                  reads=[], writes=[bps, self.b_const])
            o = self._top_alloc(3 * DC); npre_l = self.f32v(o, 3 * DC).rearrange("p (i k) -> p i k", i=3)
            o = self._top_alloc(3 * DC); npost_l = self.f32v(o, 3 * DC).rearrange("p (i k) -> p i k", i=3)
            for i_ in range(3):
                P.dma("sp", npre_l[:, i_, :], self.dram["norm_pre"][l, i_].rearrange("(k p) -> p k", p=128),
                      writes=[self.b_const], allow_slow_non_contiguous=True)
                P.dma("sp", npost_l[:, i_, :], self.dram["norm_post"][l, i_].rearrange("(k p) -> p k", p=128),
                      writes=[self.b_const], allow_slow_non_contiguous=True)
            for g in range(2):
                o = self._top_alloc(3 * DC); A = self.f32v(o, 3 * DC).rearrange("p (i k) -> p i k", i=3)
                o = self._top_alloc(3 * DC); Bt = self.f32v(o, 3 * DC).rearrange("p (i k) -> p i k", i=3)
                o = self._top_alloc(3 * DC); G = self.f32v(o, 3 * DC).rearrange("p (i k) -> p i k", i=3)
                for i in range(3):
                    coef = 1.0 if i == 1 else 0.5
                    P.add("dve", lambda e, A=A, mv=mv, npre_l=npre_l, i=i, g=g: e.scalar_tensor_tensor(
                        out=A[:, i, :], in0=mv[:, 3 * i + 1, :, g], scalar=1.0, in1=npre_l[:, i, :],
                        op0=ALU.add, op1=ALU.mult), reads=[self.b_const], writes=[self.b_const])
                    P.add("dve", lambda e, Bt=Bt, mv=mv, i=i, g=g: e.tensor_copy(out=Bt[:, i, :], in_=mv[:, 3 * i, :, g]),
                          reads=[self.b_const], writes=[self.b_const])
                    P.add("dve", lambda e, G=G, mv=mv, npost_l=npost_l, i=i, g=g, coef=coef: e.scalar_tensor_tensor(
                        out=G[:, i, :], in0=mv[:, 3 * i + 2, :, g], scalar=coef, in1=npost_l[:, i, :],
                        op0=ALU.mult, op1=ALU.mult), reads=[self.b_const], writes=[self.b_const])
                self.modtab[(l, g)] = (A, Bt, G)
        self.P.barrier()
        self.aoff = save

    def _top_alloc(self, words):
        if not hasattr(self, "topoff"):
            self.topoff = self.AW
        self.topoff -= (words + 7) // 8 * 8
        self.AW_eff = self.topoff
        return self.topoff

    def dense_alloc(self):
        c = self.cfg
        DC, FC = c.DC, c.FC
        A = {}
        def f32(name, n):
            A[name] = self.f32v(self.alloc(n, name), n); A["b_" + name] = Buf(name)
        def b16(name, n):
            A[name] = self.bf16v(self.alloc((n + 1) // 2, name), n); A["b_" + name] = Buf(name)
        f32("x", DC * 512)
        b16("h", DC * 512)
        b16("act", max((FC + 1) // 2, 8) * 512)
        f32("y", DC * 512)
        for i in range(3):
            f32("t%d" % i, 512)
        f32("rstd", 512)
        for i in range(4):
            b16("sb%d" % i, 512)
            f32("sf%d" % i, 512)
        f32("cos", 512); f32("sin", 512)
        self.wp, self.wpb = [], []
        for i in range(3):
            o = self.alloc(8192 // 2)
            self.wp.append(self.bf16v(o, 8192)); self.wpb.append([Buf("wp%d_%d" % (i, q)) for q in range(8)])
        self.wslot = 0
        self.banks = [2, 3, 4, 5, 6, 7]
        self.bank_i = 0
        self.A = A
        self.rr = {"t": 0, "sb": 0, "sf": 0}
        return A

    def rot(self, kind, n):
        i = self.rr[kind]
        self.rr[kind] = (i + 1) % n
        return "%s%d" % (kind, i)

    def x3(self, name, nt):
        c = self.cfg
        return self.A[name][:, 0:c.DC * nt].rearrange("p (k t) -> p k t", t=nt)

    def norm_stats(self, src3, nt, srcbuf, kc, scale):
        P, ps, A = self.P, self.ps, self.A
        bank = 0 if self.bank_i % 2 == 0 else 1
        for k in range(kc):
            tn = self.rot("t", 3)
            t = A[tn][:, 0:nt]
            P.add("act", lambda e, t=t, k=k: e.activation(out=t, in_=src3[:, k, :], func=AF.Square),
                  reads=[srcbuf], writes=[A["b_" + tn]])
            P.add("pe", lambda e, t=t, k=k, bank=bank: e.matmul(ps[:, bank, 0:nt], lhsT=self.onesF, rhs=t,
                                                                start=(k == 0), stop=(k == kc - 1)),
                  reads=[A["b_" + tn], self.b_const], writes=[self.psb[bank]])
        rstd = A["rstd"][:, 0:nt]
        P.add("act", lambda e: e.activation(out=rstd, in_=ps[:, bank, 0:nt], func=AF.Ln, bias=self.epsT, scale=scale),
              reads=[self.b_const], writes=[self.psb[bank], A["b_rstd"]])
        P.add("act", lambda e: e.activation(out=rstd, in_=rstd, func=AF.Exp, scale=-0.5), writes=[A["b_rstd"]])
        return rstd

    def prenorm(self, l, i, g, nt):
        c, P, A = self.cfg, self.P, self.A
        x3, h3 = self.x3("x", nt), self.x3("h", nt)
        rstd = self.norm_stats(x3, nt, A["b_x"], c.DC, 1.0 / c.D)
        At, Bt, _ = self.modtab[(l, g)]
        for k in range(c.DC):
            tn = self.rot("t", 3)
            t = A[tn][:, 0:nt]
            P.add("dve", lambda e, t=t, k=k: e.scalar_tensor_tensor(out=t, in0=x3[:, k, :], scalar=At[:, i, k:k + 1],
                                                                   in1=rstd, op0=ALU.mult, op1=ALU.mult),
                  reads=[A["b_x"], A["b_rstd"], self.b_const], writes=[A["b_" + tn]])
            P.add("act", lambda e, t=t, k=k: e.activation(out=h3[:, k, :], in_=t, func=AF.Identity,
                                                          bias=Bt[:, i, k:k + 1], scale=1.0),
                  reads=[A["b_" + tn], self.b_const], writes=[A["b_h"]])

    def resid(self, l, i, g, nt):
        c, P, A = self.cfg, self.P, self.A
        x3, y3 = self.x3("x", nt), self.x3("y", nt)
        rstd = self.norm_stats(y3, nt, A["b_y"], c.DC, 1.0 / c.D)
        _, _, G = self.modtab[(l, g)]
        for k in range(c.DC):
            tn = self.rot("t", 3)
            t = A[tn][:, 0:nt]
            P.add("dve", lambda e, t=t, k=k: e.scalar_tensor_tensor(out=t, in0=y3[:, k, :], scalar=G[:, i, k:k + 1],
                                                                   in1=rstd, op0=ALU.mult, op1=ALU.mult),
                  reads=[A["b_y"], A["b_rstd"], self.b_const], writes=[A["b_" + tn]])
            P.add("pool", lambda e, t=t, k=k: e.tensor_tensor(out=x3[:, k, :], in0=x3[:, k, :], in1=t, op=ALU.add),
                  reads=[A["b_" + tn]], writes=[A["b_x"]])

    def gemm_fm(self, xname, kc, nt, w2d, r0, cols, evac):
        P, ps, A = self.P, self.ps, self.A
        x3 = A[xname][:, 0:kc * nt].rearrange("p (k t) -> p k t", t=nt)
        pc = min(512, (8192 // kc) // 128 * 128)
        i = 0
        while i < len(cols):
            j = i
            while j + 1 < len(cols) and cols[j + 1] == cols[j] + 128 and (cols[j + 1] + 128 - cols[i]) <= pc:
                j += 1
            slot = self.next_wslot()
            ncols = cols[j] + 128 - cols[i]
            wv = self.wpanel_load(slot, w2d, r0, kc, cols[i], ncols)
            for q in range(i, j + 1):
                bank = self.next_bank()
                off = cols[q] - cols[i]
                for k in range(kc):
                    P.add("pe", lambda e, wv=wv, k=k, off=off, bank=bank: e.matmul(
                        ps[:, bank, 0:nt], lhsT=wv[:, k, off:off + 128], rhs=x3[:, k, :],
                        start=(k == 0), stop=(k == kc - 1)),
                        reads=[self.wpb[slot][k // 4], A["b_" + xname]], writes=[self.psb[bank]])
                evac(q, cols[q], bank)
            i = j + 1

    def gemm_tm(self, xname, kc, nt, w2d, r0, col0, ncols, evac):
        P, ps, A = self.P, self.ps, self.A
        x3 = A[xname][:, 0:kc * nt].rearrange("p (k t) -> p k t", t=nt)
        for c0 in range(col0, col0 + ncols, 512):
            ncl = min(512, col0 + ncols - c0)
            slot = self.next_wslot()
            wv = self.wpanel_load(slot, w2d, r0, kc, c0, ncl)
            for tt in range(nt // 128):
                bank = self.next_bank()
                for k in range(kc):
                    P.add("pe", lambda e, wv=wv, k=k, tt=tt, bank=bank, ncl=ncl: e.matmul(
                        ps[:, bank, 0:ncl], lhsT=x3[:, k, tt * 128:(tt + 1) * 128], rhs=wv[:, k, :],
                        start=(k == 0), stop=(k == kc - 1)),
                        reads=[self.wpb[slot][k // 4], A["b_" + xname]], writes=[self.psb[bank]])
                evac(tt, c0, ncl, bank)

    def ffn(self, l, which, nt):
        c, P, ps, A = self.cfg, self.P, self.ps, self.A
        wgu = self.dram["ffn%d_w_gu" % which][l]
        wdn = self.dram["ffn%d_w_dn" % which][l]
        FH = (c.FC + 1) // 2
        act3 = A["act"][:, 0:FH * nt].rearrange("p (k t) -> p k t", t=nt)
        h3 = self.x3("h", nt)
        y3 = self.x3("y", nt)
        for hf in range(2):
            jlo, jhi = hf * FH, min(c.FC, (hf + 1) * FH)
            for j0 in range(jlo, jhi, 2):
                nj = min(2, jhi - j0)
                slot = self.next_wslot()
                wv = self.wp[slot][:, 0:c.DC * 2 * nj * 128].rearrange("p (k n) -> p k n", k=c.DC)
                for half in range(2):
                    for k0 in range(0, c.DC, 4):
                        k1 = min(c.DC, k0 + 4)
                        P.dma("pool", wv[:, k0:k1, half * nj * 128:(half + 1) * nj * 128],
                              wgu[k0 * 128:k1 * 128, half * c.DFF + j0 * 128: half * c.DFF + (j0 + nj) * 128]
                              .rearrange("(k p) n -> p k n", p=128), writes=[self.wpb[slot][half * 4 + k0 // 4]], pool="w", extra=self.wextra)
                for jj in range(nj):
                    j = j0 + jj
                    bg, bu = self.next_bank(), self.next_bank()
                    for half, bank in ((0, bg), (1, bu)):
                        off = half * nj * 128 + jj * 128
                        for k in range(c.DC):
                            P.add("pe", lambda e, wv=wv, k=k, off=off, bank=bank: e.matmul(
                                ps[:, bank, 0:nt], lhsT=wv[:, k, off:off + 128], rhs=h3[:, k, :],
                                start=(k == 0), stop=(k == c.DC - 1)),
                                reads=[self.wpb[slot][half * 4 + k // 4], A["b_h"]], writes=[self.psb[bank]])
                    sn = self.rot("sf", 4)
                    sg = A[sn][:, 0:nt]
                    P.add("act", lambda e, sg=sg, bg=bg: e.activation(out=sg, in_=ps[:, bg, 0:nt], func=AF.Silu),
                          writes=[self.psb[bg], A["b_" + sn]])
                    P.add("dve", lambda e, sg=sg, bu=bu, j=j - jlo: e.tensor_tensor(out=act3[:, j, :], in0=sg,
                                                                            in1=ps[:, bu, 0:nt], op=ALU.mult),
                          reads=[A["b_" + sn]], writes=[self.psb[bu], A["b_act"]])

            def ev(q, col0, bank, hf=hf):
                if hf == 0:
                    P.add("act", lambda e: e.copy(out=y3[:, q, :], in_=ps[:, bank, 0:nt]),
                          writes=[self.psb[bank], A["b_y"]])
                else:
                    P.add("dve", lambda e: e.tensor_tensor(out=y3[:, q, :], in0=y3[:, q, :], in1=ps[:, bank, 0:nt],
                                                           op=ALU.add), writes=[self.psb[bank], A["b_y"]])
            self.gemm_fm("act", jhi - jlo, nt, wdn, jlo * 128, [k * 128 for k in range(c.DC)], ev)

    def load_x_input(self, blk):
        c, P, ps, A = self.cfg, self.P, self.ps, self.A
        t0, nt, g = blk
        src = self.dram["xs"] if g == 0 else self.dram["xp"]
        sb = self.dbuf["xs" if g == 0 else "xp"]
        r0 = t0 if g == 0 else t0 - c.TL
        x3 = self.x3("x", nt)
        stage = A["y"][:, 0:c.D]
        for tt in range(nt // 128):
            P.dma("sp", stage, src[r0 + tt * 128:r0 + (tt + 1) * 128, :], reads=[sb], writes=[A["b_y"]])
            for k in range(c.DC):
                bank = self.next_bank()
                P.add("pe", lambda e, k=k, bank=bank: e.transpose(out=ps[:, bank, 0:128], in_=stage[:, k * 128:(k + 1) * 128],
                                                                  identity=self.identF),
                      reads=[A["b_y"], self.b_const], writes=[self.psb[bank]])
                P.add("act" if k % 2 else "dve",
                      (lambda e, k=k, bank=bank, tt=tt: e.copy(out=x3[:, k, tt * 128:(tt + 1) * 128], in_=ps[:, bank, 0:128])) if k % 2 else
                      (lambda e, k=k, bank=bank, tt=tt: e.tensor_copy(out=x3[:, k, tt * 128:(tt + 1) * 128], in_=ps[:, bank, 0:128])),
                      writes=[self.psb[bank], A["b_x"]])

    def store_y_output(self, blk):
        c, P, ps, A = self.cfg, self.P, self.ps, self.A
        t0, nt, g = blk
        dst = self.dram["y_s"] if g == 0 else self.dram["y_p"]
        db = self.dbuf["y_s" if g == 0 else "y_p"]
        r0 = t0 if g == 0 else t0 - c.TL
        x3 = self.x3("x", nt)
        stage = A["y"][:, 0:c.D]
        for tt in range(nt // 128):
            for k in range(c.DC):
                bank = self.next_bank()
                P.add("pe", lambda e, k=k, bank=bank, tt=tt: e.transpose(out=ps[:, bank, 0:128], in_=x3[:, k, tt * 128:(tt + 1) * 128],
                                                                         identity=self.identF),
                      reads=[A["b_x"], self.b_const], writes=[self.psb[bank]])
                P.add("act", lambda e, k=k, bank=bank: e.copy(out=stage[:, k * 128:(k + 1) * 128], in_=ps[:, bank, 0:128]),
                      writes=[self.psb[bank], A["b_y"]])
            P.dma("sp", dst[r0 + tt * 128:r0 + (tt + 1) * 128, :], stage, reads=[A["b_y"]], writes=[db], pool="st")

    def dense_segment(self, lp, ln):
        c, P, A = self.cfg, self.P, self.dense_alloc()
        for blk in c.blocks:
            t0, nt, g = blk
            if lp is None:
                self.load_x_input(blk)
            else:
                P.dma("sp", self.x3("x", nt), self.dram["XR"][:, t0:t0 + nt].rearrange("(k p) t -> p k t", p=128),
                      reads=[self.dbuf["XR"]], writes=[A["b_x"]])
                self.merge(lp, blk)
                self.prenorm(lp, 2, g, nt)
                self.ffn(lp, 2, nt)
                self.resid(lp, 2, g, nt)
            if ln is not None:
                self.prenorm(ln, 0, g, nt)
                self.ffn(ln, 1, nt)
                self.resid(ln, 0, g, nt)
                self.prenorm(ln, 1, g, nt)
                self.proj(ln, blk)
                P.dma("sp", self.dram["XR"][:, t0:t0 + nt].rearrange("(k p) t -> p k t", p=128), self.x3("x", nt),
                      reads=[A["b_x"]], writes=[self.dbuf["XR"]], pool="st")
            else:
                self.store_y_output(blk)

    def proj(self, l, blk):
        c, P, ps, A = self.cfg, self.P, self.ps, self.A
        t0, nt, g = blk
        w = self.dram["w_in"][l]
        o = c.offs
        D = self.dram
        B = self.dbuf

        def fm_store(dst, dt_bf16, col_base, func=None):
            def ev(q, col0, bank):
                r = col0 - col_base
                sn = self.rot("sb", 4) if dt_bf16 else self.rot("sf", 4)
                sv = A[sn][:, 0:nt]
                if func is None:
                    P.add("act", lambda e: e.copy(out=sv, in_=ps[:, bank, 0:nt]), writes=[self.psb[bank], A["b_" + sn]])
                else:
                    P.add("act", lambda e: e.activation(out=sv, in_=ps[:, bank, 0:nt], func=func),
                          writes=[self.psb[bank], A["b_" + sn]])
                P.dma("sp", D[dst][r:r + 128, t0:t0 + nt], sv, reads=[A["b_" + sn]], writes=[B[dst]], pool="st")
            return ev

        def rope_store(dst, col_base):
            cos, sin = A["cos"][:, 0:nt], A["sin"][:, 0:nt]

            def ev(q, col0, bank):
                r = col0 - col_base
                fn_ = self.rot("sf", 4); qf = A[fn_][:, 0:nt]
                P.add("act", lambda e: e.copy(out=qf, in_=ps[:, bank, 0:nt]), writes=[self.psb[bank], A["b_" + fn_]])
                b2 = self.next_bank()
                P.add("pe", lambda e: e.matmul(ps[:, b2, 0:nt], lhsT=self.permF, rhs=qf, start=True, stop=True),
                      reads=[A["b_" + fn_], self.b_const], writes=[self.psb[b2]])
                tn = self.rot("t", 3); t = A[tn][:, 0:nt]
                P.add("dve", lambda e: e.tensor_tensor(out=t, in0=ps[:, b2, 0:nt], in1=sin, op=ALU.mult),
                      reads=[A["b_sin"]], writes=[self.psb[b2], A["b_" + tn]])
                P.add("pool", lambda e: e.tensor_tensor(out=qf, in0=qf, in1=cos, op=ALU.mult),
                      reads=[A["b_cos"]], writes=[A["b_" + fn_]])
                sn = self.rot("sb", 4); sv = A[sn][:, 0:nt]
                P.add("dve", lambda e: e.tensor_tensor(out=sv, in0=qf, in1=t, op=ALU.add),
                      reads=[A["b_" + fn_], A["b_" + tn]], writes=[A["b_" + sn]])
                P.dma("sp", D[dst][r:r + 128, t0:t0 + nt], sv, reads=[A["b_" + sn]], writes=[B[dst]], pool="st")
            return ev

        def chunks(i):
            return list(range(int(o[i]), int(o[i + 1]), 128))

        self.gemm_fm("h", c.DC, nt, w, 0, chunks(0), fm_store("QNA", True, int(o[0])))
        self.gemm_fm("h", c.DC, nt, w, 0, chunks(1), fm_store("KNA", True, int(o[1])))
        self.gemm_fm("h", c.DC, nt, w, 0, chunks(3), fm_store("GQ", False, int(o[3])))
        self.gemm_fm("h", c.DC, nt, w, 0, chunks(4), fm_store("GK", False, int(o[4])))
        self.gemm_fm("h", c.DC, nt, w, 0, chunks(5), fm_store("GV", False, int(o[5])))
        if g == 0:
            P.dma("sp", A["cos"][:, 0:nt], D["k_cos"][:, t0:t0 + nt], writes=[A["b_cos"]])
            P.dma("sp", A["sin"][:, 0:nt], D["k_sin"][:, t0:t0 + nt], writes=[A["b_sin"]])
            self.gemm_fm("h", c.DC, nt, w, 0, chunks(8), rope_store("DQ", int(o[8])))
            self.gemm_fm("h", c.DC, nt, w, 0, chunks(9), rope_store("DK", int(o[9])))
        else:
            self.gemm_fm("h", c.DC, nt, w, 0, chunks(8), fm_store("DQ", True, int(o[8])))
            self.gemm_fm("h", c.DC, nt, w, 0, chunks(9), fm_store("DK", True, int(o[9])))
        self.gemm_fm("h", c.DC, nt, w, 0, chunks(11), fm_store("GATE", True, int(o[11]), AF.Sigmoid))

        def tm_store(dst, col_base, bf, func=None, cache=None):
            def ev(tt, c0, ncl, bank):
                r = c0 - col_base
                tok = t0 + tt * 128
                fn_ = self.rot("sf", 4); sv = A[fn_][:, 0:ncl]
                if func is None:
                    P.add("act", lambda e: e.copy(out=sv, in_=ps[:, bank, 0:ncl]), writes=[self.psb[bank], A["b_" + fn_]])
                else:
                    P.add("act", lambda e: e.activation(out=sv, in_=ps[:, bank, 0:ncl], func=func),
                          writes=[self.psb[bank], A["b_" + fn_]])
                if dst is not None:
                    if bf:
                        sn = self.rot("sb", 4); sb_ = A[sn][:, 0:ncl]
                        P.add("dve", lambda e: e.tensor_copy(out=sb_, in_=sv), reads=[A["b_" + fn_]], writes=[A["b_" + sn]])
                        P.dma("sp", D[dst][tok:tok + 128, r:r + ncl], sb_, reads=[A["b_" + sn]], writes=[B[dst]], pool="st")
                    else:
                        P.dma("sp", D[dst][tok:tok + 128, r:r + ncl], sv, reads=[A["b_" + fn_]], writes=[B[dst]], pool="st")
                if cache is not None and g == 1:
                    ct = tok - c.TL
                    b_, s_ = ct // c.S, ct % c.S
                    P.dma("sp", D[cache][b_, l, s_:s_ + 128, r:r + ncl], sv, reads=[A["b_" + fn_]], writes=[B[cache]], pool="st")
            return ev

        self.gemm_tm("h", c.DC, nt, w, 0, int(o[2]), c.NAW, tm_store("VNA", int(o[2]), True, cache="o_na_v"))
        self.gemm_tm("h", c.DC, nt, w, 0, int(o[6]), c.GW, tm_store("GZT", int(o[6]), False, func=AF.Silu))
        self.gemm_tm("h", c.DC, nt, w, 0, int(o[7]), 32, tm_store("GAB", int(o[7]), False))
        self.gemm_tm("h", c.DC, nt, w, 0, int(o[10]), c.DW, tm_store("DV", int(o[10]), True, cache="o_d_v"))
        if g == 1:
            self.gemm_tm("h", c.DC, nt, w, 0, int(o[1]), c.NAW, tm_store(None, int(o[1]), False, cache="o_na_k"))
            self.gemm_tm("h", c.DC, nt, w, 0, int(o[9]), c.DW, tm_store(None, int(o[9]), False, cache="o_d_k"))

    def merge(self, l, blk):
        c, P, ps, A = self.cfg, self.P, self.ps, self.A
        t0, nt, g = blk
        D, B = self.dram, self.dbuf
        y3, h3 = self.x3("y", nt), self.x3("h", nt)
        o3 = A["act"][:, 0:8 * nt].rearrange("p (k t) -> p k t", t=nt)
        for br, (src, wn) in enumerate((("ONA", "w_branch_na"), ("OG", "w_branch_gdn"), ("OD", "w_branch_diff"))):
            P.dma("sp", o3, D[src][:, t0:t0 + nt].rearrange("(k p) t -> p k t", p=128), reads=[B[src]], writes=[A["b_act"]])

            def ev(q, col0, bank, br=br):
                sn = self.rot("sb", 4); gt = A[sn][:, 0:nt]
                r = br * c.D + col0
                P.dma("sp", gt, D["GATE"][r:r + 128, t0:t0 + nt], reads=[B["GATE"]], writes=[A["b_" + sn]])
                if br == 0:
                    P.add("dve", lambda e: e.tensor_tensor(out=y3[:, q, :], in0=gt, in1=ps[:, bank, 0:nt], op=ALU.mult),
                          reads=[A["b_" + sn]], writes=[self.psb[bank], A["b_y"]])
                else:
                    tn = self.rot("t", 3); t = A[tn][:, 0:nt]
                    P.add("dve", lambda e: e.tensor_tensor(out=t, in0=gt, in1=ps[:, bank, 0:nt], op=ALU.mult),
                          reads=[A["b_" + sn]], writes=[self.psb[bank], A["b_" + tn]])
                    P.add("pool", lambda e: e.tensor_tensor(out=y3[:, q, :], in0=y3[:, q, :], in1=t, op=ALU.add),
                          reads=[A["b_" + tn]], writes=[A["b_y"]])
            self.gemm_fm("act", 8, nt, D[wn][l], 0, [k * 128 for k in range(c.DC)], ev)
        for k in range(c.DC):
            P.add("act", lambda e, k=k: e.copy(out=h3[:, k, :], in_=y3[:, k, :]), reads=[A["b_y"]], writes=[A["b_h"]])

        def ev2(q, col0, bank):
            P.add("act", lambda e: e.copy(out=y3[:, q, :], in_=ps[:, bank, 0:nt]), writes=[self.psb[bank], A["b_y"]])
        self.gemm_fm("h", c.DC, nt, D["w_out"][l], 0, [k * 128 for k in range(c.DC)], ev2)
        self.resid(l, 1, g, nt)

    def mixers(self, l):
        c, P = self.cfg, self.P
        self.mix_setup(l)
        mark = self.aoff
        for h in range(NA_HEADS):
            self.aoff = mark
            self.na_head(l, h)
            P.barrier()
        P.barrier()
        for h in range(DIFF_HEADS):
            self.aoff = mark
            self.diff_head(l, h)
            P.barrier()
        self.aoff = mark
        self.gdn(l)

    def mix_setup(self, l):
        c, P, ps = self.cfg, self.P, self.ps
        D, B = self.dram, self.dbuf
        PC = c.PAST // 128
        M = {}
        self.M = M
        M["b"] = Buf("mixconst")
        o = self.alloc(8 * c.PAST // 2); M["KcNA"] = self.bf16v(o, 8 * c.PAST).rearrange("p (h t) -> p h t", h=8)
        o = self.alloc(8 * c.PAST // 2); M["KcD"] = self.bf16v(o, 8 * c.PAST).rearrange("p (h t) -> p h t", h=8)
        o = self.alloc(PC * 1024 // 2); M["VcNA"] = self.bf16v(o, PC * 1024).rearrange("p (k n) -> p k n", k=PC)
        o = self.alloc(PC * 1024 // 2); M["VcD"] = self.bf16v(o, PC * 1024).rearrange("p (k n) -> p k n", k=PC)
        o = self.alloc(8); M["lam"] = self.f32v(o, 1)
        o = self.alloc(8); M["nlam"] = self.f32v(o, 1)
        o = self.alloc(8); M["dnw"] = self.f32v(o, 2)
        mark = self.aoff
        o = self.alloc(1024); stage = self.f32v(o, 1024)
        bst = Buf("stage")
        for name, src in (("KcNA", "cna_k"), ("KcD", "cd_k")):
            for k in range(PC):
                P.dma("sp", stage, D[src][l, k * 128:(k + 1) * 128, :], writes=[bst])
                for h in range(8):
                    bank = 6 + (h % 2)
                    P.add("pe", lambda e, h=h, bank=bank: e.transpose(out=ps[:, bank, 0:128], in_=stage[:, h * 128:(h + 1) * 128],
                                                                      identity=self.identF),
                          reads=[bst, self.b_const], writes=[self.psb[bank]])
                    P.add("act", lambda e, h=h, bank=bank, k=k, name=name: e.copy(out=M[name][:, h, k * 128:(k + 1) * 128],
                                                                                 in_=ps[:, bank, 0:128]),
                          writes=[self.psb[bank], M["b"]])
        for name, src in (("VcNA", "cna_v"), ("VcD", "cd_v")):
            P.dma("pool", M[name], D[src][l].rearrange("(k p) n -> p k n", p=128), writes=[M["b"]], pool="w")
        lam_init = 0.8 - 0.6 * math.exp(-0.3 * l)
        o = self.alloc(8); lt = self.f32v(o, 4)
        o = self.alloc(8); pr = self.f32v(o, 2)
        for i_ in range(4):
            P.dma("sp", lt[:, i_:i_ + 1], D["diff_lambda"][l, i_].rearrange("(p o) -> p o", o=1), writes=[bst],
                  allow_slow_non_contiguous=True)
        P.add("dve", lambda e: e.tensor_tensor(out=pr[:, 0:1], in0=lt[:, 0:1], in1=lt[:, 1:2], op=ALU.mult), reads=[bst], writes=[bst])
        P.add("dve", lambda e: e.tensor_tensor(out=pr[:, 1:2], in0=lt[:, 2:3], in1=lt[:, 3:4], op=ALU.mult), reads=[bst], writes=[bst])
        P.add("pe", lambda e: e.matmul(ps[:, 6, 0:2], lhsT=self.onesF, rhs=pr, start=True, stop=True),
              reads=[bst, self.b_const], writes=[self.psb[6]])
        P.add("act", lambda e: e.activation(out=pr, in_=ps[:, 6, 0:2], func=AF.Exp), writes=[self.psb[6], bst])
        P.add("dve", lambda e: e.scalar_tensor_tensor(out=M["lam"], in0=pr[:, 0:1], scalar=lam_init, in1=pr[:, 1:2],
                                                      op0=ALU.add, op1=ALU.subtract), reads=[bst], writes=[M["b"]])
        P.add("dve", lambda e: e.tensor_scalar(out=M["nlam"], in0=M["lam"], scalar1=-1.0, scalar2=None, op0=ALU.mult),
              writes=[M["b"]])
        for d_ in range(2):
            P.dma("sp", M["dnw"][:, d_:d_ + 1], D["diff_norm"][l, d_ * 128:(d_ + 1) * 128].rearrange("(p o) -> p o", o=1),
                  writes=[M["b"]], allow_slow_non_contiguous=True)
        P.add("dve", lambda e: e.tensor_scalar(out=M["dnw"], in0=M["dnw"], scalar1=1.0 - lam_init, scalar2=None, op0=ALU.mult),
              writes=[M["b"]])
        P.barrier()
        self.aoff = mark

    def seqs(self):
        c = self.cfg
        out = [(0, c.TL, True)]
        for b_ in range(c.NB):
            out.append((c.TL + b_ * c.S, c.S, False))
        return out

    def na_head(self, l, h):
        c, P, ps, M = self.cfg, self.P, self.ps, self.M
        D, B = self.dram, self.dbuf
        scale = HEAD_DIM ** -0.5
        TLc = c.TL // 128
        PC = c.PAST // 128
        Tmax = c.TL
        o = self.alloc(Tmax // 2); QT = self.bf16v(o, Tmax); bQ = Buf("QT")
        o = self.alloc(Tmax // 2); KT = self.bf16v(o, Tmax); bK = Buf("KT")
        o = self.alloc(Tmax // 2); Vt = self.bf16v(o, Tmax).rearrange("p (k d) -> p k d", d=128); bV = Buf("Vt")
        o = self.alloc(Tmax // 2); ost = self.bf16v(o, Tmax); bO = Buf("ost")
        o = self.alloc(self.nty * 128); bias = self.f32v(o, self.nty * 128).rearrange("p (t q) -> p t q", q=128); bB = Buf("bias")
        pTs, bP = [], []
        for i in range(2):
            o = self.alloc(8 * 128 // 2); pTs.append(self.bf16v(o, 1024)); bP.append(Buf("pT%d" % i))
        o = self.alloc(128); rden = self.f32v(o, 128); bR = Buf("rden")
        for ty0 in range(0, self.nty, 4):
            ty1 = min(self.nty, ty0 + 4)
            P.dma("sp", bias[:, ty0:ty1, :], D["nab"][l, h, ty0:ty1].rearrange("t k q -> k t q"), writes=[bB])
        P.add("act", lambda e: e.mul(out=bias, in_=bias, mul=1.0 / scale), writes=[bB])
        it = 0
        for (t0, T, lat) in self.seqs():
            nqt = T // 128
            r0 = h * 128
            P.dma("sp", QT[:, 0:T], D["QNA"][r0:r0 + 128, t0:t0 + T], reads=[B["QNA"]], writes=[bQ])
            P.dma("sp", KT[:, 0:T], D["KNA"][r0:r0 + 128, t0:t0 + T], reads=[B["KNA"]], writes=[bK])
            for k0 in range(0, nqt, 8):
                k1 = min(nqt, k0 + 8)
                P.dma("sp", Vt[:, k0:k1, :], D["VNA"][t0 + k0 * 128:t0 + k1 * 128, r0:r0 + 128].rearrange("(k p) d -> p k d", p=128),
                      reads=[B["VNA"]], writes=[bV])
            for j in range(nqt):
                if lat:
                    chunks = [("l", m, ty) for (m, ty) in self.plan[j]] + [("c", k, None) for k in range(PC)]
                else:
                    chunks = [("l", k, None) for k in range(nqt)]
                n = len(chunks)
                assert n <= 8
                sb0 = (it % 2) * 2
                S2 = ps[:, sb0:sb0 + 2, :].rearrange("p b f -> p (b f)")
                sbufs = [self.psb[sb0], self.psb[sb0 + 1]]
                q_ap = QT[:, j * 128:(j + 1) * 128]
                for i, (kind, kc, ty) in enumerate(chunks):
                    k_ap = KT[:, kc * 128:(kc + 1) * 128] if kind == "l" else M["KcNA"][:, h, kc * 128:(kc + 1) * 128]
                    P.add("pe", lambda e, i=i, k_ap=k_ap, ty=ty, S2=S2, q_ap=q_ap: e.matmul(
                        S2[:, i * 128:(i + 1) * 128], lhsT=k_ap, rhs=q_ap, start=True, stop=(ty is None)),
                        reads=[bK, bQ, M["b"]], writes=sbufs)
                    if ty is not None:
                        P.add("pe", lambda e, i=i, ty=ty, S2=S2: e.matmul(
                            S2[:, i * 128:(i + 1) * 128], lhsT=self.identF, rhs=bias[:, ty, :], start=False, stop=True),
                            reads=[bB, self.b_const], writes=sbufs)
                pi = it % 2
                pT = pTs[pi]
                P.add("act", lambda e, pT=pT, S2=S2, n=n: e.activation(out=pT[:, 0:n * 128], in_=S2[:, 0:n * 128],
                                                                      func=AF.Exp, scale=scale),
                      writes=sbufs + [bP[pi]])
                ob = 4 + (it % 2)
                for i, (kind, kc, ty) in enumerate(chunks):
                    v_ap = Vt[:, kc, :] if kind == "l" else M["VcNA"][:, kc, h * 128:(h + 1) * 128]
                    P.add("pe", lambda e, i=i, v_ap=v_ap, pT=pT, ob=ob, n=n: e.matmul(
                        ps[:, ob, 0:128], lhsT=v_ap, rhs=pT[:, i * 128:(i + 1) * 128], start=(i == 0), stop=(i == n - 1)),
                        reads=[bV, bP[pi], M["b"]], writes=[self.psb[ob]])
                for i in range(n):
                    P.add("pe", lambda e, i=i, pT=pT, ob=ob, n=n: e.matmul(
                        ps[:, ob, 128:256], lhsT=self.onesB, rhs=pT[:, i * 128:(i + 1) * 128], start=(i == 0), stop=(i == n - 1)),
                        reads=[bP[pi], self.b_const], writes=[self.psb[ob]])
                P.add("dve", lambda e, ob=ob: e.reciprocal(out=rden, in_=ps[:, ob, 128:256]), writes=[self.psb[ob], bR])
                P.add("dve", lambda e, ob=ob, j=j: e.tensor_tensor(out=ost[:, j * 128:(j + 1) * 128], in0=ps[:, ob, 0:128],
                                                                  in1=rden, op=ALU.mult),
                      reads=[bR], writes=[self.psb[ob], bO])
                it += 1
            P.dma("sp", D["ONA"][r0:r0 + 128, t0:t0 + T], ost[:, 0:T], reads=[bO], writes=[B["ONA"]], pool="st")

    def diff_head(self, l, h):
        c, P, ps, M = self.cfg, self.P, self.ps, self.M
        D, B = self.dram, self.dbuf
        scale = HEAD_DIM ** -0.5
        PC = c.PAST // 128
        Tmax = c.TL
        QT, KT, bQ, bK = [], [], Buf("dQT"), Buf("dKT")
        for m in range(2):
            o = self.alloc(Tmax // 2); QT.append(self.bf16v(o, Tmax))
            o = self.alloc(Tmax // 2); KT.append(self.bf16v(o, Tmax))
        o = self.alloc(Tmax); Vt = self.bf16v(o, Tmax * 2).rearrange("p (k d) -> p k d", d=256); bV = Buf("dVt")
        osts, bO = [], Buf("dost")
        for d_ in range(2):
            o = self.alloc(Tmax // 2); osts.append(self.bf16v(o, Tmax))
        NKmax = (c.TL + c.PAST) // 128
        pTs, bP = [], []
        for i in range(2):
            o = self.alloc(NKmax * 256); pTs.append(self.bf16v(o, NKmax * 512).rearrange("p (k q) -> p k q", q=512)); bP.append(Buf("dpT%d" % i))
        o = self.alloc(512); rden = self.f32v(o, 512); bR = Buf("drden")
        od, bod = [], Buf("od")
        for d_ in range(2):
            o = self.alloc(256); od.append(self.f32v(o, 256))
        o = self.alloc(256); t1 = self.f32v(o, 256); bt1 = Buf("dt1")
        sq, bsq = [], []
        for d_ in range(2):
            o = self.alloc(256); sq.append(self.f32v(o, 256)); bsq.append(Buf("dsq%d" % d_))
        o = self.alloc(256); rstd = self.f32v(o, 256); brs = Buf("drstd")
        it = 0
        qit = 0
        for (t0, T, lat) in self.seqs():
            nkl = T // 128
            QB = min(256, T)
            for m in range(2):
                r0 = (h * 2 + m) * 128
                P.dma("sp", QT[m][:, 0:T], D["DQ"][r0:r0 + 128, t0:t0 + T], reads=[B["DQ"]], writes=[bQ])
                P.dma("sp", KT[m][:, 0:T], D["DK"][r0:r0 + 128, t0:t0 + T], reads=[B["DK"]], writes=[bK])
            for k0 in range(0, nkl, 8):
                k1 = min(nkl, k0 + 8)
                P.dma("sp", Vt[:, k0:k1, :], D["DV"][t0 + k0 * 128:t0 + k1 * 128, h * 256:(h + 1) * 256].rearrange("(k p) d -> p k d", p=128),
                      reads=[B["DV"]], writes=[bV])
            chunks = [("l", k) for k in range(nkl)] + ([("c", k) for k in range(PC)] if lat else [])
            n = len(chunks)
            for qb in range(T // QB):
                acc = (2, 3, 4) if qit % 2 == 0 else (5, 6, 7)
                accb = [self.psb[a] for a in acc]
                pi = qit % 2
                pT = pTs[pi]
                for i, (kind, kc) in enumerate(chunks):
                    sbk = it % 2
                    for m in range(2):
                        k_ap = KT[m][:, kc * 128:(kc + 1) * 128] if kind == "l" else M["KcD"][:, h * 2 + m, kc * 128:(kc + 1) * 128]
                        P.add("pe", lambda e, m=m, k_ap=k_ap, sbk=sbk, qb=qb, QB=QB: e.matmul(
                            ps[:, sbk, m * QB:(m + 1) * QB], lhsT=k_ap, rhs=QT[m][:, qb * QB:(qb + 1) * QB], start=True, stop=True),
                            reads=[bK, bQ, M["b"]], writes=[self.psb[sbk]])
                    P.add("act", lambda e, pT=pT, sbk=sbk, QB=QB, i=i: e.activation(out=pT[:, i, 0:2 * QB], in_=ps[:, sbk, 0:2 * QB],
                                                                                   func=AF.Exp, scale=scale),
                          writes=[self.psb[sbk], bP[pi]])
                    it += 1
                for m in range(2):
                    for d_ in range(2):
                        for i, (kind, kc) in enumerate(chunks):
                            v_ap = Vt[:, kc, d_ * 128:(d_ + 1) * 128] if kind == "l" else M["VcD"][:, kc, h * 256 + d_ * 128:h * 256 + (d_ + 1) * 128]
                            P.add("pe", lambda e, m=m, d_=d_, v_ap=v_ap, pT=pT, i=i, acc=acc, QB=QB, n=n: e.matmul(
                                ps[:, acc[d_], m * QB:(m + 1) * QB], lhsT=v_ap, rhs=pT[:, i, m * QB:(m + 1) * QB],
                                start=(i == 0), stop=(i == n - 1)),
                                reads=[bV, bP[pi], M["b"]], writes=[accb[d_]])
                    for i in range(n):
                        P.add("pe", lambda e, m=m, pT=pT, i=i, acc=acc, QB=QB, n=n: e.matmul(
                            ps[:, acc[2], m * QB:(m + 1) * QB], lhsT=self.onesB, rhs=pT[:, i, m * QB:(m + 1) * QB],
                            start=(i == 0), stop=(i == n - 1)),
                            reads=[bP[pi], self.b_const], writes=[accb[2]])
                P.add("dve", lambda e, acc=acc, QB=QB: e.reciprocal(out=rden[:, 0:2 * QB], in_=ps[:, acc[2], 0:2 * QB]),
                      writes=[accb[2], bR])
                for d_ in range(2):
                    P.add("dve", lambda e, d_=d_, acc=acc, QB=QB: e.tensor_tensor(out=od[d_][:, 0:QB], in0=ps[:, acc[d_], 0:QB],
                                                                                 in1=rden[:, 0:QB], op=ALU.mult),
                          reads=[bR], writes=[accb[d_], bod])
                    P.add("dve", lambda e, d_=d_, acc=acc, QB=QB: e.tensor_tensor(out=t1[:, 0:QB], in0=ps[:, acc[d_], QB:2 * QB],
                                                                                 in1=rden[:, QB:2 * QB], op=ALU.mult),
                          reads=[bR], writes=[accb[d_], bt1])
                    P.add("dve", lambda e, d_=d_, QB=QB: e.scalar_tensor_tensor(out=od[d_][:, 0:QB], in0=t1[:, 0:QB], scalar=M["nlam"][:, 0:1],
                                                                               in1=od[d_][:, 0:QB], op0=ALU.mult, op1=ALU.add),
                          reads=[bt1, M["b"]], writes=[bod])
                    P.add("act", lambda e, d_=d_, QB=QB: e.activation(out=sq[d_][:, 0:QB], in_=od[d_][:, 0:QB], func=AF.Square),
                          reads=[bod], writes=[bsq[d_]])
                nb = it % 2
                for d_ in range(2):
                    P.add("pe", lambda e, d_=d_, nb=nb, QB=QB: e.matmul(ps[:, nb, 0:QB], lhsT=self.onesF, rhs=sq[d_][:, 0:QB],
                                                                       start=(d_ == 0), stop=(d_ == 1)),
                          reads=[bsq[d_], self.b_const], writes=[self.psb[nb]])
                it += 1
                P.add("act", lambda e, nb=nb, QB=QB: e.activation(out=rstd[:, 0:QB], in_=ps[:, nb, 0:QB], func=AF.Ln, bias=self.epsT,
                                                                 scale=1.0 / 256.0), reads=[self.b_const], writes=[self.psb[nb], brs])
                P.add("act", lambda e, QB=QB: e.activation(out=rstd[:, 0:QB], in_=rstd[:, 0:QB], func=AF.Exp, scale=-0.5), writes=[brs])
                for d_ in range(2):
                    P.add("dve", lambda e, d_=d_, qb=qb, QB=QB: e.scalar_tensor_tensor(
                        out=osts[d_][:, qb * QB:(qb + 1) * QB], in0=od[d_][:, 0:QB], scalar=M["dnw"][:, d_:d_ + 1], in1=rstd[:, 0:QB],
                        op0=ALU.mult, op1=ALU.mult), reads=[bod, brs, M["b"]], writes=[bO])
                qit += 1
            for d_ in range(2):
                r0 = h * 256 + d_ * 128
                P.dma("sp", D["OD"][r0:r0 + 128, t0:t0 + T], osts[d_][:, 0:T], reads=[bO], writes=[B["OD"]], pool="st")

    def gdn(self, l):
        self.gdn_prep(l)
        self.P.barrier()
        self.aoff = self.gdn_mark
        self.gdn_scan(l)

    def bank1(self):
        b = self.g_b1 % 8
        self.g_b1 += 1
        return b

    def bank2(self):
        b = (self.g_b2 % 4) * 2
        self.g_b2 += 1
        return b

    def gdn_prep(self, l):
        c, P, ps = self.cfg, self.P, self.ps
        D, B = self.dram, self.dbuf
        self.g_b1, self.g_b2 = 0, 0
        self.gdn_mark = self.aoff
        G = {}
        self.G = G
        G["b"] = Buf("gconst")
        o = self.alloc(72); cw = self.f32v(o, 72).rearrange("p (j k) -> p j k", j=3)
        for j in range(3):
            P.dma("sp", cw[:, j, :], D["gdn_conv"][l, j].rearrange("(k p) -> p k", p=128), writes=[G["b"]],
                  allow_slow_non_contiguous=True)
        o = self.alloc(16); dtb = self.f32v(o, 16)
        o = self.alloc(16); nea = self.f32v(o, 16)
        o = self.alloc(128); G["gnw"] = self.f32v(o, 128)
        P.dma("sp", dtb, D["gdn_dt_bias"][l:l + 1].rearrange("o d h -> o (d h)").partition_broadcast(128), writes=[G["b"]])
        P.dma("sp", nea, D["gdn_a_log"][l:l + 1].rearrange("o d h -> o (d h)").partition_broadcast(128), writes=[G["b"]])
        P.dma("sp", G["gnw"], D["gdn_norm"][l:l + 1, :].partition_broadcast(128), writes=[G["b"]])
        P.add("act", lambda e: e.activation(out=nea, in_=nea, func=AF.Exp), writes=[G["b"]])
        P.add("dve", lambda e: e.tensor_scalar(out=nea, in0=nea, scalar1=-1.0, scalar2=None, op0=ALU.mult), writes=[G["b"]])
        o = self.alloc(8); oneT = self.f32v(o, 1)
        P.add("dve", lambda e: e.memset(oneT, 1.0), writes=[G["b"]])
        self.gdn_mark = self.aoff
        NT = 512
        xh, bxh = [], []
        for i in range(2):
            o = self.alloc(NT + 8); xh.append(self.f32v(o, NT + 2)); bxh.append(Buf("xh%d" % i))
        ts_, bts = [], []
        for i in range(2):
            o = self.alloc(NT); ts_.append(self.f32v(o, NT)); bts.append(Buf("gt%d" % i))
        ys, bys = [], []
        for i in range(2):
            o = self.alloc(NT); ys.append(self.f32v(o, NT)); bys.append(Buf("ys%d" % i))
        sqs, bsqs, rinvs, bris = [], [], [], []
        for i in range(2):
            o = self.alloc(NT); sqs.append(self.f32v(o, NT)); bsqs.append(Buf("gsq%d" % i))
            o = self.alloc(NT); rinvs.append(self.f32v(o, NT)); bris.append(Buf("grinv%d" % i))
        yn, byn = [], []
        for i in range(2):
            o = self.alloc(NT); yn.append(self.f32v(o, NT)); byn.append(Buf("yn%d" % i))
        tst, btst = [], []
        for i in range(2):
            o = self.alloc(128); tst.append(self.f32v(o, 128)); btst.append(Buf("tst%d" % i))
        ab, bab = [], []
        for i in range(2):
            o = self.alloc(32); ab.append(self.f32v(o, 32)); bab.append(Buf("ab%d" % i))
        o = self.alloc(16); xe = self.f32v(o, 16); bxe = Buf("xe")
        it = 0
        for (t0, T, lat) in self.seqs():
            for b0 in range(0, T, NT):
                nt = min(NT, T - b0)
                for fc in range(24):
                    kind = fc // 8
                    src = ("GQ", "GK", "GV")[kind]
                    r0 = (fc % 8) * 128
                    xi = it % 2
                    x_ = xh[xi]
                    t, bt, sq, bsq, rinv, bri = ts_[xi], bts[xi], sqs[xi], bsqs[xi], rinvs[xi], bris[xi]
                    lo = max(0, b0 - 1)
                    hi = min(T, b0 + nt + 1)
                    off = lo - (b0 - 1)
                    if b0 == 0:
                        P.add("pool", lambda e, x_=x_: e.memset(x_[:, 0:1], 0.0), writes=[bxh[xi]])
                    if b0 + nt == T:
                        P.add("pool", lambda e, x_=x_, nt=nt: e.memset(x_[:, nt + 1:nt + 2], 0.0), writes=[bxh[xi]])
                    P.dma("sp", x_[:, off:off + hi - lo], D[src][r0:r0 + 128, t0 + lo:t0 + hi], reads=[B[src]], writes=[bxh[xi]])
                    P.add("dve", lambda e, x_=x_, nt=nt, fc=fc, t=t: e.tensor_scalar(out=t[:, 0:nt], in0=x_[:, 0:nt], scalar1=cw[:, 0, fc:fc + 1],
                                                                                 scalar2=None, op0=ALU.mult),
                          reads=[bxh[xi], G["b"]], writes=[bt])
                    P.add("dve", lambda e, x_=x_, nt=nt, fc=fc, t=t: e.scalar_tensor_tensor(out=t[:, 0:nt], in0=x_[:, 1:nt + 1], scalar=cw[:, 1, fc:fc + 1],
                                                                                        in1=t[:, 0:nt], op0=ALU.mult, op1=ALU.add),
                          reads=[bxh[xi], G["b"]], writes=[bt])
                    P.add("dve", lambda e, x_=x_, nt=nt, fc=fc, t=t: e.scalar_tensor_tensor(out=t[:, 0:nt], in0=x_[:, 2:nt + 2], scalar=cw[:, 2, fc:fc + 1],
                                                                                        in1=t[:, 0:nt], op0=ALU.mult, op1=ALU.add),
                          reads=[bxh[xi], G["b"]], writes=[bt])
                    y_ = ys[xi]
                    P.add("act", lambda e, y_=y_, nt=nt, t=t: e.activation(out=y_[:, 0:nt], in_=t[:, 0:nt], func=AF.Silu),
                          reads=[bt], writes=[bys[xi]])
                    fin, bfin = y_, bys[xi]
                    if kind < 2:
                        P.add("act", lambda e, y_=y_, nt=nt, sq=sq: e.activation(out=sq[:, 0:nt], in_=y_[:, 0:nt], func=AF.Square),
                              reads=[bys[xi]], writes=[bsq])
                        bk = self.bank1()
                        P.add("pe", lambda e, bk=bk, nt=nt, sq=sq: e.matmul(ps[:, bk, 0:nt], lhsT=self.onesF, rhs=sq[:, 0:nt], start=True, stop=True),
                              reads=[bsq, self.b_const], writes=[self.psb[bk]])
                        P.add("act", lambda e, bk=bk, nt=nt, rinv=rinv: e.activation(out=rinv[:, 0:nt], in_=ps[:, bk, 0:nt], func=AF.Ln, bias=self.epsT, scale=1.0),
                              reads=[self.b_const], writes=[self.psb[bk], bri])
                        P.add("act", lambda e, nt=nt, rinv=rinv: e.activation(out=rinv[:, 0:nt], in_=rinv[:, 0:nt], func=AF.Exp, scale=-0.5), writes=[bri])
                        n_ = yn[xi]
                        sc = HEAD_DIM ** -0.5 if kind == 0 else 1.0
                        P.add("dve", lambda e, n_=n_, y_=y_, nt=nt, sc=sc, rinv=rinv: e.scalar_tensor_tensor(out=n_[:, 0:nt], in0=y_[:, 0:nt], scalar=sc, in1=rinv[:, 0:nt],
                                                                                                op0=ALU.mult, op1=ALU.mult),
                              reads=[bys[xi], bri], writes=[byn[xi]])
                        fin, bfin = n_, byn[xi]
                        dst = "GQN" if kind == 0 else "GKN"
                        P.dma("sp", D[dst][r0:r0 + 128, t0 + b0:t0 + b0 + nt], fin[:, 0:nt], reads=[bfin], writes=[B[dst]], pool="st")
                    if kind >= 1:
                        dstT = "GKT" if kind == 1 else "GVT"
                        for tt in range(nt // 128 if nt >= 128 else 1):
                            w_ = min(128, nt)
                            bk = self.bank1()
                            P.add("pe", lambda e, bk=bk, fin=fin, tt=tt, w_=w_: e.transpose(out=ps[0:w_, bk, 0:128], in_=fin[:, tt * 128:tt * 128 + w_],
                                                                                             identity=self.identF),
                                  reads=[bfin, self.b_const], writes=[self.psb[bk]])
                            si = (it + tt) % 2
                            P.add("act", lambda e, bk=bk, si=si, w_=w_: e.copy(out=tst[si][0:w_, :], in_=ps[0:w_, bk, 0:128]),
                                  writes=[self.psb[bk], btst[si]])
                            tok = t0 + b0 + tt * 128
                            P.dma("sp", D[dstT][tok:tok + w_, r0:r0 + 128], tst[si][0:w_, :], reads=[btst[si]], writes=[B[dstT]], pool="st")
                    it += 1
                for tt in range(max(1, nt // 128)):
                    w_ = min(128, nt)
                    tok = t0 + b0 + tt * 128
                    ai = (it + tt) % 2
                    a_ = ab[ai]
                    P.dma("sp", a_[0:w_, :], D["GAB"][tok:tok + w_, :], reads=[B["GAB"]], writes=[bab[ai]])
                    P.add("act", lambda e, a_=a_, w_=w_: e.activation(out=a_[0:w_, 0:16], in_=a_[0:w_, 0:16], func=AF.Sigmoid), writes=[bab[ai]])
                    P.add("dve", lambda e, a_=a_, w_=w_: e.tensor_tensor(out=xe[0:w_, :], in0=a_[0:w_, 16:32], in1=dtb[0:w_, :], op=ALU.add),
                          reads=[bab[ai], G["b"]], writes=[bxe])
                    P.add("act", lambda e, w_=w_: e.activation(out=xe[0:w_, :], in_=xe[0:w_, :], func=AF.Exp), writes=[bxe])
                    P.add("act", lambda e, w_=w_: e.activation(out=xe[0:w_, :], in_=xe[0:w_, :], func=AF.Ln, bias=oneT[0:w_, :], scale=1.0),
                          reads=[G["b"]], writes=[bxe])
                    P.add("dve", lambda e, a_=a_, w_=w_: e.tensor_tensor(out=a_[0:w_, 16:32], in0=xe[0:w_, :], in1=nea[0:w_, :], op=ALU.mult),
                          reads=[bxe, G["b"]], writes=[bab[ai]])
                    P.dma("sp", D["GAB2"][tok:tok + w_, :], a_[0:w_, :], reads=[bab[ai]], writes=[B["GAB2"]], pool="st")
                it += 1

    def gdn_scan(self, l):
        c, P, ps = self.cfg, self.P, self.ps
        D, B, G = self.dram, self.dbuf, self.G
        C = GDN_CHUNK
        NCH = 2
        H = GDN_HEADS // NCH

        def f3(n, inner):
            o = self.alloc(n)
            return self.f32v(o, n).rearrange("p (h x) -> p h x", x=inner)

        def f2(n):
            return self.f32v(self.alloc(n), n)

        o = self.alloc(6 * 64); msk = self.f32v(o, 384).rearrange("p (m j) -> p m j", m=6)
        o = self.alloc(2 * 64); tri = self.f32v(o, 128).rearrange("p (m j) -> p m j", m=2)
        P.dma("sp", msk[0:64], D["k_masks"].rearrange("m i j -> i m j"), writes=[G["b"]])
        P.dma("sp", tri[0:64], D["k_tri"].rearrange("m i j -> i m j"), writes=[G["b"]])
        id64 = self.identF[0:64, 0:64]

        def bc_h(ap2):
            return ap2.unsqueeze(1).to_broadcast([ap2.shape[0], H, ap2.shape[1]])

        def bc_x(ap2, n):
            return ap2.unsqueeze(2).to_broadcast([ap2.shape[0], H, n])

        def make_chain(ch):
            h0 = ch * H
            bank_lo = ch * 4
            bstate = {"i": 0}

            def bank1():
                b_ = bank_lo + bstate["i"] % 4
                bstate["i"] += 1
                return b_
            gofb, ogb, ostb = Buf("gof%d" % ch), Buf("og%d" % ch), Buf("ost%d" % ch)
            LD = []
            for i in range(2):
                d = {"Kf": f3(H * 64, 64), "Qf": f3(H * 64, 64), "Kt": f3(H * 128, 128), "Vt": f3(H * 128, 128), "ab": f2(32), "b": Buf("gld%d_%d" % (ch, i))}
                LD.append(d)
            T_ = {k: f3(H * 64, 64) for k in ("R", "Rb", "D", "E", "ET", "EM", "ETM", "ETI", "X", "XT", "Y0", "Y1", "YT0", "YT1", "nBb")}
            T_["gam"] = f2(8); T_["tot"] = f2(8); T_["kdsc"] = f2(8); T_["bes"] = f2(8); T_["nbeta"] = f2(8)
            T_["bv"] = f3(H * 128, 128); T_["Rk"] = f3(H * 128, 128)
            Tb = {k: Buf("g%d_" % ch + k) for k in T_}
            OUT = []
            for i in range(2):
                d = {"TT": f3(H * 64, 64), "pT": f3(H * 64, 64), "u": f3(H * 128, 128), "wkT": f3(H * 64, 64), "kd": f3(H * 128, 128),
                     "egam": f2(8), "gl": f2(8)}
                d["b"] = {k: Buf("go%d_%s" % (i, k)) for k in d}
                OUT.append(d)
            S_ = f3(H * 128, 128); bS = Buf("gS%d" % ch)
            w_ = f3(H * 128, 128); bw = Buf("gw%d" % ch)
            zs = f3(H * 128, 128); bzs = Buf("gzs%d" % ch)
            ot = f3(H * 128, 128); bot = Buf("got%d" % ch)
            of_ = f3(H * 128, 128); bof = Buf("gofl%d" % ch)
            gz = f3(H * 128, 128); bgz = Buf("ggz%d" % ch)
            tmp = f3(H * 128, 128); btmp = Buf("gtmp%d" % ch)
            ssq = f2(8); bssq = Buf("gssq%d" % ch)
            o = self.alloc(H * 32); ogT = self.bf16v(o, H * 64).rearrange("p (h t) -> p h t", t=64); bogT = Buf("gogT%d" % ch)

            def psv(bank, parts, n, inner):
                return ps[0:parts, bank, 0:n].rearrange("p (h x) -> p h x", x=inner)

            def ps2v(bank, parts, inner):
                return ps[0:parts, bank:bank + 2, :].rearrange("p b f -> p (b f)").rearrange("p (h x) -> p h x", x=inner)

            def prep(t0c, dr, li, oi):
                L_, O_ = LD[li], OUT[oi]
                ob = O_["b"]
                P.dma("sp", L_["Kf"], D["GKN"][h0 * 128:(h0 + H) * 128, t0c:t0c + C].rearrange("(h p) t -> p h t", p=128), reads=[B["GKN"]], writes=[L_["b"]])
                P.dma("sp", L_["Qf"], D["GQN"][h0 * 128:(h0 + H) * 128, t0c:t0c + C].rearrange("(h p) t -> p h t", p=128), reads=[B["GQN"]], writes=[L_["b"]])
                P.dma("sp", L_["Kt"][0:C], D["GKT"][t0c:t0c + C, h0 * 128:(h0 + H) * 128].rearrange("t (h d) -> t h d", d=128), reads=[B["GKT"]], writes=[L_["b"]])
                P.dma("sp", L_["Vt"][0:C], D["GVT"][t0c:t0c + C, h0 * 128:(h0 + H) * 128].rearrange("t (h d) -> t h d", d=128), reads=[B["GVT"]], writes=[L_["b"]])
                P.dma("sp", L_["ab"][0:C], D["GAB2"][t0c:t0c + C, :], reads=[B["GAB2"]], writes=[L_["b"]])
                beta = L_["ab"][0:C, dr * 8 + h0:dr * 8 + h0 + H]
                la = L_["ab"][0:C, 16 + dr * 8 + h0:16 + dr * 8 + h0 + H]
                gam, tot, kdsc, bes, nbeta = (T_[k][0:C, 0:H] for k in ("gam", "tot", "kdsc", "bes", "nbeta"))
                b0 = bank1()
                P.add("pe", lambda e: e.matmul(ps[0:C, b0, 0:H], lhsT=tri[0:C, dr, :], rhs=la, start=True, stop=True),
                      reads=[L_["b"], G["b"]], writes=[self.psb[b0]])
                P.add("pe", lambda e: e.matmul(ps[:, b0, 8:8 + H], lhsT=self.onesF[0:C, :], rhs=la, start=True, stop=True),
                      reads=[L_["b"], self.b_const], writes=[self.psb[b0]])
                P.add("act", lambda e: e.copy(out=gam, in_=ps[0:C, b0, 0:H]), writes=[self.psb[b0], Tb["gam"]])
                P.add("act", lambda e: e.activation(out=O_["egam"][0:C, 0:H], in_=ps[0:C, b0, 0:H], func=AF.Exp), writes=[self.psb[b0], ob["egam"]])
                P.add("act", lambda e: e.activation(out=O_["gl"][:, 0:H], in_=ps[:, b0, 8:8 + H], func=AF.Exp), writes=[self.psb[b0], ob["gl"]])
                P.add("act", lambda e: e.copy(out=tot, in_=ps[0:C, b0, 8:8 + H]), writes=[self.psb[b0], Tb["tot"]])
                P.add("dve", lambda e: e.tensor_tensor(out=kdsc, in0=tot, in1=gam, op=ALU.subtract), reads=[Tb["tot"], Tb["gam"]], writes=[Tb["kdsc"]])
                P.add("act", lambda e: e.activation(out=kdsc, in_=kdsc, func=AF.Exp), writes=[Tb["kdsc"]])
                P.add("dve", lambda e: e.tensor_tensor(out=bes, in0=beta, in1=O_["egam"][0:C, 0:H], op=ALU.mult), reads=[L_["b"], ob["egam"]], writes=[Tb["bes"]])
                P.add("dve", lambda e: e.tensor_scalar(out=nbeta, in0=beta, scalar1=-1.0, scalar2=None, op0=ALU.mult), reads=[L_["b"]], writes=[Tb["nbeta"]])
                R, Rb = T_["R"][0:C], T_["Rb"][0:C]
                P.add("dve", lambda e: e.tensor_tensor(out=R, in0=bc_h(id64), in1=bc_x(gam, C), op=ALU.mult), reads=[Tb["gam"], self.b_const], writes=[Tb["R"]])
                P.add("pool", lambda e: e.tensor_tensor(out=Rb, in0=bc_h(id64), in1=bc_x(nbeta, C), op=ALU.mult), reads=[Tb["nbeta"], self.b_const], writes=[Tb["Rb"]])
                bG, bBb = bank1(), bank1()
                P.add("pe", lambda e: e.matmul(ps[0:C, bG, 0:H * 64], lhsT=self.onesF[0:C, 0:C], rhs=R.rearrange("p h x -> p (h x)"), start=True, stop=True),
                      reads=[Tb["R"], self.b_const], writes=[self.psb[bG]])
                P.add("pe", lambda e: e.matmul(ps[0:C, bBb, 0:H * 64], lhsT=self.onesF[0:C, 0:C], rhs=Rb.rearrange("p h x -> p (h x)"), start=True, stop=True),
                      reads=[Tb["Rb"], self.b_const], writes=[self.psb[bBb]])
                Dm, E, ET, EM, ETM, ETI, nBb = (T_[k][0:C] for k in ("D", "E", "ET", "EM", "ETM", "ETI", "nBb"))
                Gb = psv(bG, C, H * 64, 64)
                P.add("dve", lambda e: e.tensor_tensor(out=Dm, in0=bc_x(gam, C), in1=Gb, op=ALU.subtract), reads=[Tb["gam"]], writes=[self.psb[bG], Tb["D"]])
                P.add("dve", lambda e: e.tensor_scalar(out=Dm, in0=Dm, scalar1=0.0, scalar2=None, op0=ALU.min), writes=[Tb["D"]])
                P.add("act", lambda e: e.activation(out=E, in_=Dm, func=AF.Exp), reads=[Tb["D"]], writes=[Tb["E"]])
                P.add("dve", lambda e: e.tensor_tensor(out=ET, in0=Gb, in1=bc_x(gam, C), op=ALU.subtract), reads=[Tb["gam"]], writes=[self.psb[bG], Tb["ET"]])
                P.add("dve", lambda e: e.tensor_scalar(out=ET, in0=ET, scalar1=0.0, scalar2=None, op0=ALU.min), writes=[Tb["ET"]])
                P.add("act", lambda e: e.activation(out=ET, in_=ET, func=AF.Exp), writes=[Tb["ET"]])
                P.add("act", lambda e: e.copy(out=nBb, in_=psv(bBb, C, H * 64, 64)), writes=[self.psb[bBb], Tb["nBb"]])
                m0 = dr * 3
                P.add("pool", lambda e: e.tensor_tensor(out=EM, in0=E, in1=bc_h(msk[0:C, m0, :]), op=ALU.mult), reads=[Tb["E"], G["b"]], writes=[Tb["EM"]])
                P.add("pool", lambda e: e.tensor_tensor(out=ETM, in0=ET, in1=bc_h(msk[0:C, m0 + 1, :]), op=ALU.mult), reads=[Tb["ET"], G["b"]], writes=[Tb["ETM"]])
                P.add("pool", lambda e: e.tensor_tensor(out=ETI, in0=ET, in1=bc_h(msk[0:C, m0 + 2, :]), op=ALU.mult), reads=[Tb["ET"], G["b"]], writes=[Tb["ETI"]])
                bkk, bqk = bank1(), bank1()
                for h in range(H):
                    P.add("pe", lambda e, h=h: e.matmul(ps[0:C, bkk, h * 64:(h + 1) * 64], lhsT=L_["Kf"][:, h, :], rhs=L_["Kf"][:, h, :], start=True, stop=True),
                          reads=[L_["b"]], writes=[self.psb[bkk]])
                for h in range(H):
                    P.add("pe", lambda e, h=h: e.matmul(ps[0:C, bqk, h * 64:(h + 1) * 64], lhsT=L_["Kf"][:, h, :], rhs=L_["Qf"][:, h, :], start=True, stop=True),
                          reads=[L_["b"]], writes=[self.psb[bqk]])
                X, XT = T_["X"][0:C], T_["XT"][0:C]
                kkv, qkv = psv(bkk, C, H * 64, 64), psv(bqk, C, H * 64, 64)
                P.add("dve", lambda e: e.tensor_tensor(out=X, in0=EM, in1=kkv, op=ALU.mult), reads=[Tb["EM"]], writes=[self.psb[bkk], Tb["X"]])
                P.add("dve", lambda e: e.tensor_tensor(out=X, in0=X, in1=bc_x(nbeta, C), op=ALU.mult), reads=[Tb["nbeta"]], writes=[Tb["X"]])
                P.add("dve", lambda e: e.tensor_tensor(out=XT, in0=ETM, in1=kkv, op=ALU.mult), reads=[Tb["ETM"]], writes=[self.psb[bkk], Tb["XT"]])
                P.add("dve", lambda e: e.tensor_tensor(out=XT, in0=XT, in1=nBb, op=ALU.mult), reads=[Tb["nBb"]], writes=[Tb["XT"]])
                pT = O_["pT"][0:C]
                P.add("dve", lambda e: e.tensor_tensor(out=pT, in0=ETI, in1=qkv, op=ALU.mult), reads=[Tb["ETI"]], writes=[self.psb[bqk], ob["pT"]])
                TT = O_["TT"][0:C]
                P.add("pool", lambda e: e.tensor_tensor(out=TT, in0=XT, in1=bc_h(id64), op=ALU.add), reads=[Tb["XT"], self.b_const], writes=[ob["TT"]])
                Y, YT, bY, bYT = X, XT, Tb["X"], Tb["XT"]
                for k in range(1, 6):
                    Yn, bYn = T_["Y%d" % (k % 2)][0:C], Tb["Y%d" % (k % 2)]
                    YTn, bYTn = T_["YT%d" % (k % 2)][0:C], Tb["YT%d" % (k % 2)]
                    ba = bank1()
                    for h in range(H):
                        P.add("pe", lambda e, h=h, ba=ba, Y=Y, YT=YT: e.matmul(ps[0:C, ba, h * 64:(h + 1) * 64], lhsT=YT[:, h, :], rhs=Y[:, h, :], start=True, stop=True),
                              reads=[bY, bYT], writes=[self.psb[ba]])
                    if k < 5:
                        bb_ = bank1()
                        for h in range(H):
                            P.add("pe", lambda e, h=h, bb_=bb_, Y=Y, YT=YT: e.matmul(ps[0:C, bb_, h * 64:(h + 1) * 64], lhsT=Y[:, h, :], rhs=YT[:, h, :], start=True, stop=True),
                                  reads=[bY, bYT], writes=[self.psb[bb_]])
                    P.add("act", lambda e, Yn=Yn, ba=ba: e.copy(out=Yn, in_=psv(ba, C, H * 64, 64)), writes=[self.psb[ba], bYn])
                    if k < 5:
                        P.add("act", lambda e, YTn=YTn, bb_=bb_: e.copy(out=YTn, in_=psv(bb_, C, H * 64, 64)), writes=[self.psb[bb_], bYTn])
                    bc_ = bank1()
                    for h in range(H):
                        P.add("pe", lambda e, h=h, bc_=bc_, Yn=Yn: e.matmul(ps[0:C, bc_, h * 64:(h + 1) * 64], lhsT=Yn[:, h, :], rhs=TT[:, h, :], start=True, stop=True),
                              reads=[bYn, ob["TT"]], writes=[self.psb[bc_]])
                    P.add("dve", lambda e, bc_=bc_: e.tensor_tensor(out=TT, in0=TT, in1=psv(bc_, C, H * 64, 64), op=ALU.add), writes=[self.psb[bc_], ob["TT"]])
                    Y, YT, bY, bYT = Yn, YTn, bYn, bYTn
                bv, Rk, kd = T_["bv"][0:C], T_["Rk"][0:C], O_["kd"][0:C]
                P.add("pool", lambda e: e.tensor_tensor(out=bv, in0=L_["Vt"][0:C], in1=bc_x(beta, 128), op=ALU.mult), reads=[L_["b"]], writes=[Tb["bv"]])
                P.add("pool", lambda e: e.tensor_tensor(out=Rk, in0=L_["Kt"][0:C], in1=bc_x(bes, 128), op=ALU.mult), reads=[L_["b"], Tb["bes"]], writes=[Tb["Rk"]])
                P.add("pool", lambda e: e.tensor_tensor(out=kd, in0=L_["Kt"][0:C], in1=bc_x(kdsc, 128), op=ALU.mult), reads=[L_["b"], Tb["kdsc"]], writes=[ob["kd"]])
                bu = bank1()
                uv = psv(bu, C, H * 128, 128)
                for h in range(H):
                    P.add("pe", lambda e, h=h: e.matmul(uv[:, h, :], lhsT=TT[:, h, :], rhs=bv[:, h, :], start=True, stop=True),
                          reads=[ob["TT"], Tb["bv"]], writes=[self.psb[bu]])
                P.add("act", lambda e: e.copy(out=O_["u"][0:C], in_=uv), writes=[self.psb[bu], ob["u"]])
                bwk = bank1()
                for h in range(H):
                    P.add("pe", lambda e, h=h: e.matmul(ps[:, bwk, h * 64:(h + 1) * 64], lhsT=Rk[:, h, :], rhs=TT[:, h, :], start=True, stop=True),
                          reads=[ob["TT"], Tb["Rk"]], writes=[self.psb[bwk]])
                P.add("act", lambda e: e.copy(out=O_["wkT"], in_=psv(bwk, 128, H * 64, 64)), writes=[self.psb[bwk], ob["wkT"]])

            def scan(t0c, dr, li, oi, final):
                L_, O_ = LD[li], OUT[oi]
                ob = O_["b"]
                b1 = bank1()
                wkS = psv(b1, C, H * 128, 128)
                for h in range(H):
                    P.add("pe", lambda e, h=h: e.matmul(wkS[:, h, :], lhsT=O_["wkT"][:, h, :], rhs=S_[:, h, :], start=True, stop=True),
                          reads=[ob["wkT"], bS], writes=[self.psb[b1]])
                P.add("dve", lambda e: e.tensor_tensor(out=w_[0:C], in0=O_["u"][0:C], in1=wkS, op=ALU.subtract),
                      reads=[ob["u"]], writes=[self.psb[b1], bw])
                b2 = bank1()
                zv = psv(b2, C, H * 128, 128)
                for h in range(H):
                    P.add("pe", lambda e, h=h: e.matmul(zv[:, h, :], lhsT=L_["Qf"][:, h, :], rhs=S_[:, h, :], start=True, stop=True),
                          reads=[L_["b"], bS], writes=[self.psb[b2]])
                P.add("dve", lambda e: e.tensor_tensor(out=zs[0:C], in0=zv, in1=bc_x(O_["egam"][0:C, 0:H], 128), op=ALU.mult),
                      reads=[ob["egam"]], writes=[self.psb[b2], bzs])
                b3 = bank1()
                pwv = psv(b3, C, H * 128, 128)
                for h in range(H):
                    P.add("pe", lambda e, h=h: e.matmul(pwv[:, h, :], lhsT=O_["pT"][0:C, h, :], rhs=w_[0:C, h, :], start=True, stop=True),
                          reads=[ob["pT"], bw], writes=[self.psb[b3]])
                P.add("dve", lambda e: e.tensor_tensor(out=ot[0:C], in0=zs[0:C], in1=pwv, op=ALU.add),
                      reads=[bzs], writes=[self.psb[b3], bot])
                b4 = bank1()
                sup = psv(b4, 128, H * 128, 128)
                for h in range(H):
                    P.add("pe", lambda e, h=h: e.matmul(sup[:, h, :], lhsT=O_["kd"][0:C, h, :], rhs=w_[0:C, h, :], start=True, stop=True),
                          reads=[ob["kd"], bw], writes=[self.psb[b4]])
                P.add("pool", lambda e: e.tensor_tensor(out=S_, in0=S_, in1=bc_x(O_["gl"][:, 0:H], 128), op=ALU.mult), reads=[ob["gl"]], writes=[bS])
                P.add("dve", lambda e: e.tensor_tensor(out=S_, in0=S_, in1=sup, op=ALU.add), writes=[self.psb[b4], bS])
                ofl = "(h d)"
                if dr == 0:
                    P.dma("sp", D["GOF"][t0c:t0c + C, h0 * 128:(h0 + H) * 128].rearrange("t (h d) -> t h d", d=128), ot[0:C], reads=[bot], writes=[gofb], pool="st")
                else:
                    P.dma("sp", of_[0:C], D["GOF"][t0c:t0c + C, h0 * 128:(h0 + H) * 128].rearrange("t (h d) -> t h d", d=128), reads=[gofb], writes=[bof])
                    P.dma("sp", gz[0:C], D["GZT"][t0c:t0c + C, h0 * 128:(h0 + H) * 128].rearrange("t (h d) -> t h d", d=128), reads=[B["GZT"]], writes=[bgz])
                    P.add("dve", lambda e: e.tensor_tensor(out=ot[0:C], in0=ot[0:C], in1=of_[0:C], op=ALU.add), reads=[bof], writes=[bot])
                    P.add("pool", lambda e: e.tensor_tensor(out=tmp[0:C], in0=ot[0:C], in1=ot[0:C], op=ALU.mult), reads=[bot], writes=[btmp])
                    P.add("dve", lambda e: e.tensor_reduce(out=ssq[0:C, 0:H], in_=tmp[0:C], axis=AX.X, op=ALU.add), reads=[btmp], writes=[bssq])
                    P.add("act", lambda e: e.activation(out=ssq[0:C, 0:H], in_=ssq[0:C, 0:H], func=AF.Ln, bias=self.epsT[0:C], scale=1.0 / 128.0),
                          reads=[self.b_const], writes=[bssq])
                    P.add("act", lambda e: e.activation(out=ssq[0:C, 0:H], in_=ssq[0:C, 0:H], func=AF.Exp, scale=-0.5), writes=[bssq])
                    P.add("dve", lambda e: e.tensor_tensor(out=ot[0:C], in0=ot[0:C], in1=bc_x(ssq[0:C, 0:H], 128), op=ALU.mult), reads=[bssq], writes=[bot])
                    P.add("pool", lambda e: e.tensor_tensor(out=gz[0:C], in0=gz[0:C], in1=G["gnw"][0:C].unsqueeze(1).to_broadcast([C, H, 128]), op=ALU.mult),
                          reads=[G["b"]], writes=[bgz])
                    P.add("dve", lambda e: e.tensor_tensor(out=ot[0:C], in0=ot[0:C], in1=gz[0:C], op=ALU.mult), reads=[bgz], writes=[bot])
                    b5 = bank1()
                    for h in range(H):
                        P.add("pe", lambda e, h=h: e.transpose(out=ps[:, b5, h * 64:(h + 1) * 64], in_=ot[0:C, h, :], identity=id64),
                              reads=[bot, self.b_const], writes=[self.psb[b5]])
                    P.add("act", lambda e: e.copy(out=ogT, in_=psv(b5, 128, H * 64, 64)), writes=[self.psb[b5], bogT])
                    P.dma("sp", D["OG"][h0 * 128:(h0 + H) * 128, t0c:t0c + C].rearrange("(h p) t -> p h t", p=128), ogT, reads=[bogT], writes=[ogb], pool="st")


            def init_state(lat, dr):
                if lat:
                    P.dma("sp", S_, D["sgdn"][l, dr, h0:h0 + H].rearrange("h k v -> k h v"), writes=[bS])
                else:
                    P.add("dve", lambda e: e.memset(S_, 0.0), writes=[bS])

            def store_state(b_, dr):
                P.dma("sp", D["o_gdn"][b_, l, dr, h0:h0 + H].rearrange("h k v -> k h v"), S_, reads=[bS], writes=[ostb], pool="st")
            return prep, scan, init_state, store_state

        chains = [make_chain(ch) for ch in range(NCH)]

        def emit_all(fn):
            lists = []
            for ch in range(NCH):
                keep = P.ops
                P.ops = []
                fn(chains[ch])
                lists.append(P.ops)
                P.ops = keep
            n_ = max(len(x) for x in lists)
            for i in range(n_):
                for x in lists:
                    if i < len(x):
                        P.ops.append(x[i])

        for si, (t0, T, lat) in enumerate(self.seqs()):
            ncks = T // C
            for dr in range(2):
                emit_all(lambda chn: chn[2](lat, dr))
                order = list(range(ncks)) if dr == 0 else list(range(ncks - 1, -1, -1))
                emit_all(lambda chn: chn[0](t0 + order[0] * C, dr, 0, 0))
                for n_, ci in enumerate(order):
                    if n_ + 1 < ncks:
                        emit_all(lambda chn: chn[0](t0 + order[n_ + 1] * C, dr, (n_ + 1) % 2, (n_ + 1) % 2))
                    emit_all(lambda chn: chn[1](t0 + ci * C, dr, n_ % 2, n_ % 2, n_ == ncks - 1))
                if not lat:
                    emit_all(lambda chn: chn[3](si - 1, dr))


def host_constants(cfg, na_rpb, plan_tiles):
    ident = np.eye(128, dtype=np.float32)
    perm = np.zeros((128, 128), np.float32)
    for m in range(128):
        perm[m ^ 32, m] = 1.0
    t = np.arange(cfg.TL)
    pos = np.stack([t // GRID_W, t % GRID_W], -1).astype(np.float32)
    nq = HEAD_DIM // 4
    inv = (ROPE_BASE ** (-np.arange(nq, dtype=np.float32) / nq)).astype(np.float32)
    cos = np.zeros((128, cfg.TL), np.float32)
    sin = np.zeros((128, cfg.TL), np.float32)
    for p in range(128):
        a, half, f = p // 64, (p % 64) // 32, p % 32
        ang = pos[:, a] * inv[f]
        cos[p] = np.cos(ang)
        sin[p] = np.sin(ang) * (-1.0 if half == 0 else 1.0)
    i = np.arange(64)[:, None]
    j = np.arange(64)[None, :]
    masks = np.stack([(i > j), (i > j).T, (i >= j).T, (i < j), (i < j).T, (i <= j).T]).astype(np.float32)
    tri = np.stack([(i <= j), (i >= j)]).astype(np.float32)
    L = na_rpb.shape[0]
    nty = len(plan_tiles)
    nab = np.empty((L, NA_HEADS, nty, 128, 128), np.float32)
    for ti, (dr, dc, valid) in enumerate(plan_tiles):
        gathered = na_rpb[:, :, dr, dc]
        nab[:, :, ti] = np.where(valid[None, None], gathered, np.float32(-30000.0))
    return {"k_ident": ident, "k_perm": perm, "k_cos": cos, "k_sin": sin, "k_masks": masks, "k_tri": tri, "nab": nab}


def make_in_maps(cfg, inputs, n_cores, plan_tiles):
    consts = host_constants(cfg, np.asarray(inputs["na_rpb"], np.float32), plan_tiles)
    shared = {}
    for k in ("w_mod", "b_mod", "norm_pre", "norm_post", "ffn1_w_gu", "ffn1_w_dn", "ffn2_w_gu", "ffn2_w_dn", "w_in",
              "gdn_conv", "gdn_a_log", "gdn_dt_bias", "gdn_norm", "diff_lambda", "diff_norm", "w_branch_na",
              "w_branch_gdn", "w_branch_diff", "w_out"):
        shared[k] = np.ascontiguousarray(inputs[k], dtype=np.float32)
    shared.update(consts)
    maps = []
    L = cfg.L
    for i in range(n_cores):
        m = dict(shared)
        m["xs"] = np.ascontiguousarray(inputs["x_sample"][i])
        m["xp"] = np.ascontiguousarray(inputs["x_prompt"][i * cfg.NB:(i + 1) * cfg.NB]).reshape(cfg.NB * cfg.S, cfg.D)
        m["cna_k"] = np.ascontiguousarray(inputs["cache_na_k"][i]).reshape(L, cfg.PAST, cfg.NAW)
        m["cna_v"] = np.ascontiguousarray(inputs["cache_na_v"][i]).reshape(L, cfg.PAST, cfg.NAW)
        m["sgdn"] = np.ascontiguousarray(inputs["state_gdn"][i])
        m["cd_k"] = np.ascontiguousarray(inputs["cache_diff_k"][i]).reshape(L, cfg.PAST, cfg.DW)
        m["cd_v"] = np.ascontiguousarray(inputs["cache_diff_v"][i]).reshape(L, cfg.PAST, cfg.DW)
        m["cvec"] = np.ascontiguousarray(np.stack([inputs["c"][i], inputs["c_ctx"]]))
        maps.append(m)
    return maps


def assemble(cfg, results, n_cores):
    L = cfg.L
    y_s = np.stack([r["y_s"] for r in results])
    y_p = np.concatenate([r["y_p"].reshape(cfg.NB, cfg.S, cfg.D) for r in results])
    nk = np.concatenate([r["o_na_k"].reshape(cfg.NB, L, cfg.S, NA_HEADS, 128) for r in results])
    nv = np.concatenate([r["o_na_v"].reshape(cfg.NB, L, cfg.S, NA_HEADS, 128) for r in results])
    gs = np.concatenate([r["o_gdn"] for r in results])
    dk = np.concatenate([r["o_d_k"].reshape(cfg.NB, L, cfg.S, DIFF_HEADS, 2, 128) for r in results])
    dv = np.concatenate([r["o_d_v"].reshape(cfg.NB, L, cfg.S, DIFF_HEADS, 256) for r in results])
    return tuple(np.asarray(a, np.float32) for a in (y_s, y_p, nk, nv, gs, dk, dv))


def kernel(**inputs):
    inputs = {k: np.asarray(v) for k, v in inputs.items()}
    n = 8
    cfg = Cfg()
    b = Builder(cfg)
    nc = b.build()
    maps = make_in_maps(cfg, inputs, n, b.na_tiles)
    res = run_bass_kernel_spmd(nc, maps, core_ids=list(range(n)))
    y_s, y_p, nk, nv, gs, dk, dv = assemble(cfg, res.results, n)
    return (y_p, y_s, nk, nv, gs, dk, dv)
```
